# Optimizing a Trainium2 kernel written in Bass

```python
import jax, jax.numpy as jnp
from jax import lax
import numpy as np

D_MODEL = 1024
BATCH = 8
SEQ = 2048
DEPTH = 4

GRID_W = 64
CTX_LEN = 256
N_FOURIER_GROUPS = 4
FOURIER_GROUP_DIM = D_MODEL // 8
FOURIER_WIDTH = N_FOURIER_GROUPS * FOURIER_GROUP_DIM
N_SGU_HEADS = 4
SGU_HEAD_DIM = D_MODEL // 8
SGU_WIDTH = N_SGU_HEADS * SGU_HEAD_DIM
CHUNK = 128
EVEN_IN_WIDTH = FOURIER_WIDTH + 2 * SGU_WIDTH
EVEN_OUT_WIDTH = FOURIER_WIDTH + SGU_WIDTH
NA_HEADS = 16
NA_HEAD_DIM = D_MODEL // NA_HEADS
WIN_H = 8
WIN_W = 16
D_FF = 2816
CONV_W = 3
N_MOD = 6
EPS = 1e-6
NEG_INF = -1e30
N_EVEN = (DEPTH + 1) // 2
N_ODD = DEPTH // 2

kernel_name = "hybrid_fourier_sgu_natten_dit"


def rms_norm(x, g=None):
    xf = x.astype(jnp.float32)
    y = xf * lax.rsqrt(jnp.mean(xf * xf, axis=-1, keepdims=True) + EPS)
    if g is not None:
        y = y * g.astype(jnp.float32)
    return y.astype(x.dtype)


def modulate(x, g, shift, scale):
    return rms_norm(x, g) * (1 + scale) + shift


def fourier_mix(f):
    y = jnp.fft.fftn(f.astype(jnp.float32), axes=(1, 3), norm="ortho").real
    return y.astype(f.dtype)


def chunk_sgu(u, v, w_s, b_s):
    B_, L, G, C = v.shape
    vc = rms_norm(v).reshape(B_, L // CHUNK, CHUNK, G, C)
    s = jnp.einsum('gpq,bnqgc->bnpgc', w_s, vc) + b_s.T[None, None, :, :, None]
    return u * s.reshape(B_, L, G, C)


def even_mixer(h, w_in, w_s, b_s, w_out):
    B_, L, _ = h.shape
    p = h @ w_in
    f = p[..., :FOURIER_WIDTH].reshape(B_, L, N_FOURIER_GROUPS, FOURIER_GROUP_DIM)
    uv = jax.nn.gelu(p[..., FOURIER_WIDTH:])
    u = uv[..., :SGU_WIDTH].reshape(B_, L, N_SGU_HEADS, SGU_HEAD_DIM)
    v = uv[..., SGU_WIDTH:].reshape(B_, L, N_SGU_HEADS, SGU_HEAD_DIM)
    a = fourier_mix(f).reshape(B_, L, FOURIER_WIDTH)
    b = chunk_sgu(u, v, w_s, b_s).reshape(B_, L, SGU_WIDTH)
    return jnp.concatenate([a, b], axis=-1) @ w_out


def _heads(t):
    B_, L, _ = t.shape
    return t.reshape(B_, L, NA_HEADS, NA_HEAD_DIM).transpose(0, 2, 1, 3)


def _merge(o):
    B_, H, L, d = o.shape
    return o.transpose(0, 2, 1, 3).reshape(B_, L, H * d)


def qkv_heads(h, w_qkv, q_g, k_g):
    q, k, v = jnp.split(h @ w_qkv, 3, axis=-1)
    return rms_norm(_heads(q), q_g), rms_norm(_heads(k), k_g), _heads(v)


def neighbourhood_attention(q, k, v, k_c, v_c, rpb):
    B_, H, L, d = q.shape
    rows = L // GRID_W
    kh = min(WIN_H, rows)
    r = jnp.arange(rows)
    row_start = jnp.clip(r - kh // 2, 0, rows - kh)
    row_idx = row_start[:, None] + jnp.arange(kh)[None, :]
    col = jnp.arange(GRID_W)
    col_start = jnp.clip(col - WIN_W // 2, 0, GRID_W - WIN_W)
    col_mask = (col[None, :] >= col_start[:, None]) & (col[None, :] < col_start[:, None] + WIN_W)
    dr = row_idx - r[:, None] + (WIN_H - 1)
    dc = jnp.clip(col[None, :] - col[:, None] + (WIN_W - 1), 0, 2 * WIN_W - 2)
    bias = rpb[:, dr][:, :, :, dc].transpose(0, 1, 3, 2, 4)
    scale = d ** -0.5
    qg = q.reshape(B_, H, rows, GRID_W, d)
    kg = k.reshape(B_, H, rows, GRID_W, d)[:, :, row_idx]
    vg = v.reshape(B_, H, rows, GRID_W, d)[:, :, row_idx]
    s_loc = jnp.einsum('bhrqd,bhrikd->bhrqik', qg, kg).astype(jnp.float32) * scale + bias
    s_loc = jnp.where(col_mask[:, None, :], s_loc, NEG_INF)
    s_ctx = jnp.einsum('bhrqd,bhcd->bhrqc', qg, k_c).astype(jnp.float32) * scale
    n_loc = kh * GRID_W
    s = jnp.concatenate([s_loc.reshape(B_, H, rows, GRID_W, n_loc), s_ctx], axis=-1)
    p = jax.nn.softmax(s, axis=-1).astype(v.dtype)
    p_loc = p[..., :n_loc].reshape(B_, H, rows, GRID_W, kh, GRID_W)
    p_ctx = p[..., n_loc:]
    o = jnp.einsum('bhrqik,bhrikd->bhrqd', p_loc, vg) + jnp.einsum('bhrqc,bhcd->bhrqd', p_ctx, v_c)
    return o.reshape(B_, H, L, d)


def context_attention(q_c, k_c, v_c):
    s = jnp.einsum('bhqd,bhkd->bhqk', q_c, k_c).astype(jnp.float32) * (q_c.shape[-1] ** -0.5)
    p = jax.nn.softmax(s, axis=-1).astype(v_c.dtype)
    return jnp.einsum('bhqk,bhkd->bhqd', p, v_c)


def conv_ffn(h, w_up, conv_w, conv_b, w_down):
    L = h.shape[1]
    a = h @ w_up
    gate, val = a[..., :D_FF], a[..., D_FF:]
    gp = jnp.pad(gate, ((0, 0), (CONV_W // 2, CONV_W // 2), (0, 0)))
    g = gp[:, 0:L] * conv_w[0] + gp[:, 1:1 + L] * conv_w[1] + gp[:, 2:2 + L] * conv_w[2] + conv_b
    return (jax.nn.silu(g) * val) @ w_down


def setup_inputs(seed: int = 0) -> dict:
    key = jax.random.key(seed)
    ks = jax.random.split(key, 24)
    nrm = jax.random.normal
    D = D_MODEL
    return {
        "x": nrm(ks[0], (BATCH, SEQ, D), jnp.float32),
        "c": nrm(ks[1], (BATCH, D), jnp.float32),
        "ctx": nrm(ks[2], (BATCH, CTX_LEN, D), jnp.float32),
        "c_ctx": nrm(ks[3], (D,), jnp.float32),
        "norm1_g": 1.0 + 0.1 * nrm(ks[4], (DEPTH, D), jnp.float32),
        "norm2_g": 1.0 + 0.1 * nrm(ks[5], (DEPTH, D), jnp.float32),
        "ada_w": 0.5 * D ** -0.5 * nrm(ks[6], (DEPTH, D, N_MOD * D), jnp.float32),
        "ada_b": 0.02 * nrm(ks[7], (DEPTH, N_MOD * D), jnp.float32),
        "even_w_in": D ** -0.5 * nrm(ks[8], (N_EVEN, D, EVEN_IN_WIDTH), jnp.float32),
        "even_w_s": CHUNK ** -0.5 * nrm(ks[9], (N_EVEN, N_SGU_HEADS, CHUNK, CHUNK), jnp.float32),
        "even_b_s": 1.0 + 0.1 * nrm(ks[10], (N_EVEN, N_SGU_HEADS, CHUNK), jnp.float32),
        "even_w_out": EVEN_OUT_WIDTH ** -0.5 * nrm(ks[11], (N_EVEN, EVEN_OUT_WIDTH, D), jnp.float32),
        "odd_w_qkv": D ** -0.5 * nrm(ks[12], (N_ODD, D, 3 * D), jnp.float32),
        "odd_q_g": 1.0 + 0.1 * nrm(ks[13], (N_ODD, NA_HEAD_DIM), jnp.float32),
        "odd_k_g": 1.0 + 0.1 * nrm(ks[14], (N_ODD, NA_HEAD_DIM), jnp.float32),
        "odd_rpb": 0.1 * nrm(ks[15], (N_ODD, NA_HEADS, 2 * WIN_H - 1, 2 * WIN_W - 1), jnp.float32),
        "odd_w_o": D ** -0.5 * nrm(ks[16], (N_ODD, D, D), jnp.float32),
        "ffn_w_up": D ** -0.5 * nrm(ks[17], (DEPTH, D, 2 * D_FF), jnp.float32),
        "ffn_conv_w": CONV_W ** -0.5 * nrm(ks[18], (DEPTH, CONV_W, D_FF), jnp.float32),
        "ffn_conv_b": 0.02 * nrm(ks[19], (DEPTH, D_FF), jnp.float32),
        "ffn_w_down": D_FF ** -0.5 * nrm(ks[20], (DEPTH, D_FF, D), jnp.float32),
    }


def reference(x, c, ctx, c_ctx, norm1_g, norm2_g, ada_w, ada_b, even_w_in, even_w_s, even_b_s, even_w_out,
              odd_w_qkv, odd_q_g, odd_k_g, odd_rpb, odd_w_o, ffn_w_up, ffn_conv_w, ffn_conv_b, ffn_w_down):
    s_lat = jax.nn.silu(c)
    s_ctx = jax.nn.silu(c_ctx)
    for layer in range(DEPTH):
        last = layer == DEPTH - 1
        i = layer // 2
        mod = (s_lat @ ada_w[layer] + ada_b[layer])[:, None, :]
        mod_c = s_ctx @ ada_w[layer] + ada_b[layer]
        sh1, sc1, g1, sh2, sc2, g2 = jnp.split(mod, N_MOD, axis=-1)
        csh1, csc1, cg1, csh2, csc2, cg2 = jnp.split(mod_c, N_MOD, axis=-1)
        h = modulate(x, norm1_g[layer], sh1, sc1)
        h_c = modulate(ctx, norm1_g[layer], csh1, csc1)
        if layer % 2 == 0:
            y = even_mixer(h, even_w_in[i], even_w_s[i], even_b_s[i], even_w_out[i])
            if not last:
                y_c = even_mixer(h_c, even_w_in[i], even_w_s[i], even_b_s[i], even_w_out[i])
        else:
            q, k, v = qkv_heads(h, odd_w_qkv[i], odd_q_g[i], odd_k_g[i])
            q_c, k_c, v_c = qkv_heads(h_c, odd_w_qkv[i], odd_q_g[i], odd_k_g[i])
            y = _merge(neighbourhood_attention(q, k, v, k_c, v_c, odd_rpb[i])) @ odd_w_o[i]
            if not last:
                y_c = _merge(context_attention(q_c, k_c, v_c)) @ odd_w_o[i]
        x = x + g1 * y
        x = x + g2 * conv_ffn(modulate(x, norm2_g[layer], sh2, sc2),
                              ffn_w_up[layer], ffn_conv_w[layer], ffn_conv_b[layer], ffn_w_down[layer])
        if not last:
            ctx = ctx + cg1 * y_c
            ctx = ctx + cg2 * conv_ffn(modulate(ctx, norm2_g[layer], csh2, csc2),
                                       ffn_w_up[layer], ffn_conv_w[layer], ffn_conv_b[layer], ffn_w_down[layer])
    return x
```

```python
from contextlib import ExitStack
import os
import numpy as np
import concourse.bass as bass
import concourse.mybir as mybir
from concourse.bass_utils import run_bass_kernel_spmd

F32 = mybir.dt.float32
BF16 = mybir.dt.bfloat16
ALU = mybir.AluOpType
AF = mybir.ActivationFunctionType
AX = mybir.AxisListType

ENGS = ("pe", "act", "dve", "pool", "sp")
N_DMA_SEMS = 40
GRAN = 2048


class Buf:
    __slots__ = ("name", "writes", "reads")

    def __init__(self, name=""):
        self.name = name
        self.writes = []
        self.reads = []


class Region:
    __slots__ = ("gen", "cur", "prev")

    def __init__(self):
        self.gen = None
        self.cur = {}
        self.prev = {}


class Tile:
    __slots__ = ("ap", "buf", "grans", "gen", "excl")

    def __init__(self, ap, name="", grans=(), gen=None):
        self.excl = False
        self.ap = ap
        self.buf = Buf(name)
        self.grans = grans
        self.gen = gen


class Arena:
    def __init__(self, tensor, nbytes):
        self.t = tensor
        self.nbytes = nbytes
        self.regs = [Region() for _ in range((nbytes + GRAN - 1) // GRAN)]
        self.whole = Region()
        self.use_whole = False

    def tile(self, gen, off, shape, dtype, name=""):
        esz = 4 if dtype == F32 else 2
        n = 1
        for s in shape[1:]:
            n *= s
        nb = n * esz
        assert off % 4 == 0 and off + nb <= self.nbytes, (name, off, nb, self.nbytes)
        ap = self.t[:, off // 2:(off + nb) // 2]
        if dtype == F32:
            ap = ap.bitcast(F32)
        if len(shape) == 3:
            ap = ap.rearrange("p (a b) -> p a b", a=shape[1])
        elif len(shape) == 4:
            ap = ap.rearrange("p (a b c) -> p a b c", a=shape[1], b=shape[2])
        if shape[0] != 128:
            ap = ap[0:shape[0]]
        grans = tuple(self.regs[off // GRAN:(off + nb - 1) // GRAN + 1])
        if self.use_whole:
            grans = grans + (self.whole,)
        return Tile(ap, name, grans, gen)


class Op:
    __slots__ = ("dom", "idx", "fn", "vc", "waits", "signal", "stream", "is_dma")


class Sched:
    def __init__(self, nc):
        self.nc = nc
        self.streams = {e: [] for e in ENGS}
        self.vc = {e: {} for e in ENGS}
        self.count = {}
        self.dma_last = [None] * N_DMA_SEMS
        self.dma_rr = {"pool": 0, "sp": 0, "act": 0}

    def _deps(self, reads, writes):
        deps = []
        for t in reads:
            for o in t.buf.writes:
                deps.append((o, "raw"))
            if t.excl:
                for o in t.buf.reads[-4:]:
                    deps.append((o, "rar"))
        for t in writes:
            b = t.buf
            for o in b.writes:
                deps.append((o, "waw"))
            for o in b.reads:
                deps.append((o, "war"))
        for t in list(reads) + list(writes):
            for r in t.grans:
                if r.gen != t.gen:
                    r.prev = r.cur
                    r.cur = {}
                    r.gen = t.gen
                for o in r.prev.values():
                    deps.append((o, "raw"))
        return deps

    def _book(self, op, reads, writes):
        for t in reads:
            t.buf.reads.append(op)
            if len(t.buf.reads) > 96:
                t.buf.reads = t.buf.reads[-96:]
        for t in writes:
            b = t.buf
            if b.reads:
                b.writes = [op]
                b.reads = []
            else:
                b.writes.append(op)
                if len(b.writes) > 64:
                    b.writes = b.writes[-64:]
        for t in list(reads) + list(writes):
            for r in t.grans:
                r.cur[op.dom] = op

    def _resolve(self, stream, deps):
        base = self.vc[stream]
        waits = {}
        for d, kind in deps:
            if d.dom == stream:
                if stream == "pe" or stream == "sp":
                    continue
                if kind == "war" or kind == "rar":
                    continue
            if base.get(d.dom, 0) >= d.idx:
                continue
            d.signal = True
            waits[d.dom] = max(waits.get(d.dom, 0), d.idx)
            for k, v in d.vc.items():
                if base.get(k, 0) < v:
                    base[k] = v
            base[d.dom] = max(base.get(d.dom, 0), d.idx)
        return waits

    def _mk(self, dom, stream, fn, deps, is_dma):
        o = Op()
        o.dom = dom
        o.stream = stream
        o.is_dma = is_dma
        o.idx = self.count.get(dom, 0) + 1
        self.count[dom] = o.idx
        o.fn = fn
        o.signal = is_dma
        o.waits = self._resolve(stream, deps)
        o.vc = dict(self.vc[stream])
        self.streams[stream].append(o)
        return o

    def op(self, eng, fn, reads=(), writes=()):
        deps = self._deps(reads, writes)
        o = self._mk(eng, eng, fn, deps, False)
        self._book(o, reads, writes)
        return o

    def dma(self, queue, fn, reads=(), writes=()):
        deps = self._deps(reads, writes)
        lo, n = (0, N_DMA_SEMS - 8) if queue == "pool" else (N_DMA_SEMS - 8, 8)
        k = lo + self.dma_rr[queue] % n
        self.dma_rr[queue] += 1
        prev = self.dma_last[k]
        if prev is not None:
            deps.append((prev, "raw"))
        o = self._mk("d%d" % k, queue, fn, deps, True)
        self.dma_last[k] = o
        self._book(o, reads, writes)
        return o

    def wait_ops(self, eng, ops):
        return self._mk(eng, eng, None, [(o, "raw") for o in ops], False)

    def emit(self, ctx):
        nc = self.nc
        sems = {}
        for e in ENGS:
            sems[e] = ctx.enter_context(nc.semaphore("s_" + e))
        for k in range(N_DMA_SEMS):
            sems["d%d" % k] = ctx.enter_context(nc.semaphore("s_d%d" % k))
        ticket = {}
        for e in ENGS:
            t = 0
            for o in self.streams[e]:
                if o.is_dma:
                    ticket[(o.dom, o.idx)] = 16 * o.idx
                elif o.signal:
                    t += 1
                    ticket[(o.dom, o.idx)] = t
        handles = {"pe": nc.tensor, "act": nc.scalar, "dve": nc.vector,
                   "pool": nc.gpsimd, "sp": nc.sync}

        def run(e):
            h = handles[e]
            for o in self.streams[e]:
                for dom, idx in o.waits.items():
                    h.wait_ge(sems[dom], ticket[(dom, idx)])
                if o.fn is None:
                    continue
                ins = o.fn(h)
                if o.is_dma:
                    ins.then_inc(sems[o.dom], 16)
                elif o.signal:
                    ins.then_inc(sems[o.dom], 1)

        with nc.Block() as block:
            @block.tensor
            def _(eng):
                run("pe")

            @block.scalar
            def _(eng):
                run("act")

            @block.vector
            def _(eng):
                run("dve")

            @block.gpsimd
            def _(eng):
                run("pool")

            @block.sync
            def _(eng):
                run("sp")


D = 1024
L = 2048
NCX = 256
NT = L + NCX
DFF = 2816
NFF = 22
LAT0 = 1
CTX0 = 2051
NCOL = 2308
EPS = 1e-6
NE_WIN = 23
NE = 39
DEPTH = 4


def pcol(t):
    return t + LAT0 if t < L else (t - L) + CTX0


TB512 = [(0, 512, 0), (512, 512, 0), (1024, 512, 0), (1536, 512, 0), (2048, 256, 1)]
TB256 = [(256 * k, 256, 0) for k in range(8)] + [(2048, 256, 1)]
SEGS = [(0, 406, 0), (406, 406, 0), (812, 406, 0), (1218, 415, 0), (1633, 415, 0), (2048, 256, 1)]


def build(n_layers=DEPTH, dbg=False):
    nc = bass.Bass("TRN2", target_bir_lowering=False)

    def din(name, shape):
        return nc.dram_tensor(name, list(shape), F32, kind="ExternalInput").ap()

    xin = din("xin", [D, NT])
    svec = din("svec", [128, 16])
    ada_w = din("ada_w", [4, D, 6 * D])
    ada_b = din("ada_b_r", [128, 4 * 48])
    ngd = din("ng", [128, 64])
    convd = din("convp", [128, 4 * 4 * NFF])
    wup = din("wup_r", [4, NFF, 128, 8 * 256])
    wdn = din("wdn_r", [4, 8, 128, NFF * 128])
    win = din("win_r", [2, 8, 128, 8 * 128])
    winv = din("winv_r", [2, 128, 8 * 512])
    wsT = din("wsT", [2, 128, 4 * 128])
    bsd = din("bs", [2, 1, 512])
    wout = din("wout_r", [2, 8, 128, 8 * 128])
    wqk = din("wqk_r", [2, 16, 128, 8 * 128])
    wv = din("wv_r", [2, 2, 128, 8 * 512])
    wo = din("wo_r", [2, 8, 128, 8 * 128])
    qkgd = din("qkg", [128, 4])
    tabd = din("tab", [2, 16, 128, NE * 64])
    dftL = din("dftL", [2, L, L])
    dftC = din("dftC", [128, 2 * 256])
    dft256 = din("dft256", [2, 256, 256])
    cstd = din("cst", [128, 3 * 128])
    n_out = NT if dbg else L
    yout = nc.dram_tensor("yout", [D, n_out], F32, kind="ExternalOutput").ap()

    with ExitStack() as ctx:
        def sb(name, shape, dt):
            return ctx.enter_context(nc.sbuf_tensor(name, shape, dt))

        XTt = sb("XT", [128, 8, NT], F32)
        R1t = sb("R1", [128, 8 * NCOL], BF16)
        R2t = sb("R2", [128, 18432], BF16)
        R3t = sb("R3", [128, 18432], BF16)
        R4t = sb("R4", [128, 9216], BF16)
        CONSTt = sb("CONST", [128, 3, 128], BF16)
        NGt = sb("NG", [128, 2, 4, 8], F32)
        CONVt = sb("CONVP", [128, 4, 4, NFF], F32)
        QKGt = sb("QKG", [128, 2, 2], F32)
        GQSt = sb("GQS", [128, 2, 2], F32)
        ADABt = sb("ADAB", [128, 4, 48], F32)
        SVt = sb("SV", [128, 8, 2], F32)
        STt = sb("ST", [128, 8, 2], BF16)
        MODt = [sb("MOD%d" % k, [128, 48, 2], F32) for k in range(2)]
        A12t = [sb("A12_%d" % k, [128, 2, 8, 2], F32) for k in range(2)]
        QZt = sb("QZ", [128, 4, 512], BF16)
        PSt = [ctx.enter_context(nc.psum_tensor("ps%d" % k, [128, 512], F32)) for k in range(8)]

        R1 = Arena(R1t, 8 * NCOL * 2)
        R2 = Arena(R2t, 36864)
        R3 = Arena(R3t, 36864)
        R4 = Arena(R4t, 18432)

        S = Sched(nc)
        PS = [Tile(PSt[k][:], "ps%d" % k) for k in range(8)]
        for t_ in PS:
            t_.excl = True
        psrot = [0]

        ps_default = [(0, 1, 2, 3, 4, 5, 6, 7)]

        def ps_next(pool=None):
            if pool is None:
                pool = ps_default[0]
            k = pool[psrot[0] % len(pool)]
            psrot[0] += 1
            return PS[k]

        XTT = [[Tile(XTt[:, c, u * 128:(u + 1) * 128], "xt%d_%d" % (c, u)) for u in range(18)] for c in range(8)]

        def xt_tiles(c, t0, n):
            return [XTT[c][u] for u in range(t0 // 128, (t0 + n - 1) // 128 + 1)]

        def xt_all(t0, n):
            r = []
            for c in range(8):
                r += xt_tiles(c, t0, n)
            return r

        R1.use_whole = True
        HT = R1.tile("r1", 0, [128, 8, NCOL], BF16, "HT")
        HTU = [Tile(None, "ht%d" % u, (R1.whole,), "r1") for u in range(18)]
        HTPAD = Tile(None, "htpad", (R1.whole,), "r1")

        def ht_tiles(t0, n):
            return [HTU[u] for u in range(t0 // 128, (t0 + n - 1) // 128 + 1)]

        def ht_halo(t0, n, col):
            s0, s1 = (0, L) if col == 0 else (L, NT)
            lo = max(t0 - 1, s0)
            hi = min(t0 + n + 1, s1)
            return ht_tiles(lo, hi - lo) + [HTPAD]

        QZ = [Tile(QZt[:, k, :], "qz%d" % k) for k in range(4)]
        CONST = Tile(CONSTt, "const")
        IDENT = CONSTt[:, 0, :]
        ONES = CONSTt[:, 1, :]
        BLK = CONSTt[:, 2, :]
        NG = Tile(NGt, "ng")
        CONV = Tile(CONVt, "conv")
        QKG = Tile(QKGt, "qkg")
        GQS = Tile(GQSt, "gqs")
        ADAB = Tile(ADABt, "adab")
        SV = Tile(SVt, "sv")
        ST = Tile(STt, "st")
        MOD = [Tile(MODt[k], "mod%d" % k) for k in range(2)]
        A12 = [Tile(A12t[k], "a12_%d" % k) for k in range(2)]

        RECIP = "reciprocal"

        def OP(eng, method, *args, reads=(), writes=(), **kw):
            return S.op(eng, lambda e: getattr(e, method)(*args, **kw), reads, writes)

        def MM(out, lhsT, rhs, start, stop, reads, writes):
            return S.op("pe", lambda e: e.matmul(out, lhsT, rhs, start=start, stop=stop), reads, writes)

        def DMA(queue, out, in_, reads=(), writes=()):
            return S.dma(queue, lambda e: e.dma_start(out=out, in_=in_), reads, writes)

        for c in range(8):
            DMA("sp", XTt[:, c, :], xin[c * 128:(c + 1) * 128, :], writes=XTT[c])
        DMA("pool", CONSTt[:].rearrange("p a b -> p (a b)"), cstd, writes=[CONST])
        DMA("sp", NGt[:].rearrange("p a b c -> p (a b c)"), ngd, writes=[NG])
        DMA("sp", CONVt[:].rearrange("p a b c -> p (a b c)"), convd, writes=[CONV])
        DMA("sp", QKGt[:].rearrange("p a b -> p (a b)"), qkgd, writes=[QKG])
        DMA("sp", ADABt[:].rearrange("p a b -> p (a b)"), ada_b, writes=[ADAB])
        DMA("sp", SVt[:].rearrange("p a b -> p (a b)"), svec, writes=[SV])
        OP("act", "activation", STt[:], SVt[:], AF.Silu, reads=[SV], writes=[ST])
        OP("dve", "tensor_scalar", GQSt[:], QKGt[:], 0.125, None, ALU.mult, reads=[QKG], writes=[GQS])

        def mods(l):
            gen = "L%d.mods" % l
            md = MOD[l % 2]
            mdt = MODt[l % 2]
            a12 = A12[l % 2]
            a12t = A12t[l % 2]
            BW = int(os.environ.get("K_BW", "512"))
            nb = 6144 // BW
            ADA = [R2.tile(gen, k * 8192, [128, 8, BW], BF16, "ada%d" % k) for k in range(2)]
            adv = ada_w[l].rearrange("(c p) n -> p c n", p=128)
            psm = PS[7]
            for blk in range(nb):
                a = ADA[blk % 2]
                DMA("pool", a.ap, adv[:, :, blk * BW:(blk + 1) * BW], writes=[a])
                for jj in range(BW // 128):
                    j = blk * (BW // 128) + jj
                    for kc in range(8):
                        MM(psm.ap[:, 2 * j:2 * j + 2], a.ap[:, kc, jj * 128:(jj + 1) * 128], STt[:, kc, :],
                           kc == 0, kc == 7, [a, ST], [psm])
            mods_final(l)

        def mods_final_part(l, part):
            md = MOD[l % 2]
            mdt = MODt[l % 2]
            a12 = A12[l % 2]
            a12t = A12t[l % 2]
            psm = PS[7]
            j0, j1 = (0, 16) if part == 0 else (16, 48)
            OP("dve", "tensor_tensor", mdt[:, j0:j1, :], psm.ap[:, 2 * j0:2 * j1].rearrange("p (a b) -> p a b", b=2),
               ADABt[:, l, j0:j1].unsqueeze(2).to_broadcast([128, j1 - j0, 2]), ALU.add,
               reads=[psm, ADAB], writes=[md])
            w, jj0 = (0, 8) if part == 0 else (1, 32)
            OP("dve", "tensor_scalar", a12t[:, w], mdt[:, jj0:jj0 + 8, :], 1.0, None, ALU.add,
               reads=[md], writes=[a12])
            OP("dve", "tensor_tensor", a12t[:, w], a12t[:, w],
               NGt[:, w, l, :].unsqueeze(2).to_broadcast([128, 8, 2]), ALU.mult,
               reads=[a12, NG], writes=[a12])

        def mods0():
            gen = "L0.mods"
            ADA = [R2.tile(gen, k * 8192, [128, 8, 512], BF16, "ada%d" % k) for k in range(4)] + \
                  [R4.tile(gen, k * 8192, [128, 8, 512], BF16, "adb%d" % k) for k in range(2)]
            adv = ada_w[0].rearrange("(c p) n -> p c n", p=128)
            psm = PS[7]
            st = [4]

            def dma(blk):
                a = ADA[blk % 6]
                DMA("pool", a.ap, adv[:, :, blk * 512:(blk + 1) * 512], reads=(XTT[7][:1] if blk >= 4 else []),
                    writes=[a])

            def mm(blk):
                a = ADA[blk % 6]
                for jj in range(4):
                    j = blk * 4 + jj
                    for kc in range(8):
                        MM(psm.ap[:, 2 * j:2 * j + 2], a.ap[:, kc, jj * 128:(jj + 1) * 128], STt[:, kc, :],
                           kc == 0, kc == 7, [a, ST], [psm])
                if blk + 6 < 12:
                    dma(blk + 6)

            for blk in range(6):
                dma(blk)
            for blk in range(4):
                mm(blk)
            mods_final_part(0, 0)

            def hook(bi):
                for _ in range(2):
                    if st[0] < 12:
                        mm(st[0])
                        st[0] += 1

            def finish():
                while st[0] < 12:
                    mm(st[0])
                    st[0] += 1
                mods_final_part(0, 1)
            return hook, finish

        def mods_final(l):
            md = MOD[l % 2]
            mdt = MODt[l % 2]
            a12 = A12[l % 2]
            a12t = A12t[l % 2]
            psm = PS[7]
            OP("dve", "tensor_tensor", mdt[:], psm.ap[:, 0:96].rearrange("p (a b) -> p a b", b=2),
               ADABt[:, l, :].unsqueeze(2).to_broadcast([128, 48, 2]), ALU.add,
               reads=[psm, ADAB], writes=[md])
            for w, j0 in ((0, 8), (1, 32)):
                OP("dve", "tensor_scalar", a12t[:, w], mdt[:, j0:j0 + 8, :], 1.0, None, ALU.add,
                   reads=[md], writes=[a12])
                OP("dve", "tensor_tensor", a12t[:, w], a12t[:, w],
                   NGt[:, w, l, :].unsqueeze(2).to_broadcast([128, 8, 2]), ALU.mult,
                   reads=[a12, NG], writes=[a12])

        def modulate(l, which, blocks, zero_pads, external=False, hook=None):
            gen = "L%d.m%d" % (l, which)
            md = MOD[l % 2]
            mdt = MODt[l % 2]
            a12 = A12[l % 2]
            a12t = A12t[l % 2]
            sh0 = 0 if which == 1 else 24
            SQ = [R3.tile(gen, k * 4096, [128, 4, 512], BF16, "sq%d" % k) for k in range(2)]
            RS = [R3.tile(gen, 8192 + k * 2048, [128, 512], F32, "rs%d" % k) for k in range(2)]
            RR = [R3.tile(gen, 12288 + k * 2048, [128, 512], F32, "rr%d" % k) for k in range(3)]
            T1 = [R3.tile(gen, 18432 + k * 2048, [128, 512], F32, "t1_%d" % k) for k in range(4)]
            if zero_pads:
                for c0 in (0, 2049, 2050, 2307):
                    OP("pool", "memset", HT.ap[:, :, c0:c0 + 1], 0.0, writes=[HTPAD])
            cnt = [0]

            def stage_a(bi):
                t0, n, col = blocks[bi]
                sq = SQ[0]
                sq2 = SQ[1]
                rs = RS[bi % 2]
                rr = RR[bi % 3]
                xts = xt_all(t0, n)
                OP("act", "activation", sq.ap[:, 0:4, 0:n], XTt[:, 0:4, t0:t0 + n], AF.Square, reads=xts, writes=[sq])
                OP("dve", "tensor_tensor", sq2.ap[:, :, 0:n], XTt[:, 4:8, t0:t0 + n], XTt[:, 4:8, t0:t0 + n], ALU.mult,
                   reads=xts, writes=[sq2])
                pr = ps_next()
                for c in range(8):
                    src = sq.ap[:, c, 0:n] if c < 4 else sq2.ap[:, c - 4, 0:n]
                    MM(pr.ap[:, 0:n], ONES, src, c == 0, c == 7, [sq, sq2, CONST], [pr])
                OP("act", "activation", rs.ap[:, 0:n], pr.ap[:, 0:n], AF.Ln, bias=EPS, scale=1.0 / D,
                   reads=[pr], writes=[rs])
                OP("act", "activation", rr.ap[:, 0:n], rs.ap[:, 0:n], AF.Exp, scale=-0.5, reads=[rs], writes=[rr])

            def stage_b(bi):
                t0, n, col = blocks[bi]
                rr = RR[bi % 3]
                pc0 = pcol(t0)
                hts = ht_tiles(t0, n)
                for c in range(8):
                    t1 = T1[cnt[0] % 4]
                    cnt[0] += 1
                    OP("dve", "tensor_tensor", t1.ap[:, 0:n], XTt[:, c, t0:t0 + n],
                       rr.ap[:, 0:n], ALU.mult, reads=xt_tiles(c, t0, n) + [rr], writes=[t1])
                    if c < 6:
                        OP("act", "activation", HT.ap[:, c, pc0:pc0 + n], t1.ap[:, 0:n], AF.Identity,
                           bias=mdt[:, sh0 + c, col:col + 1], scale=a12t[:, which - 1, c, col:col + 1],
                           reads=[t1, md, a12], writes=hts)
                    else:
                        OP("dve", "tensor_scalar", HT.ap[:, c, pc0:pc0 + n], t1.ap[:, 0:n],
                           a12t[:, which - 1, c, col:col + 1], mdt[:, sh0 + c, col:col + 1], ALU.mult, ALU.add,
                           reads=[t1, md, a12], writes=hts)

            nb_ = len(blocks)
            if external:
                return stage_a, stage_b
            for bi in range(nb_ + 1):
                if bi < nb_:
                    stage_a(bi)
                if hook:
                    hook(bi)
                if bi >= 1:
                    stage_b(bi - 1)

        def ffn(l, last, next_mods):
            gen = "L%d.ffn" % l
            md = MOD[l % 2]
            mdt = MODt[l % 2]
            segs = SEGS[:5] if last else SEGS
            passes = [segs[0:3], segs[3:]]
            HW = 1218
            NWU, NT1, NWD, NAD = 3, 2, 2, 2
            WU = [R3.tile(gen, k * 4096, [128, 8, 256], BF16, "wu%d" % k) for k in range(NWU)]
            T1 = [R3.tile(gen, 12288 + k * 2048, [128, 512], F32, "ft1_%d" % k) for k in range(NT1)]
            T2 = [R3.tile(gen, 16384 + k * 2048, [128, 512], F32, "ft2_%d" % k) for k in range(NT1)]
            ADAI = [R3.tile(gen, 20480 + k * 2048, [128, 8, 128], BF16, "adai%d" % k) for k in range(NAD)]
            WD = [R3.tile(gen, 24576 + k * 5632, [128, NFF, 128], BF16, "wd%d" % k) for k in range(NWD)]
            H = [(R2.tile(gen, j * 2 * HW, [128, HW], BF16, "h%d" % j) if j < 15 else
                  R4.tile(gen, (j - 15) * 2 * HW, [128, HW], BF16, "h%d" % j)) for j in range(NFF)]
            if next_mods:
                advn = ada_w[l + 1].rearrange("(c p) n -> p c n", p=128)
            psm = PS[7]
            P7 = (0, 1, 2, 3, 4, 5, 6)
            mstate = [0]

            def mods_step():
                if not next_mods:
                    return
                g = mstate[0]
                mstate[0] += 1
                if g < 48:
                    a = ADAI[g % NAD]
                    DMA("pool", a.ap, advn[:, :, g * 128:(g + 1) * 128], writes=[a])
                jm = g - (NAD - 1)
                if 0 <= jm < 48:
                    a = ADAI[jm % NAD]
                    for kc in range(8):
                        MM(psm.ap[:, 2 * jm:2 * jm + 2], a.ap[:, kc, :], STt[:, kc, :], kc == 0, kc == 7,
                           [a, ST], [psm])

            ucnt = 0
            for pi, pss in enumerate(passes):
                NPRE = NWU - 1
                for j in range(NFF + NPRE):
                    jj = j
                    if jj < NFF:
                        wt = WU[jj % NWU]
                        DMA("pool", wt.ap.rearrange("p a b -> p (a b)"), wup[l, jj], writes=[wt])
                    j = j - NPRE
                    if j < 0:
                        continue
                    wt = WU[j % NWU]
                    so = 0
                    mods_step()
                    for (t0, n, col) in pss:
                        pc0 = pcol(t0)
                        pg = ps_next(P7)
                        pv = ps_next(P7)
                        hts = ht_halo(t0, n, col)
                        for kc in range(8):
                            MM(pg.ap[:, 0:n + 2], wt.ap[:, kc, 0:128], HT.ap[:, kc, pc0 - 1:pc0 + n + 1],
                               kc == 0, kc == 7, [wt] + hts, [pg])
                        for kc in range(8):
                            MM(pv.ap[:, 0:n], wt.ap[:, kc, 128:256], HT.ap[:, kc, pc0:pc0 + n],
                               kc == 0, kc == 7, [wt] + hts, [pv])
                        t1 = T1[ucnt % NT1]
                        t2 = T2[ucnt % NT1]
                        ucnt += 1
                        OP("act", "activation", t1.ap[:, 0:n], pg.ap[:, 1:n + 1], AF.Identity,
                           bias=CONVt[:, l, 3, j:j + 1], scale=CONVt[:, l, 1, j:j + 1],
                           reads=[pg, CONV], writes=[t1])
                        OP("dve", "scalar_tensor_tensor", t1.ap[:, 0:n], pg.ap[:, 0:n], CONVt[:, l, 0, j:j + 1],
                           t1.ap[:, 0:n], ALU.mult, ALU.add, reads=[pg, CONV, t1], writes=[t1])
                        OP("dve", "scalar_tensor_tensor", t1.ap[:, 0:n], pg.ap[:, 2:n + 2], CONVt[:, l, 2, j:j + 1],
                           t1.ap[:, 0:n], ALU.mult, ALU.add, reads=[pg, CONV, t1], writes=[t1])
                        OP("act", "activation", t2.ap[:, 0:n], t1.ap[:, 0:n], AF.Silu, reads=[t1], writes=[t2])
                        OP("dve", "tensor_tensor", H[j].ap[:, so:so + n], t2.ap[:, 0:n], pv.ap[:, 0:n], ALU.mult,
                           reads=[t2, pv], writes=[H[j]])
                        so += n
                MPRE = NWD - 1
                for m in range(8 + MPRE):
                    mm_ = m
                    if mm_ < 8:
                        wt = WD[mm_ % NWD]
                        DMA("pool", wt.ap.rearrange("p a b -> p (a b)"), wdn[l, mm_], writes=[wt])
                    m = m - MPRE
                    if m < 0:
                        continue
                    wt = WD[m % NWD]
                    so = 0
                    mods_step()
                    for (t0, n, col) in pss:
                        po = ps_next(P7)
                        for kc in range(NFF):
                            MM(po.ap[:, 0:n], wt.ap[:, kc, :], H[kc].ap[:, so:so + n], kc == 0, kc == NFF - 1,
                               [wt, H[kc]], [po])
                        xts = xt_tiles(m, t0, n)
                        OP("dve", "scalar_tensor_tensor", XTt[:, m, t0:t0 + n], po.ap[:, 0:n],
                           mdt[:, 40 + m, col:col + 1], XTt[:, m, t0:t0 + n], ALU.mult, ALU.add,
                           reads=[po, md] + xts, writes=xts)
                        so += n
            if next_mods:
                while mstate[0] < 48 + NAD:
                    mods_step()
                mods_final(l + 1)

        def mk_w(gen, woff=0):
            return [R4.tile(gen, woff + k * 2048, [128, 8, 128], BF16, "w%d" % k) for k in range(3)]

        def proj_stream(W, wsrc_fn, nj, blocks, rhs_fn, rhs_tiles_fn, evac_fn, flush_fn=None, post_fn=None,
                        blocks_fn=None):
            NW = len(W)
            for j in range(nj + NW - 1):
                if j < nj:
                    wt = W[j % NW]
                    DMA("pool", wt.ap.rearrange("p a b -> p (a b)"), wsrc_fn(j), writes=[wt])
                jj = j - (NW - 1)
                if jj < 0:
                    continue
                wt = W[jj % NW]
                for (t0, n, col) in (blocks_fn(jj) if blocks_fn else blocks):
                    p = ps_next()
                    for kc in range(8):
                        MM(p.ap[:, 0:n], wt.ap[:, kc, :], rhs_fn(kc, t0, n), kc == 0, kc == 7,
                           [wt] + rhs_tiles_fn(kc, t0, n), [p])
                    evac_fn(jj, p, t0, n, col)
                if post_fn:
                    post_fn(jj)
            if flush_fn:
                flush_fn()

        def out_proj_overlap(l, gen, wsrc_fn, blocks, rhs_fn, rhs_tiles_fn, last):
            WR = [R4.tile(gen, k * 2048, [128, 8, 128], BF16, "wr%d" % k) for k in range(8)]
            for m in range(8):
                DMA("pool", WR[m].ap.rearrange("p a b -> p (a b)"), wsrc_fn(m), writes=[WR[m]])
            ev = resid_evac(l, 16)
            mblocks = TB512[:4] if last else TB512
            assert list(mblocks) == list(blocks)
            sa, sb_ = modulate(l, 2, mblocks, True, external=True)
            nb_ = len(blocks)
            for b in range(nb_ + 2):
                if b < nb_:
                    t0, n, col = blocks[b]
                    for m in range(8):
                        p = ps_next()
                        for kc in range(8):
                            MM(p.ap[:, 0:n], WR[m].ap[:, kc, :], rhs_fn(kc, t0, n), kc == 0, kc == 7,
                               [WR[m]] + rhs_tiles_fn(kc, t0, n), [p])
                        ev(m, p, t0, n, col)
                if 1 <= b <= nb_:
                    sa(b - 1)
                if 2 <= b:
                    sb_(b - 2)
            sb_(nb_ - 1)

        def resid_evac(l, which_g):
            md = MOD[l % 2]
            mdt = MODt[l % 2]

            def f(m, p, t0, n, col):
                xts = xt_tiles(m, t0, n)
                OP("dve", "scalar_tensor_tensor", XTt[:, m, t0:t0 + n], p.ap[:, 0:n],
                   mdt[:, which_g + m, col:col + 1], XTt[:, m, t0:t0 + n], ALU.mult, ALU.add,
                   reads=[p, md] + xts, writes=xts)
            return f

        def even_mixer(l, last):
            i = l // 2
            gen = "L%d.even" % l
            blocks = TB512[:4] if last else TB512
            nun = 16 if last else 18
            FT = R2.tile(gen, 0, [128, 4, NT], BF16, "FT")
            UT = R2.tile(gen, 18432, [128, 4, NT], BF16, "UT")
            FTU = [[Tile(None, "ft%d_%d" % (g, u), FT.grans, gen) for u in range(18)] for g in range(4)]
            UTU = [[Tile(None, "ut%d_%d" % (g, u), UT.grans, gen) for u in range(18)] for g in range(4)]

            def fu(TU, g, t0, n):
                return [TU[g][u] for u in range(t0 // 128, (t0 + n - 1) // 128 + 1)]

            def evac_a(j, p, t0, n, col):
                if j < 4:
                    OP("dve", "tensor_copy", FT.ap[:, j, t0:t0 + n], p.ap[:, 0:n], reads=[p], writes=fu(FTU, j, t0, n))
                else:
                    OP("act", "activation", UT.ap[:, j - 4, t0:t0 + n], p.ap[:, 0:n], AF.Gelu, reads=[p],
                       writes=fu(UTU, j - 4, t0, n))
            proj_stream(mk_w(gen), lambda j: win[i, j], 8, blocks,
                        lambda kc, t0, n: HT.ap[:, kc, pcol(t0):pcol(t0) + n],
                        lambda kc, t0, n: ht_tiles(t0, n), evac_a)

            WV = R4.tile(gen, 6144, [128, 8, 512], BF16, "wvin")
            WS = R4.tile(gen, 14336, [128, 4, 128], BF16, "wsT")
            BS = R4.tile(gen, 15360, [1, 512], BF16, "bs")
            DMA("pool", WV.ap.rearrange("p a b -> p (a b)"), winv[i], writes=[WV])
            DMA("pool", WS.ap.rearrange("p a b -> p (a b)"), wsT[i], writes=[WS])
            DMA("pool", BS.ap, bsd[i], writes=[BS])
            VN = [R3.tile(gen, 18432 + k * 4096, [128, 4, 4, 128], BF16, "vn%d" % k) for k in range(2)]
            VF = [R3.tile(gen, 26624 + k * 2048, [128, 512], F32, "vf%d" % k) for k in range(2)]
            SQV = [R3.tile(gen, 30720 + k * 2048, [128, 512], F32, "sqv%d" % k) for k in range(2)]
            SSV = [R3.tile(gen, 34816 + k * 64, [128, 4], F32, "ssv%d" % k) for k in range(2)]
            RSV = [R3.tile(gen, 34944 + k * 64, [128, 4], F32, "rsv%d" % k) for k in range(2)]
            RRV = [R3.tile(gen, 35072 + k * 64, [128, 4], F32, "rrv%d" % k) for k in range(2)]
            NEGH = R3.tile(gen, 35200, [128, 4], F32, "negh")
            OP("pool", "memset", NEGH.ap, -0.5, writes=[NEGH])
            groups = [(512 * g, 4) for g in range(4)] + ([] if last else [(2048, 2)])
            cnt = 0
            sgu_pend = []
            for gi, (t0g, ntt) in enumerate(groups):
                vn = VN[gi % 2]
                for tt in range(ntt):
                    t0 = t0g + tt * 128
                    pc0 = pcol(t0)
                    p = ps_next()
                    for kc in range(8):
                        MM(p.ap[:, :], HT.ap[:, kc, pc0:pc0 + 128], WV.ap[:, kc, :], kc == 0, kc == 7,
                           [WV] + ht_tiles(t0, 128), [p])
                    vf = VF[cnt % 2]
                    sqv = SQV[cnt % 2]
                    ssv = SSV[cnt % 2]
                    rsv = RSV[cnt % 2]
                    rrv = RRV[cnt % 2]
                    cnt += 1
                    OP("act", "activation", vf.ap, p.ap, AF.Gelu, reads=[p], writes=[vf])
                    OP("pool", "tensor_tensor", sqv.ap, vf.ap, vf.ap, ALU.mult, reads=[vf], writes=[sqv])
                    OP("dve", "tensor_reduce", ssv.ap, sqv.ap.rearrange("p (a b) -> p a b", a=4), AX.X, ALU.add,
                       reads=[sqv], writes=[ssv])
                    OP("dve", "tensor_scalar", rsv.ap, ssv.ap, 1.0 / 128, EPS, ALU.mult, ALU.add,
                       reads=[ssv], writes=[rsv])
                    OP("pool", "tensor_tensor", rrv.ap, rsv.ap, NEGH.ap, ALU.pow, reads=[rsv, NEGH], writes=[rrv])
                    OP("dve", "tensor_tensor", vn.ap[:, tt], vf.ap.rearrange("p (a b) -> p a b", a=4),
                       rrv.ap.unsqueeze(2).to_broadcast([128, 4, 128]), ALU.mult, reads=[vf, rrv], writes=[vn])
                def sgu(vn=vn, t0g=t0g, ntt=ntt):
                    for g in range(4):
                        p = ps_next()
                        for tt in range(ntt):
                            MM(p.ap[:, tt * 128:(tt + 1) * 128], vn.ap[:, tt, g, :], WS.ap[:, g, :], True, False,
                               [vn, WS], [p])
                            MM(p.ap[:, tt * 128:(tt + 1) * 128], ONES[0:1, :], BS.ap[0:1, g * 128:(g + 1) * 128],
                               False, True, [BS, CONST], [p])
                        n = ntt * 128
                        uts = fu(UTU, g, t0g, n)
                        OP("dve", "tensor_tensor", UT.ap[:, g, t0g:t0g + n], UT.ap[:, g, t0g:t0g + n], p.ap[:, 0:n],
                           ALU.mult, reads=[p] + uts, writes=uts)
                if sgu_pend:
                    sgu_pend.pop(0)()
                sgu_pend.append(sgu)
            while sgu_pend:
                sgu_pend.pop(0)()

            gen2 = "L%d.four" % l
            CS = R4.tile(gen2, 6144, [128, 2, 256], BF16, "cs")
            DMA("pool", CS.ap.rearrange("p a b -> p (a b)"), dftC, writes=[CS])
            PQ = [R3.tile(gen2, t * 2048, [128, 4, 256], BF16, "pq%d" % t) for t in range(18)]
            for t in range(nun):
                v = 0 if t < 16 else 1
                for gp in range(2):
                    p = ps_next()
                    for g2 in range(2):
                        g = 2 * gp + g2
                        MM(p.ap[:, g2 * 256:(g2 + 1) * 256], FT.ap[:, g, t * 128:(t + 1) * 128], CS.ap[:, v, :],
                           True, True, [CS] + fu(FTU, g, t * 128, 128), [p])
                    if (t + gp) % 2 == 0:
                        OP("act", "activation", PQ[t].ap[:, 2 * gp:2 * gp + 2, :].rearrange("p a b -> p (a b)"),
                           p.ap, AF.Copy, reads=[p], writes=[PQ[t]])
                    else:
                        OP("dve", "tensor_copy", PQ[t].ap[:, 2 * gp:2 * gp + 2, :].rearrange("p a b -> p (a b)"),
                           p.ap, reads=[p], writes=[PQ[t]])
            DT = [[R1.tile(gen2, (b * 2 + mtx) * 8192, [128, 16, 256], BF16, "dt%d_%d" % (b, mtx)) for mtx in range(2)]
                  for b in range(2)]
            dlv = [dftL[mtx].rearrange("(c p) n -> p c n", p=128) for mtx in range(2)]
            ecnt = 0
            for kt in range(8):
                d = DT[kt % 2]
                for mtx in range(2):
                    DMA("pool", d[mtx].ap, dlv[mtx][:, :, kt * 256:(kt + 1) * 256], writes=[d[mtx]])
                for g in range(4):
                    p = ps_next()
                    for lc in range(16):
                        MM(p.ap[:, 0:256], PQ[lc].ap[:, g, 0:128], d[0].ap[:, lc, :], lc == 0, False,
                           [PQ[lc], d[0]], [p])
                        MM(p.ap[:, 0:256], PQ[lc].ap[:, g, 128:256], d[1].ap[:, lc, :], False, lc == 15,
                           [PQ[lc], d[1]], [p])
                    fts = fu(FTU, g, kt * 256, 256)
                    if ecnt % 2 == 0:
                        OP("act", "activation", FT.ap[:, g, kt * 256:(kt + 1) * 256], p.ap[:, 0:256], AF.Copy,
                           reads=[p], writes=fts)
                    else:
                        OP("dve", "tensor_copy", FT.ap[:, g, kt * 256:(kt + 1) * 256], p.ap[:, 0:256],
                           reads=[p], writes=fts)
                    ecnt += 1
            if not last:
                D2 = R4.tile(gen2, 8192, [128, 2, 2, 256], BF16, "d256")
                for mtx in range(2):
                    DMA("pool", D2.ap[:, mtx], dft256[mtx].rearrange("(c p) n -> p c n", p=128), writes=[D2])
                for g in range(4):
                    p = ps_next()
                    for lc in range(2):
                        MM(p.ap[:, 0:256], PQ[16 + lc].ap[:, g, 0:128], D2.ap[:, 0, lc, :], lc == 0, False,
                           [PQ[16 + lc], D2], [p])
                        MM(p.ap[:, 0:256], PQ[16 + lc].ap[:, g, 128:256], D2.ap[:, 1, lc, :], False, lc == 1,
                           [PQ[16 + lc], D2], [p])
                    OP("act", "activation", FT.ap[:, g, 2048:2304], p.ap[:, 0:256], AF.Copy, reads=[p],
                       writes=fu(FTU, g, 2048, 256))

            def rhs_d(kc, t0, n):
                return FT.ap[:, kc, t0:t0 + n] if kc < 4 else UT.ap[:, kc - 4, t0:t0 + n]

            def rhs_t(kc, t0, n):
                return fu(FTU, kc, t0, n) if kc < 4 else fu(UTU, kc - 4, t0, n)
            out_proj_overlap(l, "L%d.eout" % l, lambda m: wout[i, m], blocks, rhs_d, rhs_t, last)

        def odd_mixer(l, last):
            i = l // 2
            gen = "L%d.odd" % l
            KT = R2.tile(gen, 0, [128, 8, NT], BF16, "KT")
            KTU = [[Tile(None, "kt%d_%d" % (c, u), KT.grans, gen) for u in range(18)] for c in range(8)]
            VT = [R3.tile(gen, t * 2048, [128, 1024], BF16, "vt%d" % t) for t in range(18)]

            def ktu(c, t0, n):
                return [KTU[c][u] for u in range(t0 // 128, (t0 + n - 1) // 128 + 1)]

            def norm_evac(W0, gsrc, dst_fn, bs, gen, nset, depth=2, sq_tiles=None):
                if sq_tiles is None:
                    SQ = [R4.tile(gen, W0 + k * bs * 2, [128, bs], BF16, "qsq%d" % k) for k in range(nset)]
                    o2 = W0 + nset * bs * 2
                else:
                    SQ = sq_tiles
                    o2 = W0
                RS = [R4.tile(gen, o2 + k * bs * 4, [128, bs], F32, "qrs%d" % k) for k in range(nset)]
                assert o2 + nset * bs * 4 <= 18432
                cnt = [0]
                pend = []

                def flush(keep=0):
                    while len(pend) > keep:
                        pend.pop(0)()

                def f(j, p, t0, n, col):
                    k = cnt[0] % nset
                    cnt[0] += 1
                    sq, rs = SQ[k], RS[k]
                    OP("act", "activation", sq.ap[:, 0:n], p.ap[:, 0:n], AF.Square, reads=[p], writes=[sq])

                    def part2():
                        p2 = ps_next()
                        MM(p2.ap[:, 0:n], BLK, sq.ap[:, 0:n], True, True, [sq, CONST], [p2])
                        OP("act", "activation", rs.ap[:, 0:n], p2.ap[:, 0:n], AF.Ln, bias=EPS, scale=1.0 / 64,
                           reads=[p2], writes=[rs])
                        OP("act", "activation", rs.ap[:, 0:n], rs.ap[:, 0:n], AF.Exp, scale=-0.5, reads=[rs], writes=[rs])
                        dst, dtiles = dst_fn(j, t0, n)
                        OP("dve", "scalar_tensor_tensor", dst, p.ap[:, 0:n], gsrc, rs.ap[:, 0:n], ALU.mult, ALU.mult,
                           reads=[p, rs, GQS, QKG], writes=dtiles)
                    pend.append(part2)
                    flush(depth)
                return f, flush

            kev, kflush = norm_evac(6144, QKGt[:, i, 1:2],
                                    lambda j, t0, n: (KT.ap[:, j, t0:t0 + n], ktu(j, t0, n)), 512, gen, 3)
            proj_stream(mk_w(gen), lambda j: wqk[i, 8 + j], 8, TB512,
                        lambda kc, t0, n: HT.ap[:, kc, pcol(t0):pcol(t0) + n],
                        lambda kc, t0, n: ht_tiles(t0, n), kev, flush_fn=kflush)

            genv = "L%d.oddv" % l
            WVt = [R4.tile(genv, k * 8192, [128, 8, 512], BF16, "wvo%d" % k) for k in range(2)]
            ec = 0
            for nt_ in range(2):
                DMA("pool", WVt[nt_].ap.rearrange("p a b -> p (a b)"), wv[i, nt_], writes=[WVt[nt_]])
            for nt_ in range(2):
                for t in range(18):
                    p = ps_next()
                    pc0 = pcol(t * 128)
                    for kc in range(8):
                        MM(p.ap[:, :], HT.ap[:, kc, pc0:pc0 + 128], WVt[nt_].ap[:, kc, :], kc == 0, kc == 7,
                           [WVt[nt_]] + ht_tiles(t * 128, 128), [p])
                    if ec % 2 == 0:
                        OP("act", "activation", VT[t].ap[:, nt_ * 512:(nt_ + 1) * 512], p.ap, AF.Copy,
                           reads=[p], writes=[VT[t]])
                    else:
                        OP("dve", "tensor_copy", VT[t].ap[:, nt_ * 512:(nt_ + 1) * 512], p.ap,
                           reads=[p], writes=[VT[t]])
                    ec += 1

            genq = "L%d.oddq" % l
            qpass = TB512[:4] if last else TB512
            QTMP = R4.tile(genq, 6144, [128, 8, 512], BF16, "qtmp")
            WQ = mk_w(genq)
            sqz = []
            for kz in range(3):
                tz = Tile(QZt[:, kz, 0:256], "qzsq%d" % kz)
                tz.buf = QZ[kz].buf
                sqz.append(tz)
            qev, qflush = norm_evac(14336, GQSt[:, i, 0:1],
                                    lambda j, t0_, n_: (QTMP.ap[:, j % 8, (t0_ % 512):(t0_ % 512) + n_], [QTMP]),
                                    256, genq, 3, sq_tiles=sqz)

            def qpost(jj):
                if jj % 8 == 7:
                    qflush()
                    t0, n, col = qpass[jj // 8]
                    pc0 = pcol(t0)
                    OP("dve", "tensor_copy", HT.ap[:, :, pc0:pc0 + n], QTMP.ap[:, :, 0:n], reads=[QTMP],
                       writes=ht_tiles(t0, n))

            def qblocks_fn(jj):
                t0, n, col = qpass[jj // 8]
                return [(t0 + 256 * s_, 256, col) for s_ in range(n // 256)]
            proj_stream(WQ, lambda j: wqk[i, j % 8], 8 * len(qpass), None,
                        lambda kc, t0_, n_: HT.ap[:, kc, pcol(t0_):pcol(t0_) + n_],
                        lambda kc, t0_, n_: ht_tiles(t0_, n_), qev, post_fn=qpost, blocks_fn=qblocks_fn)

            gena = "L%d.attn" % l
            TAB = [R4.tile(gena, k * 4992, [128, NE, 64], BF16, "tab%d" % k) for k in range(2)]
            PT = [R4.tile(gena, 9984 + k * 1024, [128, 512], BF16, "pt%d" % k) for k in range(4)]
            RD = [R4.tile(gena, 14080 + k * 2048, [128, 512], F32, "rd%d" % k) for k in range(2)]
            qbs = []
            qbs.append((0, 256, [(kr0, NE_WIN + (7 - kr0)) for kr0 in (0, 2, 4, 6)]))
            for k in range(3):
                qr0 = 4 + 8 * k
                qbs.append((64 * qr0, 512, [(kr0, qr0 - kr0 + 7 + 4) for kr0 in range(qr0 - 4, qr0 + 12, 2)]))
            qbs.append((1792, 256, [(kr0, NE_WIN + (35 - kr0)) for kr0 in (24, 26, 28, 30)]))
            if not last:
                qbs.append((2048, 256, []))
            SK = int(os.environ.get('K_SK', '2'))
            items = []
            ocnt = 0
            for h in range(16):
                for (qt0, nq, loc) in qbs:
                    chunks = [(kr0 // 2, ei0) for (kr0, ei0) in loc] + [(16, None), (17, None)]
                    for ci, (t, ei0) in enumerate(chunks):
                        items.append((h, qt0, nq, t, ei0, ci == 0, ci == len(chunks) - 1, ocnt))
                    ocnt += 1
            staged = []
            tab_loaded = -1
            for kz in range(4):
                OP("pool", "memset", QZt[:, kz, :], 0.0, writes=[QZ[kz]])
            qz_cur = {}
            qz_cnt = [0, 0]
            for k in range(len(items) + SK):
                if k < len(items):
                    h, qt0, nq, t, ei0, first, lastc, oc = items[k]
                    i2 = h // 2
                    rows = slice(64 * (h % 2), 64 * (h % 2) + 64)
                    tb = TAB[h % 2]
                    if tab_loaded < h:
                        DMA("pool", tb.ap.rearrange("p a b -> p (a b)"), tabd[i, h], writes=[tb])
                        tab_loaded = h
                    pq0 = pcol(qt0)
                    qtiles = ht_tiles(qt0, nq)
                    ps_ = PS[k % 3]
                    pt = PT[k % 4]
                    use_qz = os.environ.get('K_NOQZ') != '1'
                    if not use_qz:
                        MM(ps_.ap[:, 0:nq], KT.ap[rows, i2, t * 128:(t + 1) * 128], HT.ap[rows, i2, pq0:pq0 + nq],
                           True, ei0 is None, ktu(i2, t * 128, 128) + qtiles, [ps_])
                    if first and use_qz:
                        par = h % 2
                        qz = QZ[2 * par + qz_cnt[par] % 2]
                        qz_cnt[par] += 1
                        OP("pool", "tensor_copy", qz.ap[rows, 0:nq], HT.ap[rows, i2, pq0:pq0 + nq], reads=qtiles,
                           writes=[qz])
                        qz_cur[oc] = qz
                    if use_qz:
                        qz = qz_cur[oc]
                        MM(ps_.ap[:, 0:nq], KT.ap[:, i2, t * 128:(t + 1) * 128], qz.ap[:, 0:nq],
                           True, ei0 is None, ktu(i2, t * 128, 128) + [qz], [ps_])
                    if ei0 is not None:
                        MM(ps_.ap[:, 0:nq], IDENT, tb.ap[:, ei0:ei0 + nq // 64, :].rearrange("p a b -> p (a b)"),
                           False, True, [tb, CONST], [ps_])
                    OP("act", "activation", pt.ap[:, 0:nq], ps_.ap[:, 0:nq], AF.Exp, reads=[ps_], writes=[pt])
                    staged.append(pt)
                kk = k - SK
                if kk >= 0:
                    h, qt0, nq, t, ei0, first, lastc, oc = items[kk]
                    i2 = h // 2
                    rows = slice(64 * (h % 2), 64 * (h % 2) + 64)
                    pt_ = staged[kk]
                    po = PS[3 + 2 * (oc % 2)]
                    pd = PS[4 + 2 * (oc % 2)]
                    MM(po.ap[:, 0:nq], VT[t].ap[:, i2 * 128:(i2 + 1) * 128], pt_.ap[:, 0:nq], first, lastc,
                       [VT[t], pt_], [po])
                    MM(pd.ap[:, 0:nq], ONES, pt_.ap[:, 0:nq], first, lastc, [pt_, CONST], [pd])
                    if lastc:
                        pq0 = pcol(qt0)
                        rd = RD[oc % 2]
                        OP("dve", RECIP, rd.ap[rows, 0:nq], pd.ap[rows, 0:nq], reads=[pd], writes=[rd])
                        OP("dve", "tensor_tensor", HT.ap[rows, i2, pq0:pq0 + nq], po.ap[rows, 0:nq],
                           rd.ap[rows, 0:nq], ALU.mult, reads=[po, rd], writes=ht_tiles(qt0, nq))

            blocks = TB512[:4] if last else TB512
            genw = "L%d.wo" % l
            out_proj_overlap(l, genw, lambda m: wo[i, m], blocks,
                             lambda kc, t0, n: HT.ap[:, kc, pcol(t0):pcol(t0) + n],
                             lambda kc, t0, n: ht_tiles(t0, n), last)

        for l in range(n_layers):
            last = (l == DEPTH - 1)
            if l == 0 and os.environ.get('K_MODS0') != 'old':
                hook0, fin0 = mods0()
                ps_default[0] = (0, 1, 2, 3, 4, 5, 6)
                modulate(l, 1, TB512, False, hook=hook0)
                fin0()
                ps_default[0] = (0, 1, 2, 3, 4, 5, 6, 7)
            else:
                if l == 0:
                    mods(l)
                modulate(l, 1, TB512 if (not last or l % 2 == 1) else TB512[:4], False)
            if l % 2 == 0:
                even_mixer(l, last)
            else:
                odd_mixer(l, last)
            nm = (l + 1 < n_layers) and os.environ.get('K_NOMODS') != '1'
            ffn(l, last, nm)
            if (l + 1 < n_layers) and (not nm or os.environ.get('K_BOTH') == '1'):
                mods(l + 1)

        outs = []
        for c in range(8):
            outs.append(DMA("sp", yout[c * 128:(c + 1) * 128, :], XTt[:, c, 0:n_out], reads=XTT[c]))
        S.wait_ops("sp", outs)
        S.emit(ctx)
    return nc


def _consts():
    ident = np.eye(128, dtype=np.float32)
    ones = np.ones((128, 128), np.float32)
    blk = np.zeros((128, 128), np.float32)
    blk[:64, :64] = 1.0
    blk[64:, 64:] = 1.0
    cst = np.stack([ident, ones, blk], axis=1).reshape(128, 384)
    ll = np.arange(L, dtype=np.float64)
    ang = 2.0 * np.pi * ((ll[:, None] * ll[None, :]) % L) / L
    dftL = np.stack([np.cos(ang), -np.sin(ang)]).astype(np.float32)
    l2 = np.arange(256, dtype=np.float64)
    ang2 = 2.0 * np.pi * ((l2[:, None] * l2[None, :]) % 256) / 256
    dft256 = np.stack([np.cos(ang2), -np.sin(ang2)]).astype(np.float32)
    cc = np.arange(128, dtype=np.float64)
    angc = 2.0 * np.pi * ((cc[:, None] * cc[None, :]) % 128) / 128
    dftC = np.zeros((128, 2, 256), np.float64)
    for v, n in enumerate((L, 256)):
        s = 1.0 / np.sqrt(n * 128.0)
        dftC[:, v, :128] = np.cos(angc) * s
        dftC[:, v, 128:] = np.sin(angc) * s
    return cst, dftL, dft256, dftC.reshape(128, 512).astype(np.float32)


def _rpb_tables(rpb):
    NEG = np.float32(-1e30)
    kc = np.arange(64)[:, None]
    qc = np.arange(64)[None, :]
    dc = np.clip(kc - qc + 15, 0, 30)
    cs = np.clip(qc - 8, 0, 48)
    valid = (kc >= cs) & (kc < cs + 16)
    out = np.full((2, 16, 128, NE, 64), NEG, np.float32)
    for e in range(15):
        B = np.where(valid[None, None], rpb[:, :, 14 - e][:, :, dc], NEG)
        for krl in range(2):
            ee = e + krl
            if 4 <= e <= 11 and 0 <= ee + 4 < NE_WIN:
                out[:, :, krl * 64:(krl + 1) * 64, ee + 4, :] = B
            if 0 <= ee < 16:
                out[:, :, krl * 64:(krl + 1) * 64, NE_WIN + ee, :] = B
    return out.reshape(2, 16, 128, NE * 64)


def _prep(inputs):
    f = lambda a: np.ascontiguousarray(np.asarray(a, dtype=np.float32))
    x = f(inputs["x"]); c = f(inputs["c"]); cx = f(inputs["ctx"]); c_ctx = f(inputs["c_ctx"])
    cst, dftL, dft256, dftC = _consts()
    sh = {}
    sh["ada_w"] = f(inputs["ada_w"])
    sh["ada_b_r"] = f(f(inputs["ada_b"]).reshape(4, 48, 128).transpose(2, 0, 1).reshape(128, 192))
    ng = np.stack([f(inputs["norm1_g"]), f(inputs["norm2_g"])])
    sh["ng"] = f(ng.reshape(2, 4, 8, 128).transpose(3, 0, 1, 2).reshape(128, 64))
    cw = f(inputs["ffn_conv_w"]); cb = f(inputs["ffn_conv_b"])
    cp = np.concatenate([cw, cb[:, None, :]], axis=1)
    sh["convp"] = f(cp.reshape(4, 4, NFF, 128).transpose(3, 0, 1, 2).reshape(128, 4 * 4 * NFF))
    wu = f(inputs["ffn_w_up"]).reshape(4, 8, 128, 2, NFF, 128)
    sh["wup_r"] = f(wu.transpose(0, 4, 2, 1, 3, 5).reshape(4, NFF, 128, 8 * 256))
    wd = f(inputs["ffn_w_down"]).reshape(4, NFF, 128, 8, 128)
    sh["wdn_r"] = f(wd.transpose(0, 3, 2, 1, 4).reshape(4, 8, 128, NFF * 128))
    wi = f(inputs["even_w_in"])
    wfu = wi[:, :, :1024].reshape(2, 8, 128, 8, 128)
    sh["win_r"] = f(wfu.transpose(0, 3, 2, 1, 4).reshape(2, 8, 128, 1024))
    wvv = wi[:, :, 1024:].reshape(2, 8, 128, 512)
    sh["winv_r"] = f(wvv.transpose(0, 2, 1, 3).reshape(2, 128, 4096))
    sh["wsT"] = f(f(inputs["even_w_s"]).transpose(0, 3, 1, 2).reshape(2, 128, 512))
    sh["bs"] = f(f(inputs["even_b_s"]).reshape(2, 1, 512))
    wo_ = f(inputs["even_w_out"]).reshape(2, 8, 128, 8, 128)
    sh["wout_r"] = f(wo_.transpose(0, 3, 2, 1, 4).reshape(2, 8, 128, 1024))
    wq = f(inputs["odd_w_qkv"])
    wqk_ = wq[:, :, :2048].reshape(2, 8, 128, 16, 128)
    sh["wqk_r"] = f(wqk_.transpose(0, 3, 2, 1, 4).reshape(2, 16, 128, 1024))
    wv_ = wq[:, :, 2048:].reshape(2, 8, 128, 2, 512)
    sh["wv_r"] = f(wv_.transpose(0, 3, 2, 1, 4).reshape(2, 2, 128, 4096))
    woo = f(inputs["odd_w_o"]).reshape(2, 8, 128, 8, 128)
    sh["wo_r"] = f(woo.transpose(0, 3, 2, 1, 4).reshape(2, 8, 128, 1024))
    qg = f(inputs["odd_q_g"]); kg = f(inputs["odd_k_g"])
    qk = np.stack([qg, kg], axis=1)
    qk = np.concatenate([qk, qk], axis=2)
    sh["qkg"] = f(qk.transpose(2, 0, 1).reshape(128, 4))
    sh["tab"] = _rpb_tables(f(inputs["odd_rpb"]))
    sh["dftL"] = dftL
    sh["dftC"] = dftC
    sh["dft256"] = dft256
    sh["cst"] = cst
    in_maps = []
    for b in range(8):
        m = dict(sh)
        m["xin"] = f(np.concatenate([x[b].T, cx[b].T], axis=1))
        sv = np.stack([c[b], c_ctx], axis=1)
        m["svec"] = f(sv.reshape(8, 128, 2).transpose(1, 0, 2).reshape(128, 16))
        in_maps.append(m)
    return in_maps


_NC_CACHE = {}


def kernel(**inputs):
    in_maps = _prep(inputs)
    if "nc" not in _NC_CACHE:
        _NC_CACHE["nc"] = build()
    nc = _NC_CACHE["nc"]
    res = run_bass_kernel_spmd(nc, in_maps, core_ids=list(range(8)))
    out = np.stack([np.ascontiguousarray(r["yout"].T) for r in res.results], axis=0)
    return out.astype(np.float32)
```

```python
from contextlib import ExitStack
import os
import numpy as np
import concourse.bass as bass
import concourse.mybir as mybir
from concourse.bass_utils import run_bass_kernel_spmd

F32 = mybir.dt.float32
BF16 = mybir.dt.bfloat16
ALU = mybir.AluOpType
AF = mybir.ActivationFunctionType
AX = mybir.AxisListType

ENGS = ("pe", "act", "dve", "pool", "sp")
N_DMA_SEMS = 40
GRAN = 2048


class Buf:
    __slots__ = ("name", "writes", "reads")

    def __init__(self, name=""):
        self.name = name
        self.writes = []
        self.reads = []


class Region:
    __slots__ = ("gen", "cur", "prev")

    def __init__(self):
        self.gen = None
        self.cur = {}
        self.prev = {}


class Tile:
    __slots__ = ("ap", "buf", "grans", "gen", "excl")

    def __init__(self, ap, name="", grans=(), gen=None):
        self.excl = False
        self.ap = ap
        self.buf = Buf(name)
        self.grans = grans
        self.gen = gen


class Arena:
    def __init__(self, tensor, nbytes):
        self.t = tensor
        self.nbytes = nbytes
        self.regs = [Region() for _ in range((nbytes + GRAN - 1) // GRAN)]
        self.whole = Region()
        self.use_whole = False

    def tile(self, gen, off, shape, dtype, name=""):
        esz = 4 if dtype == F32 else 2
        n = 1
        for s in shape[1:]:
            n *= s
        nb = n * esz
        assert off % 4 == 0 and off + nb <= self.nbytes, (name, off, nb, self.nbytes)
        ap = self.t[:, off // 2:(off + nb) // 2]
        if dtype == F32:
            ap = ap.bitcast(F32)
        if len(shape) == 3:
            ap = ap.rearrange("p (a b) -> p a b", a=shape[1])
        elif len(shape) == 4:
            ap = ap.rearrange("p (a b c) -> p a b c", a=shape[1], b=shape[2])
        if shape[0] != 128:
            ap = ap[0:shape[0]]
        grans = tuple(self.regs[off // GRAN:(off + nb - 1) // GRAN + 1])
        if self.use_whole:
            grans = grans + (self.whole,)
        return Tile(ap, name, grans, gen)


class Op:
    __slots__ = ("dom", "idx", "fn", "vc", "waits", "signal", "stream", "is_dma")


class Sched:
    def __init__(self, nc):
        self.nc = nc
        self.streams = {e: [] for e in ENGS}
        self.vc = {e: {} for e in ENGS}
        self.count = {}
        self.dma_last = [None] * N_DMA_SEMS
        self.dma_rr = {"pool": 0, "sp": 0, "act": 0}

    def _deps(self, reads, writes):
        deps = []
        for t in reads:
            for o in t.buf.writes:
                deps.append((o, "raw"))
            if t.excl:
                for o in t.buf.reads[-4:]:
                    deps.append((o, "rar"))
        for t in writes:
            b = t.buf
            for o in b.writes:
                deps.append((o, "waw"))
            for o in b.reads:
                deps.append((o, "war"))
        for t in list(reads) + list(writes):
            for r in t.grans:
                if r.gen != t.gen:
                    r.prev = r.cur
                    r.cur = {}
                    r.gen = t.gen
                for o in r.prev.values():
                    deps.append((o, "raw"))
        return deps

    def _book(self, op, reads, writes):
        for t in reads:
            t.buf.reads.append(op)
            if len(t.buf.reads) > 96:
                t.buf.reads = t.buf.reads[-96:]
        for t in writes:
            b = t.buf
            if b.reads:
                b.writes = [op]
                b.reads = []
            else:
                b.writes.append(op)
                if len(b.writes) > 64:
                    b.writes = b.writes[-64:]
        for t in list(reads) + list(writes):
            for r in t.grans:
                r.cur[op.dom] = op

    def _resolve(self, stream, deps):
        base = self.vc[stream]
        waits = {}
        for d, kind in deps:
            if d.dom == stream:
                if stream == "pe" or stream == "sp":
                    continue
                if kind == "rar":
                    continue
            if base.get(d.dom, 0) >= d.idx:
                continue
            d.signal = True
            waits[d.dom] = max(waits.get(d.dom, 0), d.idx)
            for k, v in d.vc.items():
                if base.get(k, 0) < v:
                    base[k] = v
            base[d.dom] = max(base.get(d.dom, 0), d.idx)
        return waits

    def _mk(self, dom, stream, fn, deps, is_dma):
        o = Op()
        o.dom = dom
        o.stream = stream
        o.is_dma = is_dma
        o.idx = self.count.get(dom, 0) + 1
        self.count[dom] = o.idx
        o.fn = fn
        o.signal = is_dma
        o.waits = self._resolve(stream, deps)
        o.vc = dict(self.vc[stream])
        self.streams[stream].append(o)
        return o

    def op(self, eng, fn, reads=(), writes=()):
        deps = self._deps(reads, writes)
        o = self._mk(eng, eng, fn, deps, False)
        self._book(o, reads, writes)
        return o

    def dma(self, queue, fn, reads=(), writes=()):
        deps = self._deps(reads, writes)
        lo, n = (0, N_DMA_SEMS - 8) if queue == "pool" else (N_DMA_SEMS - 8, 8)
        k = lo + self.dma_rr[queue] % n
        self.dma_rr[queue] += 1
        prev = self.dma_last[k]
        if prev is not None:
            deps.append((prev, "raw"))
        o = self._mk("d%d" % k, queue, fn, deps, True)
        self.dma_last[k] = o
        self._book(o, reads, writes)
        return o

    def wait_ops(self, eng, ops):
        return self._mk(eng, eng, None, [(o, "raw") for o in ops], False)

    def emit(self, ctx):
        nc = self.nc
        sems = {}
        for e in ENGS:
            sems[e] = ctx.enter_context(nc.semaphore("s_" + e))
        for k in range(N_DMA_SEMS):
            sems["d%d" % k] = ctx.enter_context(nc.semaphore("s_d%d" % k))
        ticket = {}
        for e in ENGS:
            t = 0
            for o in self.streams[e]:
                if o.is_dma:
                    ticket[(o.dom, o.idx)] = 16 * o.idx
                elif o.signal:
                    t += 1
                    ticket[(o.dom, o.idx)] = t
        handles = {"pe": nc.tensor, "act": nc.scalar, "dve": nc.vector,
                   "pool": nc.gpsimd, "sp": nc.sync}

        def run(e):
            h = handles[e]
            for o in self.streams[e]:
                for dom, idx in o.waits.items():
                    h.wait_ge(sems[dom], ticket[(dom, idx)])
                if o.fn is None:
                    continue
                ins = o.fn(h)
                if o.is_dma:
                    ins.then_inc(sems[o.dom], 16)
                elif o.signal:
                    ins.then_inc(sems[o.dom], 1)

        with nc.Block() as block:
            @block.tensor
            def _(eng):
                run("pe")

            @block.scalar
            def _(eng):
                run("act")

            @block.vector
            def _(eng):
                run("dve")

            @block.gpsimd
            def _(eng):
                run("pool")

            @block.sync
            def _(eng):
                run("sp")


D = 1024
L = 2048
NCX = 256
NT = L + NCX
DFF = 2816
NFF = 22
LAT0 = 1
CTX0 = 2051
NCOL = 2308
EPS = 1e-6
NE_WIN = 23
NE = 39
DEPTH = 4


def pcol(t):
    return t + LAT0 if t < L else (t - L) + CTX0


TB512 = [(0, 512, 0), (512, 512, 0), (1024, 512, 0), (1536, 512, 0), (2048, 256, 1)]
TB256 = [(256 * k, 256, 0) for k in range(8)] + [(2048, 256, 1)]
SEGS = [(0, 406, 0), (406, 406, 0), (812, 406, 0), (1218, 415, 0), (1633, 415, 0), (2048, 256, 1)]


def build(n_layers=DEPTH, dbg=False):
    nc = bass.Bass("TRN2", target_bir_lowering=False)

    def din(name, shape):
        return nc.dram_tensor(name, list(shape), F32, kind="ExternalInput").ap()

    xin = din("xin", [D, NT])
    svec = din("svec", [128, 16])
    ada_w = din("ada_w", [4, D, 6 * D])
    ada_b = din("ada_b_r", [128, 4 * 48])
    ngd = din("ng", [128, 64])
    convd = din("convp", [128, 4 * 4 * NFF])
    wup = din("wup_r", [4, NFF, 128, 8 * 256])
    wdn = din("wdn_r", [4, 8, 128, NFF * 128])
    win = din("win_r", [2, 8, 128, 8 * 128])
    winv = din("winv_r", [2, 128, 8 * 512])
    wsT = din("wsT", [2, 128, 4 * 128])
    bsd = din("bs", [2, 1, 512])
    wout = din("wout_r", [2, 8, 128, 8 * 128])
    wqk = din("wqk_r", [2, 16, 128, 8 * 128])
    wv = din("wv_r", [2, 2, 128, 8 * 512])
    wo = din("wo_r", [2, 8, 128, 8 * 128])
    qkgd = din("qkg", [128, 4])
    tabd = din("tab", [2, 16, 128, NE * 64])
    dftL = din("dftL", [2, L, L])
    dftC = din("dftC", [128, 2 * 256])
    dft256 = din("dft256", [2, 256, 256])
    cstd = din("cst", [128, 3 * 128])
    n_out = NT if dbg else L
    yout = nc.dram_tensor("yout", [D, n_out], F32, kind="ExternalOutput").ap()

    with ExitStack() as ctx:
        def sb(name, shape, dt):
            return ctx.enter_context(nc.sbuf_tensor(name, shape, dt))

        XTt = sb("XT", [128, 8, NT], F32)
        R1t = sb("R1", [128, 8 * NCOL], BF16)
        R2t = sb("R2", [128, 18432], BF16)
        R3t = sb("R3", [128, 18432], BF16)
        R4t = sb("R4", [128, 9216], BF16)
        CONSTt = sb("CONST", [128, 3, 128], BF16)
        NGt = sb("NG", [128, 2, 4, 8], F32)
        CONVt = sb("CONVP", [128, 4, 4, NFF], F32)
        QKGt = sb("QKG", [128, 2, 2], F32)
        GQSt = sb("GQS", [128, 2, 2], F32)
        ADABt = sb("ADAB", [128, 4, 48], F32)
        SVt = sb("SV", [128, 8, 2], F32)
        STt = sb("ST", [128, 8, 2], BF16)
        MODt = [sb("MOD%d" % k, [128, 48, 2], F32) for k in range(2)]
        A12t = [sb("A12_%d" % k, [128, 2, 8, 2], F32) for k in range(2)]
        QZt = sb("QZ", [128, 4, 512], BF16)
        PSt = [ctx.enter_context(nc.psum_tensor("ps%d" % k, [128, 512], F32)) for k in range(8)]

        R1 = Arena(R1t, 8 * NCOL * 2)
        R2 = Arena(R2t, 36864)
        R3 = Arena(R3t, 36864)
        R4 = Arena(R4t, 18432)

        S = Sched(nc)
        PS = [Tile(PSt[k][:], "ps%d" % k) for k in range(8)]
        for t_ in PS:
            t_.excl = True
        psrot = [0]

        ps_default = [(0, 1, 2, 3, 4, 5, 6, 7)]

        def ps_next(pool=None):
            if pool is None:
                pool = ps_default[0]
            k = pool[psrot[0] % len(pool)]
            psrot[0] += 1
            return PS[k]

        XTT = [[Tile(XTt[:, c, u * 128:(u + 1) * 128], "xt%d_%d" % (c, u)) for u in range(18)] for c in range(8)]

        def xt_tiles(c, t0, n):
            return [XTT[c][u] for u in range(t0 // 128, (t0 + n - 1) // 128 + 1)]

        def xt_all(t0, n):
            r = []
            for c in range(8):
                r += xt_tiles(c, t0, n)
            return r

        R1.use_whole = True
        HT = R1.tile("r1", 0, [128, 8, NCOL], BF16, "HT")
        HTU = [Tile(None, "ht%d" % u, (R1.whole,), "r1") for u in range(18)]
        HTPAD = Tile(None, "htpad", (R1.whole,), "r1")

        def ht_tiles(t0, n):
            return [HTU[u] for u in range(t0 // 128, (t0 + n - 1) // 128 + 1)]

        def ht_halo(t0, n, col):
            s0, s1 = (0, L) if col == 0 else (L, NT)
            lo = max(t0 - 1, s0)
            hi = min(t0 + n + 1, s1)
            return ht_tiles(lo, hi - lo) + [HTPAD]

        QZ = [Tile(QZt[:, k, :], "qz%d" % k) for k in range(4)]
        CONST = Tile(CONSTt, "const")
        IDENT = CONSTt[:, 0, :]
        ONES = CONSTt[:, 1, :]
        BLK = CONSTt[:, 2, :]
        NG = Tile(NGt, "ng")
        CONV = Tile(CONVt, "conv")
        QKG = Tile(QKGt, "qkg")
        GQS = Tile(GQSt, "gqs")
        ADAB = Tile(ADABt, "adab")
        SV = Tile(SVt, "sv")
        ST = Tile(STt, "st")
        MOD = [Tile(MODt[k], "mod%d" % k) for k in range(2)]
        A12 = [Tile(A12t[k], "a12_%d" % k) for k in range(2)]

        RECIP = "reciprocal"

        def OP(eng, method, *args, reads=(), writes=(), **kw):
            return S.op(eng, lambda e: getattr(e, method)(*args, **kw), reads, writes)

        def MM(out, lhsT, rhs, start, stop, reads, writes):
            return S.op("pe", lambda e: e.matmul(out, lhsT, rhs, start=start, stop=stop), reads, writes)

        def DMA(queue, out, in_, reads=(), writes=()):
            return S.dma(queue, lambda e: e.dma_start(out=out, in_=in_), reads, writes)

        for c in range(8):
            DMA("sp", XTt[:, c, :], xin[c * 128:(c + 1) * 128, :], writes=XTT[c])
        DMA("pool", CONSTt[:].rearrange("p a b -> p (a b)"), cstd, writes=[CONST])
        DMA("sp", NGt[:].rearrange("p a b c -> p (a b c)"), ngd, writes=[NG])
        DMA("sp", CONVt[:].rearrange("p a b c -> p (a b c)"), convd, writes=[CONV])
        DMA("sp", QKGt[:].rearrange("p a b -> p (a b)"), qkgd, writes=[QKG])
        DMA("sp", ADABt[:].rearrange("p a b -> p (a b)"), ada_b, writes=[ADAB])
        DMA("sp", SVt[:].rearrange("p a b -> p (a b)"), svec, writes=[SV])
        OP("act", "activation", STt[:], SVt[:], AF.Silu, reads=[SV], writes=[ST])
        OP("dve", "tensor_scalar", GQSt[:], QKGt[:], 0.125, None, ALU.mult, reads=[QKG], writes=[GQS])

        def mods(l):
            gen = "L%d.mods" % l
            md = MOD[l % 2]
            mdt = MODt[l % 2]
            a12 = A12[l % 2]
            a12t = A12t[l % 2]
            BW = int(os.environ.get("K_BW", "512"))
            nb = 6144 // BW
            ADA = [R2.tile(gen, k * 8192, [128, 8, BW], BF16, "ada%d" % k) for k in range(2)]
            adv = ada_w[l].rearrange("(c p) n -> p c n", p=128)
            psm = PS[7]
            for blk in range(nb):
                a = ADA[blk % 2]
                DMA("pool", a.ap, adv[:, :, blk * BW:(blk + 1) * BW], writes=[a])
                for jj in range(BW // 128):
                    j = blk * (BW // 128) + jj
                    for kc in range(8):
                        MM(psm.ap[:, 2 * j:2 * j + 2], a.ap[:, kc, jj * 128:(jj + 1) * 128], STt[:, kc, :],
                           kc == 0, kc == 7, [a, ST], [psm])
            mods_final(l)

        def mods_final_part(l, part):
            md = MOD[l % 2]
            mdt = MODt[l % 2]
            a12 = A12[l % 2]
            a12t = A12t[l % 2]
            psm = PS[7]
            j0, j1 = (0, 16) if part == 0 else (16, 48)
            OP("dve", "tensor_tensor", mdt[:, j0:j1, :], psm.ap[:, 2 * j0:2 * j1].rearrange("p (a b) -> p a b", b=2),
               ADABt[:, l, j0:j1].unsqueeze(2).to_broadcast([128, j1 - j0, 2]), ALU.add,
               reads=[psm, ADAB], writes=[md])
            w, jj0 = (0, 8) if part == 0 else (1, 32)
            OP("dve", "tensor_scalar", a12t[:, w], mdt[:, jj0:jj0 + 8, :], 1.0, None, ALU.add,
               reads=[md], writes=[a12])
            OP("dve", "tensor_tensor", a12t[:, w], a12t[:, w],
               NGt[:, w, l, :].unsqueeze(2).to_broadcast([128, 8, 2]), ALU.mult,
               reads=[a12, NG], writes=[a12])

        def mods0():
            gen = "L0.mods"
            ADA = [R2.tile(gen, k * 8192, [128, 8, 512], BF16, "ada%d" % k) for k in range(4)] + \
                  [R4.tile(gen, k * 8192, [128, 8, 512], BF16, "adb%d" % k) for k in range(2)]
            adv = ada_w[0].rearrange("(c p) n -> p c n", p=128)
            psm = PS[7]
            st = [4]

            def dma(blk):
                a = ADA[blk % 6]
                DMA("pool", a.ap, adv[:, :, blk * 512:(blk + 1) * 512], reads=(XTT[7][:1] if blk >= 4 else []),
                    writes=[a])

            def mm(blk):
                a = ADA[blk % 6]
                for jj in range(4):
                    j = blk * 4 + jj
                    for kc in range(8):
                        MM(psm.ap[:, 2 * j:2 * j + 2], a.ap[:, kc, jj * 128:(jj + 1) * 128], STt[:, kc, :],
                           kc == 0, kc == 7, [a, ST], [psm])
                if blk + 6 < 12:
                    dma(blk + 6)

            for blk in range(6):
                dma(blk)
            for blk in range(4):
                mm(blk)
            mods_final_part(0, 0)

            def hook(bi):
                for _ in range(2):
                    if st[0] < 12:
                        mm(st[0])
                        st[0] += 1

            def finish():
                while st[0] < 12:
                    mm(st[0])
                    st[0] += 1
                mods_final_part(0, 1)
            return hook, finish

        def mods_final(l):
            md = MOD[l % 2]
            mdt = MODt[l % 2]
            a12 = A12[l % 2]
            a12t = A12t[l % 2]
            psm = PS[7]
            OP("dve", "tensor_tensor", mdt[:], psm.ap[:, 0:96].rearrange("p (a b) -> p a b", b=2),
               ADABt[:, l, :].unsqueeze(2).to_broadcast([128, 48, 2]), ALU.add,
               reads=[psm, ADAB], writes=[md])
            for w, j0 in ((0, 8), (1, 32)):
                OP("dve", "tensor_scalar", a12t[:, w], mdt[:, j0:j0 + 8, :], 1.0, None, ALU.add,
                   reads=[md], writes=[a12])
                OP("dve", "tensor_tensor", a12t[:, w], a12t[:, w],
                   NGt[:, w, l, :].unsqueeze(2).to_broadcast([128, 8, 2]), ALU.mult,
                   reads=[a12, NG], writes=[a12])

        def modulate(l, which, blocks, zero_pads, external=False, hook=None):
            gen = "L%d.m%d" % (l, which)
            md = MOD[l % 2]
            mdt = MODt[l % 2]
            a12 = A12[l % 2]
            a12t = A12t[l % 2]
            sh0 = 0 if which == 1 else 24
            SQ = [R3.tile(gen, k * 4096, [128, 4, 512], BF16, "sq%d" % k) for k in range(2)]
            RS = [R3.tile(gen, 8192 + k * 2048, [128, 512], F32, "rs%d" % k) for k in range(2)]
            RR = [R3.tile(gen, 12288 + k * 2048, [128, 512], F32, "rr%d" % k) for k in range(3)]
            T1 = [R3.tile(gen, 18432 + k * 2048, [128, 512], F32, "t1_%d" % k) for k in range(4)]
            if zero_pads:
                for c0 in (0, 2049, 2050, 2307):
                    OP("pool", "memset", HT.ap[:, :, c0:c0 + 1], 0.0, writes=[HTPAD])
            cnt = [0]

            def stage_a(bi):
                t0, n, col = blocks[bi]
                sq = SQ[0]
                sq2 = SQ[1]
                rs = RS[bi % 2]
                rr = RR[bi % 3]
                xts = xt_all(t0, n)
                OP("act", "activation", sq.ap[:, 0:4, 0:n], XTt[:, 0:4, t0:t0 + n], AF.Square, reads=xts, writes=[sq])
                OP("dve", "tensor_tensor", sq2.ap[:, :, 0:n], XTt[:, 4:8, t0:t0 + n], XTt[:, 4:8, t0:t0 + n], ALU.mult,
                   reads=xts, writes=[sq2])
                pr = ps_next()
                for c in range(8):
                    src = sq.ap[:, c, 0:n] if c < 4 else sq2.ap[:, c - 4, 0:n]
                    MM(pr.ap[:, 0:n], ONES, src, c == 0, c == 7, [sq, sq2, CONST], [pr])
                OP("act", "activation", rs.ap[:, 0:n], pr.ap[:, 0:n], AF.Ln, bias=EPS, scale=1.0 / D,
                   reads=[pr], writes=[rs])
                OP("act", "activation", rr.ap[:, 0:n], rs.ap[:, 0:n], AF.Exp, scale=-0.5, reads=[rs], writes=[rr])

            def stage_b(bi):
                t0, n, col = blocks[bi]
                rr = RR[bi % 3]
                pc0 = pcol(t0)
                hts = ht_tiles(t0, n)
                for c in range(8):
                    t1 = T1[cnt[0] % 4]
                    cnt[0] += 1
                    OP("dve", "tensor_tensor", t1.ap[:, 0:n], XTt[:, c, t0:t0 + n],
                       rr.ap[:, 0:n], ALU.mult, reads=xt_tiles(c, t0, n) + [rr], writes=[t1])
                    if c < 6:
                        OP("act", "activation", HT.ap[:, c, pc0:pc0 + n], t1.ap[:, 0:n], AF.Identity,
                           bias=mdt[:, sh0 + c, col:col + 1], scale=a12t[:, which - 1, c, col:col + 1],
                           reads=[t1, md, a12], writes=hts)
                    else:
                        OP("dve", "tensor_scalar", HT.ap[:, c, pc0:pc0 + n], t1.ap[:, 0:n],
                           a12t[:, which - 1, c, col:col + 1], mdt[:, sh0 + c, col:col + 1], ALU.mult, ALU.add,
                           reads=[t1, md, a12], writes=hts)

            nb_ = len(blocks)
            if external:
                return stage_a, stage_b
            for bi in range(nb_ + 1):
                if bi < nb_:
                    stage_a(bi)
                if hook:
                    hook(bi)
                if bi >= 1:
                    stage_b(bi - 1)

        def ffn(l, last, next_mods):
            gen = "L%d.ffn" % l
            md = MOD[l % 2]
            mdt = MODt[l % 2]
            segs = SEGS[:5] if last else SEGS
            passes = [segs[0:3], segs[3:]]
            HW = 1218
            NWU, NT1, NWD, NAD = 3, 2, 2, 2
            WU = [R3.tile(gen, k * 4096, [128, 8, 256], BF16, "wu%d" % k) for k in range(NWU)]
            T1 = [R3.tile(gen, 12288 + k * 2048, [128, 512], F32, "ft1_%d" % k) for k in range(NT1)]
            T2 = [R3.tile(gen, 16384 + k * 2048, [128, 512], F32, "ft2_%d" % k) for k in range(NT1)]
            ADAI = [R3.tile(gen, 20480 + k * 2048, [128, 8, 128], BF16, "adai%d" % k) for k in range(NAD)]
            WD = [R3.tile(gen, 24576 + k * 5632, [128, NFF, 128], BF16, "wd%d" % k) for k in range(NWD)]
            H = [(R2.tile(gen, j * 2 * HW, [128, HW], BF16, "h%d" % j) if j < 15 else
                  R4.tile(gen, (j - 15) * 2 * HW, [128, HW], BF16, "h%d" % j)) for j in range(NFF)]
            if next_mods:
                advn = ada_w[l + 1].rearrange("(c p) n -> p c n", p=128)
            psm = PS[7]
            P7 = (0, 1, 2, 3, 4, 5, 6)
            mstate = [0]

            def mods_step():
                if not next_mods:
                    return
                g = mstate[0]
                mstate[0] += 1
                if g < 48:
                    a = ADAI[g % NAD]
                    DMA("pool", a.ap, advn[:, :, g * 128:(g + 1) * 128], writes=[a])
                jm = g - (NAD - 1)
                if 0 <= jm < 48:
                    a = ADAI[jm % NAD]
                    for kc in range(8):
                        MM(psm.ap[:, 2 * jm:2 * jm + 2], a.ap[:, kc, :], STt[:, kc, :], kc == 0, kc == 7,
                           [a, ST], [psm])

            ucnt = 0
            for pi, pss in enumerate(passes):
                NPRE = NWU - 1
                for j in range(NFF + NPRE):
                    jj = j
                    if jj < NFF:
                        wt = WU[jj % NWU]
                        DMA("pool", wt.ap.rearrange("p a b -> p (a b)"), wup[l, jj], writes=[wt])
                    j = j - NPRE
                    if j < 0:
                        continue
                    wt = WU[j % NWU]
                    so = 0
                    mods_step()
                    for (t0, n, col) in pss:
                        pc0 = pcol(t0)
                        pg = ps_next(P7)
                        pv = ps_next(P7)
                        hts = ht_halo(t0, n, col)
                        for kc in range(8):
                            MM(pg.ap[:, 0:n + 2], wt.ap[:, kc, 0:128], HT.ap[:, kc, pc0 - 1:pc0 + n + 1],
                               kc == 0, kc == 7, [wt] + hts, [pg])
                        for kc in range(8):
                            MM(pv.ap[:, 0:n], wt.ap[:, kc, 128:256], HT.ap[:, kc, pc0:pc0 + n],
                               kc == 0, kc == 7, [wt] + hts, [pv])
                        t1 = T1[ucnt % NT1]
                        t2 = T2[ucnt % NT1]
                        ucnt += 1
                        OP("act", "activation", t1.ap[:, 0:n], pg.ap[:, 1:n + 1], AF.Identity,
                           bias=CONVt[:, l, 3, j:j + 1], scale=CONVt[:, l, 1, j:j + 1],
                           reads=[pg, CONV], writes=[t1])
                        OP("dve", "scalar_tensor_tensor", t1.ap[:, 0:n], pg.ap[:, 0:n], CONVt[:, l, 0, j:j + 1],
                           t1.ap[:, 0:n], ALU.mult, ALU.add, reads=[pg, CONV, t1], writes=[t1])
                        OP("dve", "scalar_tensor_tensor", t1.ap[:, 0:n], pg.ap[:, 2:n + 2], CONVt[:, l, 2, j:j + 1],
                           t1.ap[:, 0:n], ALU.mult, ALU.add, reads=[pg, CONV, t1], writes=[t1])
                        OP("act", "activation", t2.ap[:, 0:n], t1.ap[:, 0:n], AF.Silu, reads=[t1], writes=[t2])
                        OP("dve", "tensor_tensor", H[j].ap[:, so:so + n], t2.ap[:, 0:n], pv.ap[:, 0:n], ALU.mult,
                           reads=[t2, pv], writes=[H[j]])
                        so += n
                MPRE = NWD - 1
                for m in range(8 + MPRE):
                    mm_ = m
                    if mm_ < 8:
                        wt = WD[mm_ % NWD]
                        DMA("pool", wt.ap.rearrange("p a b -> p (a b)"), wdn[l, mm_], writes=[wt])
                    m = m - MPRE
                    if m < 0:
                        continue
                    wt = WD[m % NWD]
                    so = 0
                    mods_step()
                    for (t0, n, col) in pss:
                        po = ps_next(P7)
                        for kc in range(NFF):
                            MM(po.ap[:, 0:n], wt.ap[:, kc, :], H[kc].ap[:, so:so + n], kc == 0, kc == NFF - 1,
                               [wt, H[kc]], [po])
                        xts = xt_tiles(m, t0, n)
                        OP("dve", "scalar_tensor_tensor", XTt[:, m, t0:t0 + n], po.ap[:, 0:n],
                           mdt[:, 40 + m, col:col + 1], XTt[:, m, t0:t0 + n], ALU.mult, ALU.add,
                           reads=[po, md] + xts, writes=xts)
                        so += n
            if next_mods:
                while mstate[0] < 48 + NAD:
                    mods_step()
                mods_final(l + 1)

        def mk_w(gen, woff=0):
            return [R4.tile(gen, woff + k * 2048, [128, 8, 128], BF16, "w%d" % k) for k in range(3)]

        def proj_stream(W, wsrc_fn, nj, blocks, rhs_fn, rhs_tiles_fn, evac_fn, flush_fn=None, post_fn=None,
                        blocks_fn=None):
            NW = len(W)
            for j in range(nj + NW - 1):
                if j < nj:
                    wt = W[j % NW]
                    DMA("pool", wt.ap.rearrange("p a b -> p (a b)"), wsrc_fn(j), writes=[wt])
                jj = j - (NW - 1)
                if jj < 0:
                    continue
                wt = W[jj % NW]
                for (t0, n, col) in (blocks_fn(jj) if blocks_fn else blocks):
                    p = ps_next()
                    for kc in range(8):
                        MM(p.ap[:, 0:n], wt.ap[:, kc, :], rhs_fn(kc, t0, n), kc == 0, kc == 7,
                           [wt] + rhs_tiles_fn(kc, t0, n), [p])
                    evac_fn(jj, p, t0, n, col)
                if post_fn:
                    post_fn(jj)
            if flush_fn:
                flush_fn()

        def out_proj_overlap(l, gen, wsrc_fn, blocks, rhs_fn, rhs_tiles_fn, last):
            WR = [R4.tile(gen, k * 2048, [128, 8, 128], BF16, "wr%d" % k) for k in range(8)]
            for m in range(8):
                DMA("pool", WR[m].ap.rearrange("p a b -> p (a b)"), wsrc_fn(m), writes=[WR[m]])
            ev = resid_evac(l, 16)
            mblocks = TB512[:4] if last else TB512
            assert list(mblocks) == list(blocks)
            sa, sb_ = modulate(l, 2, mblocks, True, external=True)
            nb_ = len(blocks)
            for b in range(nb_ + 2):
                if b < nb_:
                    t0, n, col = blocks[b]
                    for m in range(8):
                        p = ps_next()
                        for kc in range(8):
                            MM(p.ap[:, 0:n], WR[m].ap[:, kc, :], rhs_fn(kc, t0, n), kc == 0, kc == 7,
                               [WR[m]] + rhs_tiles_fn(kc, t0, n), [p])
                        ev(m, p, t0, n, col)
                if 1 <= b <= nb_:
                    sa(b - 1)
                if 2 <= b:
                    sb_(b - 2)
            sb_(nb_ - 1)

        def resid_evac(l, which_g):
            md = MOD[l % 2]
            mdt = MODt[l % 2]

            def f(m, p, t0, n, col):
                xts = xt_tiles(m, t0, n)
                OP("dve", "scalar_tensor_tensor", XTt[:, m, t0:t0 + n], p.ap[:, 0:n],
                   mdt[:, which_g + m, col:col + 1], XTt[:, m, t0:t0 + n], ALU.mult, ALU.add,
                   reads=[p, md] + xts, writes=xts)
            return f

        def even_mixer(l, last):
            i = l // 2
            gen = "L%d.even" % l
            blocks = TB512[:4] if last else TB512
            nun = 16 if last else 18
            FT = R2.tile(gen, 0, [128, 4, NT], BF16, "FT")
            UT = R2.tile(gen, 18432, [128, 4, NT], BF16, "UT")
            FTU = [[Tile(None, "ft%d_%d" % (g, u), FT.grans, gen) for u in range(18)] for g in range(4)]
            UTU = [[Tile(None, "ut%d_%d" % (g, u), UT.grans, gen) for u in range(18)] for g in range(4)]

            def fu(TU, g, t0, n):
                return [TU[g][u] for u in range(t0 // 128, (t0 + n - 1) // 128 + 1)]

            def evac_a(j, p, t0, n, col):
                if j < 4:
                    OP("dve", "tensor_copy", FT.ap[:, j, t0:t0 + n], p.ap[:, 0:n], reads=[p], writes=fu(FTU, j, t0, n))
                else:
                    OP("act", "activation", UT.ap[:, j - 4, t0:t0 + n], p.ap[:, 0:n], AF.Gelu, reads=[p],
                       writes=fu(UTU, j - 4, t0, n))
            proj_stream(mk_w(gen), lambda j: win[i, j], 8, blocks,
                        lambda kc, t0, n: HT.ap[:, kc, pcol(t0):pcol(t0) + n],
                        lambda kc, t0, n: ht_tiles(t0, n), evac_a)

            WV = R4.tile(gen, 6144, [128, 8, 512], BF16, "wvin")
            WS = R4.tile(gen, 14336, [128, 4, 128], BF16, "wsT")
            BS = R4.tile(gen, 15360, [1, 512], BF16, "bs")
            DMA("pool", WV.ap.rearrange("p a b -> p (a b)"), winv[i], writes=[WV])
            DMA("pool", WS.ap.rearrange("p a b -> p (a b)"), wsT[i], writes=[WS])
            DMA("pool", BS.ap, bsd[i], writes=[BS])
            VN = [R3.tile(gen, 18432 + k * 4096, [128, 4, 4, 128], BF16, "vn%d" % k) for k in range(2)]
            VF = [R3.tile(gen, 26624 + k * 2048, [128, 512], F32, "vf%d" % k) for k in range(2)]
            SQV = [R3.tile(gen, 30720 + k * 2048, [128, 512], F32, "sqv%d" % k) for k in range(2)]
            SSV = [R3.tile(gen, 34816 + k * 64, [128, 4], F32, "ssv%d" % k) for k in range(2)]
            RSV = [R3.tile(gen, 34944 + k * 64, [128, 4], F32, "rsv%d" % k) for k in range(2)]
            RRV = [R3.tile(gen, 35072 + k * 64, [128, 4], F32, "rrv%d" % k) for k in range(2)]
            NEGH = R3.tile(gen, 35200, [128, 4], F32, "negh")
            OP("pool", "memset", NEGH.ap, -0.5, writes=[NEGH])
            groups = [(512 * g, 4) for g in range(4)] + ([] if last else [(2048, 2)])
            cnt = 0
            sgu_pend = []
            for gi, (t0g, ntt) in enumerate(groups):
                vn = VN[gi % 2]
                for tt in range(ntt):
                    t0 = t0g + tt * 128
                    pc0 = pcol(t0)
                    p = ps_next()
                    for kc in range(8):
                        MM(p.ap[:, :], HT.ap[:, kc, pc0:pc0 + 128], WV.ap[:, kc, :], kc == 0, kc == 7,
                           [WV] + ht_tiles(t0, 128), [p])
                    vf = VF[cnt % 2]
                    sqv = SQV[cnt % 2]
                    ssv = SSV[cnt % 2]
                    rsv = RSV[cnt % 2]
                    rrv = RRV[cnt % 2]
                    cnt += 1
                    OP("act", "activation", vf.ap, p.ap, AF.Gelu, reads=[p], writes=[vf])
                    OP("pool", "tensor_tensor", sqv.ap, vf.ap, vf.ap, ALU.mult, reads=[vf], writes=[sqv])
                    OP("dve", "tensor_reduce", ssv.ap, sqv.ap.rearrange("p (a b) -> p a b", a=4), AX.X, ALU.add,
                       reads=[sqv], writes=[ssv])
                    OP("dve", "tensor_scalar", rsv.ap, ssv.ap, 1.0 / 128, EPS, ALU.mult, ALU.add,
                       reads=[ssv], writes=[rsv])
                    OP("pool", "tensor_tensor", rrv.ap, rsv.ap, NEGH.ap, ALU.pow, reads=[rsv, NEGH], writes=[rrv])
                    OP("dve", "tensor_tensor", vn.ap[:, tt], vf.ap.rearrange("p (a b) -> p a b", a=4),
                       rrv.ap.unsqueeze(2).to_broadcast([128, 4, 128]), ALU.mult, reads=[vf, rrv], writes=[vn])
                def sgu(vn=vn, t0g=t0g, ntt=ntt):
                    for g in range(4):
                        p = ps_next()
                        for tt in range(ntt):
                            MM(p.ap[:, tt * 128:(tt + 1) * 128], vn.ap[:, tt, g, :], WS.ap[:, g, :], True, False,
                               [vn, WS], [p])
                            MM(p.ap[:, tt * 128:(tt + 1) * 128], ONES[0:1, :], BS.ap[0:1, g * 128:(g + 1) * 128],
                               False, True, [BS, CONST], [p])
                        n = ntt * 128
                        uts = fu(UTU, g, t0g, n)
                        OP("dve", "tensor_tensor", UT.ap[:, g, t0g:t0g + n], UT.ap[:, g, t0g:t0g + n], p.ap[:, 0:n],
                           ALU.mult, reads=[p] + uts, writes=uts)
                if sgu_pend:
                    sgu_pend.pop(0)()
                sgu_pend.append(sgu)
            while sgu_pend:
                sgu_pend.pop(0)()

            gen2 = "L%d.four" % l
            CS = R4.tile(gen2, 6144, [128, 2, 256], BF16, "cs")
            DMA("pool", CS.ap.rearrange("p a b -> p (a b)"), dftC, writes=[CS])
            PQ = [R3.tile(gen2, t * 2048, [128, 4, 256], BF16, "pq%d" % t) for t in range(18)]
            for t in range(nun):
                v = 0 if t < 16 else 1
                for gp in range(2):
                    p = ps_next()
                    for g2 in range(2):
                        g = 2 * gp + g2
                        MM(p.ap[:, g2 * 256:(g2 + 1) * 256], FT.ap[:, g, t * 128:(t + 1) * 128], CS.ap[:, v, :],
                           True, True, [CS] + fu(FTU, g, t * 128, 128), [p])
                    if (t + gp) % 2 == 0:
                        OP("act", "activation", PQ[t].ap[:, 2 * gp:2 * gp + 2, :].rearrange("p a b -> p (a b)"),
                           p.ap, AF.Copy, reads=[p], writes=[PQ[t]])
                    else:
                        OP("dve", "tensor_copy", PQ[t].ap[:, 2 * gp:2 * gp + 2, :].rearrange("p a b -> p (a b)"),
                           p.ap, reads=[p], writes=[PQ[t]])
            DT = [[R1.tile(gen2, (b * 2 + mtx) * 8192, [128, 16, 256], BF16, "dt%d_%d" % (b, mtx)) for mtx in range(2)]
                  for b in range(2)]
            dlv = [dftL[mtx].rearrange("(c p) n -> p c n", p=128) for mtx in range(2)]
            ecnt = 0
            for kt in range(8):
                d = DT[kt % 2]
                for mtx in range(2):
                    DMA("pool", d[mtx].ap, dlv[mtx][:, :, kt * 256:(kt + 1) * 256], writes=[d[mtx]])
                for g in range(4):
                    p = ps_next()
                    for lc in range(16):
                        MM(p.ap[:, 0:256], PQ[lc].ap[:, g, 0:128], d[0].ap[:, lc, :], lc == 0, False,
                           [PQ[lc], d[0]], [p])
                        MM(p.ap[:, 0:256], PQ[lc].ap[:, g, 128:256], d[1].ap[:, lc, :], False, lc == 15,
                           [PQ[lc], d[1]], [p])
                    fts = fu(FTU, g, kt * 256, 256)
                    if ecnt % 2 == 0:
                        OP("act", "activation", FT.ap[:, g, kt * 256:(kt + 1) * 256], p.ap[:, 0:256], AF.Copy,
                           reads=[p], writes=fts)
                    else:
                        OP("dve", "tensor_copy", FT.ap[:, g, kt * 256:(kt + 1) * 256], p.ap[:, 0:256],
                           reads=[p], writes=fts)
                    ecnt += 1
            if not last:
                D2 = R4.tile(gen2, 8192, [128, 2, 2, 256], BF16, "d256")
                for mtx in range(2):
                    DMA("pool", D2.ap[:, mtx], dft256[mtx].rearrange("(c p) n -> p c n", p=128), writes=[D2])
                for g in range(4):
                    p = ps_next()
                    for lc in range(2):
                        MM(p.ap[:, 0:256], PQ[16 + lc].ap[:, g, 0:128], D2.ap[:, 0, lc, :], lc == 0, False,
                           [PQ[16 + lc], D2], [p])
                        MM(p.ap[:, 0:256], PQ[16 + lc].ap[:, g, 128:256], D2.ap[:, 1, lc, :], False, lc == 1,
                           [PQ[16 + lc], D2], [p])
                    OP("act", "activation", FT.ap[:, g, 2048:2304], p.ap[:, 0:256], AF.Copy, reads=[p],
                       writes=fu(FTU, g, 2048, 256))

            def rhs_d(kc, t0, n):
                return FT.ap[:, kc, t0:t0 + n] if kc < 4 else UT.ap[:, kc - 4, t0:t0 + n]

            def rhs_t(kc, t0, n):
                return fu(FTU, kc, t0, n) if kc < 4 else fu(UTU, kc - 4, t0, n)
            out_proj_overlap(l, "L%d.eout" % l, lambda m: wout[i, m], blocks, rhs_d, rhs_t, last)

        def odd_mixer(l, last):
            i = l // 2
            gen = "L%d.odd" % l
            KT = R2.tile(gen, 0, [128, 8, NT], BF16, "KT")
            KTU = [[Tile(None, "kt%d_%d" % (c, u), KT.grans, gen) for u in range(18)] for c in range(8)]
            VT = [R3.tile(gen, t * 2048, [128, 1024], BF16, "vt%d" % t) for t in range(18)]

            def ktu(c, t0, n):
                return [KTU[c][u] for u in range(t0 // 128, (t0 + n - 1) // 128 + 1)]

            def norm_evac(W0, gsrc, dst_fn, bs, gen, nset, depth=2, sq_tiles=None):
                if sq_tiles is None:
                    SQ = [R4.tile(gen, W0 + k * bs * 2, [128, bs], BF16, "qsq%d" % k) for k in range(nset)]
                    o2 = W0 + nset * bs * 2
                else:
                    SQ = sq_tiles
                    o2 = W0
                RS = [R4.tile(gen, o2 + k * bs * 4, [128, bs], F32, "qrs%d" % k) for k in range(nset)]
                assert o2 + nset * bs * 4 <= 18432
                cnt = [0]
                pend = []

                def flush(keep=0):
                    while len(pend) > keep:
                        pend.pop(0)()

                def f(j, p, t0, n, col):
                    k = cnt[0] % nset
                    cnt[0] += 1
                    sq, rs = SQ[k], RS[k]
                    OP("act", "activation", sq.ap[:, 0:n], p.ap[:, 0:n], AF.Square, reads=[p], writes=[sq])

                    def part2():
                        p2 = ps_next()
                        MM(p2.ap[:, 0:n], BLK, sq.ap[:, 0:n], True, True, [sq, CONST], [p2])
                        OP("act", "activation", rs.ap[:, 0:n], p2.ap[:, 0:n], AF.Ln, bias=EPS, scale=1.0 / 64,
                           reads=[p2], writes=[rs])
                        OP("act", "activation", rs.ap[:, 0:n], rs.ap[:, 0:n], AF.Exp, scale=-0.5, reads=[rs], writes=[rs])
                        dst, dtiles = dst_fn(j, t0, n)
                        OP("dve", "scalar_tensor_tensor", dst, p.ap[:, 0:n], gsrc, rs.ap[:, 0:n], ALU.mult, ALU.mult,
                           reads=[p, rs, GQS, QKG], writes=dtiles)
                    pend.append(part2)
                    flush(depth)
                return f, flush

            kev, kflush = norm_evac(6144, QKGt[:, i, 1:2],
                                    lambda j, t0, n: (KT.ap[:, j, t0:t0 + n], ktu(j, t0, n)), 512, gen, 3)
            proj_stream(mk_w(gen), lambda j: wqk[i, 8 + j], 8, TB512,
                        lambda kc, t0, n: HT.ap[:, kc, pcol(t0):pcol(t0) + n],
                        lambda kc, t0, n: ht_tiles(t0, n), kev, flush_fn=kflush)

            genv = "L%d.oddv" % l
            WVt = [R4.tile(genv, k * 8192, [128, 8, 512], BF16, "wvo%d" % k) for k in range(2)]
            ec = 0
            for nt_ in range(2):
                DMA("pool", WVt[nt_].ap.rearrange("p a b -> p (a b)"), wv[i, nt_], writes=[WVt[nt_]])
            for nt_ in range(2):
                for t in range(18):
                    p = ps_next()
                    pc0 = pcol(t * 128)
                    for kc in range(8):
                        MM(p.ap[:, :], HT.ap[:, kc, pc0:pc0 + 128], WVt[nt_].ap[:, kc, :], kc == 0, kc == 7,
                           [WVt[nt_]] + ht_tiles(t * 128, 128), [p])
                    if ec % 2 == 0:
                        OP("act", "activation", VT[t].ap[:, nt_ * 512:(nt_ + 1) * 512], p.ap, AF.Copy,
                           reads=[p], writes=[VT[t]])
                    else:
                        OP("dve", "tensor_copy", VT[t].ap[:, nt_ * 512:(nt_ + 1) * 512], p.ap,
                           reads=[p], writes=[VT[t]])
                    ec += 1

            genq = "L%d.oddq" % l
            qpass = TB512[:4] if last else TB512
            QTMP = R4.tile(genq, 6144, [128, 8, 512], BF16, "qtmp")
            WQ = mk_w(genq)
            sqz = []
            for kz in range(3):
                tz = Tile(QZt[:, kz, 0:256], "qzsq%d" % kz)
                tz.buf = QZ[kz].buf
                sqz.append(tz)
            qev, qflush = norm_evac(14336, GQSt[:, i, 0:1],
                                    lambda j, t0_, n_: (QTMP.ap[:, j % 8, (t0_ % 512):(t0_ % 512) + n_], [QTMP]),
                                    256, genq, 3, sq_tiles=sqz)

            def qpost(jj):
                if jj % 8 == 7:
                    qflush()
                    t0, n, col = qpass[jj // 8]
                    pc0 = pcol(t0)
                    OP("dve", "tensor_copy", HT.ap[:, :, pc0:pc0 + n], QTMP.ap[:, :, 0:n], reads=[QTMP],
                       writes=ht_tiles(t0, n))

            def qblocks_fn(jj):
                t0, n, col = qpass[jj // 8]
                return [(t0 + 256 * s_, 256, col) for s_ in range(n // 256)]
            proj_stream(WQ, lambda j: wqk[i, j % 8], 8 * len(qpass), None,
                        lambda kc, t0_, n_: HT.ap[:, kc, pcol(t0_):pcol(t0_) + n_],
                        lambda kc, t0_, n_: ht_tiles(t0_, n_), qev, post_fn=qpost, blocks_fn=qblocks_fn)

            gena = "L%d.attn" % l
            TAB = [R4.tile(gena, k * 4992, [128, NE, 64], BF16, "tab%d" % k) for k in range(2)]
            PT = [R4.tile(gena, 9984 + k * 1024, [128, 512], BF16, "pt%d" % k) for k in range(4)]
            RD = [R4.tile(gena, 14080 + k * 2048, [128, 512], F32, "rd%d" % k) for k in range(2)]
            qbs = []
            qbs.append((0, 256, [(kr0, NE_WIN + (7 - kr0)) for kr0 in (0, 2, 4, 6)]))
            for k in range(3):
                qr0 = 4 + 8 * k
                qbs.append((64 * qr0, 512, [(kr0, qr0 - kr0 + 7 + 4) for kr0 in range(qr0 - 4, qr0 + 12, 2)]))
            qbs.append((1792, 256, [(kr0, NE_WIN + (35 - kr0)) for kr0 in (24, 26, 28, 30)]))
            if not last:
                qbs.append((2048, 256, []))
            SK = int(os.environ.get('K_SK', '2'))
            items = []
            ocnt = 0
            for h in range(16):
                for (qt0, nq, loc) in qbs:
                    chunks = [(kr0 // 2, ei0) for (kr0, ei0) in loc] + [(16, None), (17, None)]
                    for ci, (t, ei0) in enumerate(chunks):
                        items.append((h, qt0, nq, t, ei0, ci == 0, ci == len(chunks) - 1, ocnt))
                    ocnt += 1
            staged = []
            tab_loaded = -1
            for kz in range(4):
                OP("pool", "memset", QZt[:, kz, :], 0.0, writes=[QZ[kz]])
            qz_cur = {}
            qz_cnt = [0, 0]
            for k in range(len(items) + SK):
                if k < len(items):
                    h, qt0, nq, t, ei0, first, lastc, oc = items[k]
                    i2 = h // 2
                    rows = slice(64 * (h % 2), 64 * (h % 2) + 64)
                    tb = TAB[h % 2]
                    if tab_loaded < h:
                        DMA("pool", tb.ap.rearrange("p a b -> p (a b)"), tabd[i, h], writes=[tb])
                        tab_loaded = h
                    pq0 = pcol(qt0)
                    qtiles = ht_tiles(qt0, nq)
                    ps_ = PS[k % 3]
                    pt = PT[k % 4]
                    use_qz = os.environ.get('K_NOQZ') != '1'
                    if not use_qz:
                        MM(ps_.ap[:, 0:nq], KT.ap[rows, i2, t * 128:(t + 1) * 128], HT.ap[rows, i2, pq0:pq0 + nq],
                           True, ei0 is None, ktu(i2, t * 128, 128) + qtiles, [ps_])
                    if first and use_qz:
                        par = h % 2
                        qz = QZ[2 * par + qz_cnt[par] % 2]
                        qz_cnt[par] += 1
                        OP("pool", "tensor_copy", qz.ap[rows, 0:nq], HT.ap[rows, i2, pq0:pq0 + nq], reads=qtiles,
                           writes=[qz])
                        qz_cur[oc] = qz
                    if use_qz:
                        qz = qz_cur[oc]
                        MM(ps_.ap[:, 0:nq], KT.ap[:, i2, t * 128:(t + 1) * 128], qz.ap[:, 0:nq],
                           True, ei0 is None, ktu(i2, t * 128, 128) + [qz], [ps_])
                    if ei0 is not None:
                        MM(ps_.ap[:, 0:nq], IDENT, tb.ap[:, ei0:ei0 + nq // 64, :].rearrange("p a b -> p (a b)"),
                           False, True, [tb, CONST], [ps_])
                    OP("act", "activation", pt.ap[:, 0:nq], ps_.ap[:, 0:nq], AF.Exp, reads=[ps_], writes=[pt])
                    staged.append(pt)
                kk = k - SK
                if kk >= 0:
                    h, qt0, nq, t, ei0, first, lastc, oc = items[kk]
                    i2 = h // 2
                    rows = slice(64 * (h % 2), 64 * (h % 2) + 64)
                    pt_ = staged[kk]
                    po = PS[3 + 2 * (oc % 2)]
                    pd = PS[4 + 2 * (oc % 2)]
                    MM(po.ap[:, 0:nq], VT[t].ap[:, i2 * 128:(i2 + 1) * 128], pt_.ap[:, 0:nq], first, lastc,
                       [VT[t], pt_], [po])
                    MM(pd.ap[:, 0:nq], ONES, pt_.ap[:, 0:nq], first, lastc, [pt_, CONST], [pd])
                    if lastc:
                        pq0 = pcol(qt0)
                        rd = RD[oc % 2]
                        OP("dve", RECIP, rd.ap[rows, 0:nq], pd.ap[rows, 0:nq], reads=[pd], writes=[rd])
                        OP("dve", "tensor_tensor", HT.ap[rows, i2, pq0:pq0 + nq], po.ap[rows, 0:nq],
                           rd.ap[rows, 0:nq], ALU.mult, reads=[po, rd], writes=ht_tiles(qt0, nq))

            blocks = TB512[:4] if last else TB512
            genw = "L%d.wo" % l
            out_proj_overlap(l, genw, lambda m: wo[i, m], blocks,
                             lambda kc, t0, n: HT.ap[:, kc, pcol(t0):pcol(t0) + n],
                             lambda kc, t0, n: ht_tiles(t0, n), last)

        for l in range(n_layers):
            last = (l == DEPTH - 1)
            if l == 0 and os.environ.get('K_MODS0') != 'old':
                hook0, fin0 = mods0()
                ps_default[0] = (0, 1, 2, 3, 4, 5, 6)
                modulate(l, 1, TB512, False, hook=hook0)
                fin0()
                ps_default[0] = (0, 1, 2, 3, 4, 5, 6, 7)
            else:
                if l == 0:
                    mods(l)
                modulate(l, 1, TB512 if (not last or l % 2 == 1) else TB512[:4], False)
            if l % 2 == 0:
                even_mixer(l, last)
            else:
                odd_mixer(l, last)
            nm = (l + 1 < n_layers) and os.environ.get('K_NOMODS') != '1'
            ffn(l, last, nm)
            if (l + 1 < n_layers) and (not nm or os.environ.get('K_BOTH') == '1'):
                mods(l + 1)

        outs = []
        for c in range(8):
            outs.append(DMA("sp", yout[c * 128:(c + 1) * 128, :], XTt[:, c, 0:n_out], reads=XTT[c]))
        S.wait_ops("sp", outs)
        S.emit(ctx)
    return nc


def _consts():
    ident = np.eye(128, dtype=np.float32)
    ones = np.ones((128, 128), np.float32)
    blk = np.zeros((128, 128), np.float32)
    blk[:64, :64] = 1.0
    blk[64:, 64:] = 1.0
    cst = np.stack([ident, ones, blk], axis=1).reshape(128, 384)
    ll = np.arange(L, dtype=np.float64)
    ang = 2.0 * np.pi * ((ll[:, None] * ll[None, :]) % L) / L
    dftL = np.stack([np.cos(ang), -np.sin(ang)]).astype(np.float32)
    l2 = np.arange(256, dtype=np.float64)
    ang2 = 2.0 * np.pi * ((l2[:, None] * l2[None, :]) % 256) / 256
    dft256 = np.stack([np.cos(ang2), -np.sin(ang2)]).astype(np.float32)
    cc = np.arange(128, dtype=np.float64)
    angc = 2.0 * np.pi * ((cc[:, None] * cc[None, :]) % 128) / 128
    dftC = np.zeros((128, 2, 256), np.float64)
    for v, n in enumerate((L, 256)):
        s = 1.0 / np.sqrt(n * 128.0)
        dftC[:, v, :128] = np.cos(angc) * s
        dftC[:, v, 128:] = np.sin(angc) * s
    return cst, dftL, dft256, dftC.reshape(128, 512).astype(np.float32)


def _rpb_tables(rpb):
    NEG = np.float32(-1e30)
    kc = np.arange(64)[:, None]
    qc = np.arange(64)[None, :]
    dc = np.clip(kc - qc + 15, 0, 30)
    cs = np.clip(qc - 8, 0, 48)
    valid = (kc >= cs) & (kc < cs + 16)
    out = np.full((2, 16, 128, NE, 64), NEG, np.float32)
    for e in range(15):
        B = np.where(valid[None, None], rpb[:, :, 14 - e][:, :, dc], NEG)
        for krl in range(2):
            ee = e + krl
            if 4 <= e <= 11 and 0 <= ee + 4 < NE_WIN:
                out[:, :, krl * 64:(krl + 1) * 64, ee + 4, :] = B
            if 0 <= ee < 16:
                out[:, :, krl * 64:(krl + 1) * 64, NE_WIN + ee, :] = B
    return out.reshape(2, 16, 128, NE * 64)


def _prep(inputs):
    f = lambda a: np.ascontiguousarray(np.asarray(a, dtype=np.float32))
    x = f(inputs["x"]); c = f(inputs["c"]); cx = f(inputs["ctx"]); c_ctx = f(inputs["c_ctx"])
    cst, dftL, dft256, dftC = _consts()
    sh = {}
    sh["ada_w"] = f(inputs["ada_w"])
    sh["ada_b_r"] = f(f(inputs["ada_b"]).reshape(4, 48, 128).transpose(2, 0, 1).reshape(128, 192))
    ng = np.stack([f(inputs["norm1_g"]), f(inputs["norm2_g"])])
    sh["ng"] = f(ng.reshape(2, 4, 8, 128).transpose(3, 0, 1, 2).reshape(128, 64))
    cw = f(inputs["ffn_conv_w"]); cb = f(inputs["ffn_conv_b"])
    cp = np.concatenate([cw, cb[:, None, :]], axis=1)
    sh["convp"] = f(cp.reshape(4, 4, NFF, 128).transpose(3, 0, 1, 2).reshape(128, 4 * 4 * NFF))
    wu = f(inputs["ffn_w_up"]).reshape(4, 8, 128, 2, NFF, 128)
    sh["wup_r"] = f(wu.transpose(0, 4, 2, 1, 3, 5).reshape(4, NFF, 128, 8 * 256))
    wd = f(inputs["ffn_w_down"]).reshape(4, NFF, 128, 8, 128)
    sh["wdn_r"] = f(wd.transpose(0, 3, 2, 1, 4).reshape(4, 8, 128, NFF * 128))
    wi = f(inputs["even_w_in"])
    wfu = wi[:, :, :1024].reshape(2, 8, 128, 8, 128)
    sh["win_r"] = f(wfu.transpose(0, 3, 2, 1, 4).reshape(2, 8, 128, 1024))
    wvv = wi[:, :, 1024:].reshape(2, 8, 128, 512)
    sh["winv_r"] = f(wvv.transpose(0, 2, 1, 3).reshape(2, 128, 4096))
    sh["wsT"] = f(f(inputs["even_w_s"]).transpose(0, 3, 1, 2).reshape(2, 128, 512))
    sh["bs"] = f(f(inputs["even_b_s"]).reshape(2, 1, 512))
    wo_ = f(inputs["even_w_out"]).reshape(2, 8, 128, 8, 128)
    sh["wout_r"] = f(wo_.transpose(0, 3, 2, 1, 4).reshape(2, 8, 128, 1024))
    wq = f(inputs["odd_w_qkv"])
    wqk_ = wq[:, :, :2048].reshape(2, 8, 128, 16, 128)
    sh["wqk_r"] = f(wqk_.transpose(0, 3, 2, 1, 4).reshape(2, 16, 128, 1024))
    wv_ = wq[:, :, 2048:].reshape(2, 8, 128, 2, 512)
    sh["wv_r"] = f(wv_.transpose(0, 3, 2, 1, 4).reshape(2, 2, 128, 4096))
    woo = f(inputs["odd_w_o"]).reshape(2, 8, 128, 8, 128)
    sh["wo_r"] = f(woo.transpose(0, 3, 2, 1, 4).reshape(2, 8, 128, 1024))
    qg = f(inputs["odd_q_g"]); kg = f(inputs["odd_k_g"])
    qk = np.stack([qg, kg], axis=1)
    qk = np.concatenate([qk, qk], axis=2)
    sh["qkg"] = f(qk.transpose(2, 0, 1).reshape(128, 4))
    sh["tab"] = _rpb_tables(f(inputs["odd_rpb"]))
    sh["dftL"] = dftL
    sh["dftC"] = dftC
    sh["dft256"] = dft256
    sh["cst"] = cst
    in_maps = []
    for b in range(8):
        m = dict(sh)
        m["xin"] = f(np.concatenate([x[b].T, cx[b].T], axis=1))
        sv = np.stack([c[b], c_ctx], axis=1)
        m["svec"] = f(sv.reshape(8, 128, 2).transpose(1, 0, 2).reshape(128, 16))
        in_maps.append(m)
    return in_maps


_NC_CACHE = {}


def kernel(**inputs):
    in_maps = _prep(inputs)
    if "nc" not in _NC_CACHE:
        _NC_CACHE["nc"] = build()
    nc = _NC_CACHE["nc"]
    res = run_bass_kernel_spmd(nc, in_maps, core_ids=list(range(8)))
    out = np.stack([np.ascontiguousarray(r["yout"].T) for r in res.results], axis=0)
    return out.astype(np.float32)
```

```python
from contextlib import ExitStack
import os
import numpy as np
import concourse.bass as bass
import concourse.mybir as mybir
from concourse.bass_utils import run_bass_kernel_spmd

F32 = mybir.dt.float32
BF16 = mybir.dt.bfloat16
ALU = mybir.AluOpType
AF = mybir.ActivationFunctionType
AX = mybir.AxisListType

ENGS = ("pe", "act", "dve", "pool", "sp")
N_DMA_SEMS = 40
GRAN = 2048


class Buf:
    __slots__ = ("name", "writes", "reads")

    def __init__(self, name=""):
        self.name = name
        self.writes = []
        self.reads = []


class Region:
    __slots__ = ("gen", "cur", "prev")

    def __init__(self):
        self.gen = None
        self.cur = {}
        self.prev = {}


class Tile:
    __slots__ = ("ap", "buf", "grans", "gen", "excl")

    def __init__(self, ap, name="", grans=(), gen=None):
        self.excl = False
        self.ap = ap
        self.buf = Buf(name)
        self.grans = grans
        self.gen = gen


class Arena:
    def __init__(self, tensor, nbytes):
        self.t = tensor
        self.nbytes = nbytes
        self.regs = [Region() for _ in range((nbytes + GRAN - 1) // GRAN)]
        self.whole = Region()
        self.use_whole = False

    def tile(self, gen, off, shape, dtype, name=""):
        esz = 4 if dtype == F32 else 2
        n = 1
        for s in shape[1:]:
            n *= s
        nb = n * esz
        assert off % 4 == 0 and off + nb <= self.nbytes, (name, off, nb, self.nbytes)
        ap = self.t[:, off // 2:(off + nb) // 2]
        if dtype == F32:
            ap = ap.bitcast(F32)
        if len(shape) == 3:
            ap = ap.rearrange("p (a b) -> p a b", a=shape[1])
        elif len(shape) == 4:
            ap = ap.rearrange("p (a b c) -> p a b c", a=shape[1], b=shape[2])
        if shape[0] != 128:
            ap = ap[0:shape[0]]
        grans = tuple(self.regs[off // GRAN:(off + nb - 1) // GRAN + 1])
        if self.use_whole:
            grans = grans + (self.whole,)
        return Tile(ap, name, grans, gen)


class Op:
    __slots__ = ("dom", "idx", "fn", "vc", "waits", "signal", "stream", "is_dma")


class Sched:
    def __init__(self, nc):
        self.nc = nc
        self.streams = {e: [] for e in ENGS}
        self.vc = {e: {} for e in ENGS}
        self.count = {}
        self.dma_last = [None] * N_DMA_SEMS
        self.dma_rr = {"pool": 0, "sp": 0, "act": 0}

    def _deps(self, reads, writes):
        deps = []
        for t in reads:
            for o in t.buf.writes:
                deps.append((o, "raw"))
            if t.excl:
                for o in t.buf.reads[-4:]:
                    deps.append((o, "rar"))
        for t in writes:
            b = t.buf
            for o in b.writes:
                deps.append((o, "waw"))
            for o in b.reads:
                deps.append((o, "war"))
        for t in list(reads) + list(writes):
            for r in t.grans:
                if r.gen != t.gen:
                    r.prev = r.cur
                    r.cur = {}
                    r.gen = t.gen
                for o in r.prev.values():
                    deps.append((o, "raw"))
        return deps

    def _book(self, op, reads, writes):
        for t in reads:
            t.buf.reads.append(op)
            if len(t.buf.reads) > 96:
                t.buf.reads = t.buf.reads[-96:]
        for t in writes:
            b = t.buf
            if b.reads:
                b.writes = [op]
                b.reads = []
            else:
                b.writes.append(op)
                if len(b.writes) > 64:
                    b.writes = b.writes[-64:]
        for t in list(reads) + list(writes):
            for r in t.grans:
                r.cur[op.dom] = op

    def _resolve(self, stream, deps):
        base = self.vc[stream]
        waits = {}
        for d, kind in deps:
            if d.dom == stream:
                if stream == "pe" or stream == "sp":
                    continue
                if kind == "rar":
                    continue
            if base.get(d.dom, 0) >= d.idx:
                continue
            d.signal = True
            waits[d.dom] = max(waits.get(d.dom, 0), d.idx)
            for k, v in d.vc.items():
                if base.get(k, 0) < v:
                    base[k] = v
            base[d.dom] = max(base.get(d.dom, 0), d.idx)
        return waits

    def _mk(self, dom, stream, fn, deps, is_dma):
        o = Op()
        o.dom = dom
        o.stream = stream
        o.is_dma = is_dma
        o.idx = self.count.get(dom, 0) + 1
        self.count[dom] = o.idx
        o.fn = fn
        o.signal = is_dma
        o.waits = self._resolve(stream, deps)
        o.vc = dict(self.vc[stream])
        self.streams[stream].append(o)
        return o

    def op(self, eng, fn, reads=(), writes=()):
        deps = self._deps(reads, writes)
        o = self._mk(eng, eng, fn, deps, False)
        self._book(o, reads, writes)
        return o

    def dma(self, queue, fn, reads=(), writes=()):
        deps = self._deps(reads, writes)
        lo, n = (0, N_DMA_SEMS - 8) if queue == "pool" else (N_DMA_SEMS - 8, 8)
        k = lo + self.dma_rr[queue] % n
        self.dma_rr[queue] += 1
        prev = self.dma_last[k]
        if prev is not None:
            deps.append((prev, "raw"))
        o = self._mk("d%d" % k, queue, fn, deps, True)
        self.dma_last[k] = o
        self._book(o, reads, writes)
        return o

    def wait_ops(self, eng, ops):
        return self._mk(eng, eng, None, [(o, "raw") for o in ops], False)

    def emit(self, ctx):
        nc = self.nc
        sems = {}
        for e in ENGS:
            sems[e] = ctx.enter_context(nc.semaphore("s_" + e))
        for k in range(N_DMA_SEMS):
            sems["d%d" % k] = ctx.enter_context(nc.semaphore("s_d%d" % k))
        ticket = {}
        for e in ENGS:
            t = 0
            for o in self.streams[e]:
                if o.is_dma:
                    ticket[(o.dom, o.idx)] = 16 * o.idx
                elif o.signal:
                    t += 1
                    ticket[(o.dom, o.idx)] = t
        handles = {"pe": nc.tensor, "act": nc.scalar, "dve": nc.vector,
                   "pool": nc.gpsimd, "sp": nc.sync}

        def run(e):
            h = handles[e]
            for o in self.streams[e]:
                for dom, idx in o.waits.items():
                    h.wait_ge(sems[dom], ticket[(dom, idx)])
                if o.fn is None:
                    continue
                ins = o.fn(h)
                if o.is_dma:
                    ins.then_inc(sems[o.dom], 16)
                elif o.signal:
                    ins.then_inc(sems[o.dom], 1)

        with nc.Block() as block:
            @block.tensor
            def _(eng):
                run("pe")

            @block.scalar
            def _(eng):
                run("act")

            @block.vector
            def _(eng):
                run("dve")

            @block.gpsimd
            def _(eng):
                run("pool")

            @block.sync
            def _(eng):
                run("sp")


D = 1024
L = 2048
NCX = 256
NT = L + NCX
DFF = 2816
NFF = 22
LAT0 = 1
CTX0 = 2051
NCOL = 2308
EPS = 1e-6
NE_WIN = 23
NE = 39
DEPTH = 4


def pcol(t):
    return t + LAT0 if t < L else (t - L) + CTX0


TB512 = [(0, 512, 0), (512, 512, 0), (1024, 512, 0), (1536, 512, 0), (2048, 256, 1)]
TB256 = [(256 * k, 256, 0) for k in range(8)] + [(2048, 256, 1)]
SEGS = [(0, 406, 0), (406, 406, 0), (812, 406, 0), (1218, 415, 0), (1633, 415, 0), (2048, 256, 1)]


def build(n_layers=DEPTH, dbg=False):
    nc = bass.Bass("TRN2", target_bir_lowering=False)

    def din(name, shape):
        return nc.dram_tensor(name, list(shape), F32, kind="ExternalInput").ap()

    xin = din("xin", [D, NT])
    svec = din("svec", [128, 16])
    ada_w = din("ada_w", [4, D, 6 * D])
    ada_b = din("ada_b_r", [128, 4 * 48])
    ngd = din("ng", [128, 64])
    convd = din("convp", [128, 4 * 4 * NFF])
    wup = din("wup_r", [4, NFF, 128, 8 * 256])
    wdn = din("wdn_r", [4, 8, 128, NFF * 128])
    win = din("win_r", [2, 8, 128, 8 * 128])
    winv = din("winv_r", [2, 128, 8 * 512])
    wsT = din("wsT", [2, 128, 4 * 128])
    bsd = din("bs", [2, 1, 512])
    wout = din("wout_r", [2, 8, 128, 8 * 128])
    wqk = din("wqk_r", [2, 16, 128, 8 * 128])
    wv = din("wv_r", [2, 2, 128, 8 * 512])
    wo = din("wo_r", [2, 8, 128, 8 * 128])
    qkgd = din("qkg", [128, 4])
    tabd = din("tab", [2, 16, 128, NE * 64])
    dftL = din("dftL", [2, L, L])
    dftC = din("dftC", [128, 2 * 256])
    dft256 = din("dft256", [2, 256, 256])
    cstd = din("cst", [128, 3 * 128])
    n_out = NT if dbg else L
    yout = nc.dram_tensor("yout", [D, n_out], F32, kind="ExternalOutput").ap()

    with ExitStack() as ctx:
        def sb(name, shape, dt):
            return ctx.enter_context(nc.sbuf_tensor(name, shape, dt))

        XTt = sb("XT", [128, 8, NT], F32)
        R1t = sb("R1", [128, 8 * NCOL], BF16)
        R2t = sb("R2", [128, 18432], BF16)
        R3t = sb("R3", [128, 18432], BF16)
        R4t = sb("R4", [128, 9216], BF16)
        CONSTt = sb("CONST", [128, 3, 128], BF16)
        NGt = sb("NG", [128, 2, 4, 8], F32)
        CONVt = sb("CONVP", [128, 4, 4, NFF], F32)
        QKGt = sb("QKG", [128, 2, 2], F32)
        GQSt = sb("GQS", [128, 2, 2], F32)
        ADABt = sb("ADAB", [128, 4, 48], F32)
        SVt = sb("SV", [128, 8, 2], F32)
        STt = sb("ST", [128, 8, 2], BF16)
        MODt = [sb("MOD%d" % k, [128, 48, 2], F32) for k in range(2)]
        A12t = [sb("A12_%d" % k, [128, 2, 8, 2], F32) for k in range(2)]
        QZt = sb("QZ", [128, 4, 512], BF16)
        PSt = [ctx.enter_context(nc.psum_tensor("ps%d" % k, [128, 512], F32)) for k in range(8)]

        R1 = Arena(R1t, 8 * NCOL * 2)
        R2 = Arena(R2t, 36864)
        R3 = Arena(R3t, 36864)
        R4 = Arena(R4t, 18432)

        S = Sched(nc)
        PS = [Tile(PSt[k][:], "ps%d" % k) for k in range(8)]
        for t_ in PS:
            t_.excl = True
        psrot = [0]

        ps_default = [(0, 1, 2, 3, 4, 5, 6, 7)]

        def ps_next(pool=None):
            if pool is None:
                pool = ps_default[0]
            k = pool[psrot[0] % len(pool)]
            psrot[0] += 1
            return PS[k]

        XTT = [[Tile(XTt[:, c, u * 128:(u + 1) * 128], "xt%d_%d" % (c, u)) for u in range(18)] for c in range(8)]

        def xt_tiles(c, t0, n):
            return [XTT[c][u] for u in range(t0 // 128, (t0 + n - 1) // 128 + 1)]

        def xt_all(t0, n):
            r = []
            for c in range(8):
                r += xt_tiles(c, t0, n)
            return r

        R1.use_whole = True
        HT = R1.tile("r1", 0, [128, 8, NCOL], BF16, "HT")
        HTU = [Tile(None, "ht%d" % u, (R1.whole,), "r1") for u in range(18)]
        HTPAD = Tile(None, "htpad", (R1.whole,), "r1")

        def ht_tiles(t0, n):
            return [HTU[u] for u in range(t0 // 128, (t0 + n - 1) // 128 + 1)]

        def ht_halo(t0, n, col):
            s0, s1 = (0, L) if col == 0 else (L, NT)
            lo = max(t0 - 1, s0)
            hi = min(t0 + n + 1, s1)
            return ht_tiles(lo, hi - lo) + [HTPAD]

        QZ = [Tile(QZt[:, k, :], "qz%d" % k) for k in range(4)]
        CONST = Tile(CONSTt, "const")
        IDENT = CONSTt[:, 0, :]
        ONES = CONSTt[:, 1, :]
        BLK = CONSTt[:, 2, :]
        NG = Tile(NGt, "ng")
        CONV = Tile(CONVt, "conv")
        QKG = Tile(QKGt, "qkg")
        GQS = Tile(GQSt, "gqs")
        ADAB = Tile(ADABt, "adab")
        SV = Tile(SVt, "sv")
        ST = Tile(STt, "st")
        MOD = [Tile(MODt[k], "mod%d" % k) for k in range(2)]
        A12 = [Tile(A12t[k], "a12_%d" % k) for k in range(2)]

        RECIP = "reciprocal"

        def OP(eng, method, *args, reads=(), writes=(), **kw):
            return S.op(eng, lambda e: getattr(e, method)(*args, **kw), reads, writes)

        def MM(out, lhsT, rhs, start, stop, reads, writes):
            return S.op("pe", lambda e: e.matmul(out, lhsT, rhs, start=start, stop=stop), reads, writes)

        def DMA(queue, out, in_, reads=(), writes=()):
            return S.dma(queue, lambda e: e.dma_start(out=out, in_=in_), reads, writes)

        for c in range(8):
            DMA("sp", XTt[:, c, :], xin[c * 128:(c + 1) * 128, :], writes=XTT[c])
        DMA("pool", CONSTt[:].rearrange("p a b -> p (a b)"), cstd, writes=[CONST])
        DMA("sp", NGt[:].rearrange("p a b c -> p (a b c)"), ngd, writes=[NG])
        DMA("sp", CONVt[:].rearrange("p a b c -> p (a b c)"), convd, writes=[CONV])
        DMA("sp", QKGt[:].rearrange("p a b -> p (a b)"), qkgd, writes=[QKG])
        DMA("sp", ADABt[:].rearrange("p a b -> p (a b)"), ada_b, writes=[ADAB])
        DMA("sp", SVt[:].rearrange("p a b -> p (a b)"), svec, writes=[SV])
        OP("act", "activation", STt[:], SVt[:], AF.Silu, reads=[SV], writes=[ST])
        OP("dve", "tensor_scalar", GQSt[:], QKGt[:], 0.125, None, ALU.mult, reads=[QKG], writes=[GQS])

        def mods(l):
            gen = "L%d.mods" % l
            md = MOD[l % 2]
            mdt = MODt[l % 2]
            a12 = A12[l % 2]
            a12t = A12t[l % 2]
            BW = int(os.environ.get("K_BW", "512"))
            nb = 6144 // BW
            ADA = [R2.tile(gen, k * 8192, [128, 8, BW], BF16, "ada%d" % k) for k in range(2)]
            adv = ada_w[l].rearrange("(c p) n -> p c n", p=128)
            psm = PS[7]
            for blk in range(nb):
                a = ADA[blk % 2]
                DMA("pool", a.ap, adv[:, :, blk * BW:(blk + 1) * BW], writes=[a])
                for jj in range(BW // 128):
                    j = blk * (BW // 128) + jj
                    for kc in range(8):
                        MM(psm.ap[:, 2 * j:2 * j + 2], a.ap[:, kc, jj * 128:(jj + 1) * 128], STt[:, kc, :],
                           kc == 0, kc == 7, [a, ST], [psm])
            mods_final(l)

        def mods_final_part(l, part):
            md = MOD[l % 2]
            mdt = MODt[l % 2]
            a12 = A12[l % 2]
            a12t = A12t[l % 2]
            psm = PS[7]
            j0, j1 = (0, 16) if part == 0 else (16, 48)
            OP("dve", "tensor_tensor", mdt[:, j0:j1, :], psm.ap[:, 2 * j0:2 * j1].rearrange("p (a b) -> p a b", b=2),
               ADABt[:, l, j0:j1].unsqueeze(2).to_broadcast([128, j1 - j0, 2]), ALU.add,
               reads=[psm, ADAB], writes=[md])
            w, jj0 = (0, 8) if part == 0 else (1, 32)
            OP("dve", "tensor_scalar", a12t[:, w], mdt[:, jj0:jj0 + 8, :], 1.0, None, ALU.add,
               reads=[md], writes=[a12])
            OP("dve", "tensor_tensor", a12t[:, w], a12t[:, w],
               NGt[:, w, l, :].unsqueeze(2).to_broadcast([128, 8, 2]), ALU.mult,
               reads=[a12, NG], writes=[a12])

        def mods0():
            gen = "L0.mods"
            ADA = [R2.tile(gen, k * 8192, [128, 8, 512], BF16, "ada%d" % k) for k in range(4)] + \
                  [R4.tile(gen, k * 8192, [128, 8, 512], BF16, "adb%d" % k) for k in range(2)]
            adv = ada_w[0].rearrange("(c p) n -> p c n", p=128)
            psm = PS[7]
            st = [4]

            def dma(blk):
                a = ADA[blk % 6]
                DMA("pool", a.ap, adv[:, :, blk * 512:(blk + 1) * 512], reads=(XTT[7][:1] if blk >= 4 else []),
                    writes=[a])

            def mm(blk):
                a = ADA[blk % 6]
                for jj in range(4):
                    j = blk * 4 + jj
                    for kc in range(8):
                        MM(psm.ap[:, 2 * j:2 * j + 2], a.ap[:, kc, jj * 128:(jj + 1) * 128], STt[:, kc, :],
                           kc == 0, kc == 7, [a, ST], [psm])
                if blk + 6 < 12:
                    dma(blk + 6)

            for blk in range(6):
                dma(blk)
            for blk in range(4):
                mm(blk)
            mods_final_part(0, 0)

            def hook(bi):
                for _ in range(2):
                    if st[0] < 12:
                        mm(st[0])
                        st[0] += 1

            def finish():
                while st[0] < 12:
                    mm(st[0])
                    st[0] += 1
                mods_final_part(0, 1)
            return hook, finish

        def mods_final(l):
            md = MOD[l % 2]
            mdt = MODt[l % 2]
            a12 = A12[l % 2]
            a12t = A12t[l % 2]
            psm = PS[7]
            OP("dve", "tensor_tensor", mdt[:], psm.ap[:, 0:96].rearrange("p (a b) -> p a b", b=2),
               ADABt[:, l, :].unsqueeze(2).to_broadcast([128, 48, 2]), ALU.add,
               reads=[psm, ADAB], writes=[md])
            for w, j0 in ((0, 8), (1, 32)):
                OP("dve", "tensor_scalar", a12t[:, w], mdt[:, j0:j0 + 8, :], 1.0, None, ALU.add,
                   reads=[md], writes=[a12])
                OP("dve", "tensor_tensor", a12t[:, w], a12t[:, w],
                   NGt[:, w, l, :].unsqueeze(2).to_broadcast([128, 8, 2]), ALU.mult,
                   reads=[a12, NG], writes=[a12])

        def modulate(l, which, blocks, zero_pads, external=False, hook=None):
            gen = "L%d.m%d" % (l, which)
            md = MOD[l % 2]
            mdt = MODt[l % 2]
            a12 = A12[l % 2]
            a12t = A12t[l % 2]
            sh0 = 0 if which == 1 else 24
            SQ = [R3.tile(gen, k * 4096, [128, 4, 512], BF16, "sq%d" % k) for k in range(2)]
            RS = [R3.tile(gen, 8192 + k * 2048, [128, 512], F32, "rs%d" % k) for k in range(2)]
            RR = [R3.tile(gen, 12288 + k * 2048, [128, 512], F32, "rr%d" % k) for k in range(3)]
            T1 = [R3.tile(gen, 18432 + k * 2048, [128, 512], F32, "t1_%d" % k) for k in range(4)]
            if zero_pads:
                for c0 in (0, 2049, 2050, 2307):
                    OP("pool", "memset", HT.ap[:, :, c0:c0 + 1], 0.0, writes=[HTPAD])
            cnt = [0]

            def stage_a(bi):
                t0, n, col = blocks[bi]
                sq = SQ[0]
                sq2 = SQ[1]
                rs = RS[bi % 2]
                rr = RR[bi % 3]
                xts = xt_all(t0, n)
                OP("act", "activation", sq.ap[:, 0:4, 0:n], XTt[:, 0:4, t0:t0 + n], AF.Square, reads=xts, writes=[sq])
                OP("dve", "tensor_tensor", sq2.ap[:, :, 0:n], XTt[:, 4:8, t0:t0 + n], XTt[:, 4:8, t0:t0 + n], ALU.mult,
                   reads=xts, writes=[sq2])
                pr = ps_next()
                for c in range(8):
                    src = sq.ap[:, c, 0:n] if c < 4 else sq2.ap[:, c - 4, 0:n]
                    MM(pr.ap[:, 0:n], ONES, src, c == 0, c == 7, [sq, sq2, CONST], [pr])
                OP("act", "activation", rs.ap[:, 0:n], pr.ap[:, 0:n], AF.Ln, bias=EPS, scale=1.0 / D,
                   reads=[pr], writes=[rs])
                OP("act", "activation", rr.ap[:, 0:n], rs.ap[:, 0:n], AF.Exp, scale=-0.5, reads=[rs], writes=[rr])

            def stage_b(bi):
                t0, n, col = blocks[bi]
                rr = RR[bi % 3]
                pc0 = pcol(t0)
                hts = ht_tiles(t0, n)
                for c in range(8):
                    t1 = T1[cnt[0] % 4]
                    cnt[0] += 1
                    OP("dve", "tensor_tensor", t1.ap[:, 0:n], XTt[:, c, t0:t0 + n],
                       rr.ap[:, 0:n], ALU.mult, reads=xt_tiles(c, t0, n) + [rr], writes=[t1])
                    if c < 6:
                        OP("act", "activation", HT.ap[:, c, pc0:pc0 + n], t1.ap[:, 0:n], AF.Identity,
                           bias=mdt[:, sh0 + c, col:col + 1], scale=a12t[:, which - 1, c, col:col + 1],
                           reads=[t1, md, a12], writes=hts)
                    else:
                        OP("dve", "tensor_scalar", HT.ap[:, c, pc0:pc0 + n], t1.ap[:, 0:n],
                           a12t[:, which - 1, c, col:col + 1], mdt[:, sh0 + c, col:col + 1], ALU.mult, ALU.add,
                           reads=[t1, md, a12], writes=hts)

            nb_ = len(blocks)
            if external:
                return stage_a, stage_b
            for bi in range(nb_ + 1):
                if bi < nb_:
                    stage_a(bi)
                if hook:
                    hook(bi)
                if bi >= 1:
                    stage_b(bi - 1)

        def ffn(l, last, next_mods):
            gen = "L%d.ffn" % l
            md = MOD[l % 2]
            mdt = MODt[l % 2]
            segs = SEGS[:5] if last else SEGS
            passes = [segs[0:3], segs[3:]]
            HW = 1218
            NWU, NT1, NWD, NAD = 3, 2, 2, 2
            WU = [R3.tile(gen, k * 4096, [128, 8, 256], BF16, "wu%d" % k) for k in range(NWU)]
            T1 = [R3.tile(gen, 12288 + k * 2048, [128, 512], F32, "ft1_%d" % k) for k in range(NT1)]
            T2 = [R3.tile(gen, 16384 + k * 2048, [128, 512], F32, "ft2_%d" % k) for k in range(NT1)]
            ADAI = [R3.tile(gen, 20480 + k * 2048, [128, 8, 128], BF16, "adai%d" % k) for k in range(NAD)]
            WD = [R3.tile(gen, 24576 + k * 5632, [128, NFF, 128], BF16, "wd%d" % k) for k in range(NWD)]
            H = [(R2.tile(gen, j * 2 * HW, [128, HW], BF16, "h%d" % j) if j < 15 else
                  R4.tile(gen, (j - 15) * 2 * HW, [128, HW], BF16, "h%d" % j)) for j in range(NFF)]
            if next_mods:
                advn = ada_w[l + 1].rearrange("(c p) n -> p c n", p=128)
            psm = PS[7]
            P7 = (0, 1, 2, 3, 4, 5, 6)
            mstate = [0]

            def mods_step():
                if not next_mods:
                    return
                g = mstate[0]
                mstate[0] += 1
                if g < 48:
                    a = ADAI[g % NAD]
                    DMA("pool", a.ap, advn[:, :, g * 128:(g + 1) * 128], writes=[a])
                jm = g - (NAD - 1)
                if 0 <= jm < 48:
                    a = ADAI[jm % NAD]
                    for kc in range(8):
                        MM(psm.ap[:, 2 * jm:2 * jm + 2], a.ap[:, kc, :], STt[:, kc, :], kc == 0, kc == 7,
                           [a, ST], [psm])

            ucnt = 0
            for pi, pss in enumerate(passes):
                NPRE = NWU - 1
                for j in range(NFF + NPRE):
                    jj = j
                    if jj < NFF:
                        wt = WU[jj % NWU]
                        DMA("pool", wt.ap.rearrange("p a b -> p (a b)"), wup[l, jj], writes=[wt])
                    j = j - NPRE
                    if j < 0:
                        continue
                    wt = WU[j % NWU]
                    so = 0
                    mods_step()
                    for (t0, n, col) in pss:
                        pc0 = pcol(t0)
                        pg = ps_next(P7)
                        pv = ps_next(P7)
                        hts = ht_halo(t0, n, col)
                        for kc in range(8):
                            MM(pg.ap[:, 0:n + 2], wt.ap[:, kc, 0:128], HT.ap[:, kc, pc0 - 1:pc0 + n + 1],
                               kc == 0, kc == 7, [wt] + hts, [pg])
                        for kc in range(8):
                            MM(pv.ap[:, 0:n], wt.ap[:, kc, 128:256], HT.ap[:, kc, pc0:pc0 + n],
                               kc == 0, kc == 7, [wt] + hts, [pv])
                        t1 = T1[ucnt % NT1]
                        t2 = T2[ucnt % NT1]
                        ucnt += 1
                        OP("act", "activation", t1.ap[:, 0:n], pg.ap[:, 1:n + 1], AF.Identity,
                           bias=CONVt[:, l, 3, j:j + 1], scale=CONVt[:, l, 1, j:j + 1],
                           reads=[pg, CONV], writes=[t1])
                        OP("dve", "scalar_tensor_tensor", t1.ap[:, 0:n], pg.ap[:, 0:n], CONVt[:, l, 0, j:j + 1],
                           t1.ap[:, 0:n], ALU.mult, ALU.add, reads=[pg, CONV, t1], writes=[t1])
                        OP("dve", "scalar_tensor_tensor", t1.ap[:, 0:n], pg.ap[:, 2:n + 2], CONVt[:, l, 2, j:j + 1],
                           t1.ap[:, 0:n], ALU.mult, ALU.add, reads=[pg, CONV, t1], writes=[t1])
                        OP("act", "activation", t2.ap[:, 0:n], t1.ap[:, 0:n], AF.Silu, reads=[t1], writes=[t2])
                        OP("dve", "tensor_tensor", H[j].ap[:, so:so + n], t2.ap[:, 0:n], pv.ap[:, 0:n], ALU.mult,
                           reads=[t2, pv], writes=[H[j]])
                        so += n
                MPRE = NWD - 1
                for m in range(8 + MPRE):
                    mm_ = m
                    if mm_ < 8:
                        wt = WD[mm_ % NWD]
                        DMA("pool", wt.ap.rearrange("p a b -> p (a b)"), wdn[l, mm_], writes=[wt])
                    m = m - MPRE
                    if m < 0:
                        continue
                    wt = WD[m % NWD]
                    so = 0
                    mods_step()
                    for (t0, n, col) in pss:
                        po = ps_next(P7)
                        for kc in range(NFF):
                            MM(po.ap[:, 0:n], wt.ap[:, kc, :], H[kc].ap[:, so:so + n], kc == 0, kc == NFF - 1,
                               [wt, H[kc]], [po])
                        xts = xt_tiles(m, t0, n)
                        OP("dve", "scalar_tensor_tensor", XTt[:, m, t0:t0 + n], po.ap[:, 0:n],
                           mdt[:, 40 + m, col:col + 1], XTt[:, m, t0:t0 + n], ALU.mult, ALU.add,
                           reads=[po, md] + xts, writes=xts)
                        so += n
            if next_mods:
                while mstate[0] < 48 + NAD:
                    mods_step()
                mods_final(l + 1)

        def mk_w(gen, woff=0):
            return [R4.tile(gen, woff + k * 2048, [128, 8, 128], BF16, "w%d" % k) for k in range(3)]

        def proj_stream(W, wsrc_fn, nj, blocks, rhs_fn, rhs_tiles_fn, evac_fn, flush_fn=None, post_fn=None,
                        blocks_fn=None):
            NW = len(W)
            for j in range(nj + NW - 1):
                if j < nj:
                    wt = W[j % NW]
                    DMA("pool", wt.ap.rearrange("p a b -> p (a b)"), wsrc_fn(j), writes=[wt])
                jj = j - (NW - 1)
                if jj < 0:
                    continue
                wt = W[jj % NW]
                for (t0, n, col) in (blocks_fn(jj) if blocks_fn else blocks):
                    p = ps_next()
                    for kc in range(8):
                        MM(p.ap[:, 0:n], wt.ap[:, kc, :], rhs_fn(kc, t0, n), kc == 0, kc == 7,
                           [wt] + rhs_tiles_fn(kc, t0, n), [p])
                    evac_fn(jj, p, t0, n, col)
                if post_fn:
                    post_fn(jj)
            if flush_fn:
                flush_fn()

        def out_proj_overlap(l, gen, wsrc_fn, blocks, rhs_fn, rhs_tiles_fn, last):
            WR = [R4.tile(gen, k * 2048, [128, 8, 128], BF16, "wr%d" % k) for k in range(8)]
            for m in range(8):
                DMA("pool", WR[m].ap.rearrange("p a b -> p (a b)"), wsrc_fn(m), writes=[WR[m]])
            ev = resid_evac(l, 16)
            mblocks = TB512[:4] if last else TB512
            assert list(mblocks) == list(blocks)
            sa, sb_ = modulate(l, 2, mblocks, True, external=True)
            nb_ = len(blocks)
            for b in range(nb_ + 2):
                if b < nb_:
                    t0, n, col = blocks[b]
                    for m in range(8):
                        p = ps_next()
                        for kc in range(8):
                            MM(p.ap[:, 0:n], WR[m].ap[:, kc, :], rhs_fn(kc, t0, n), kc == 0, kc == 7,
                               [WR[m]] + rhs_tiles_fn(kc, t0, n), [p])
                        ev(m, p, t0, n, col)
                if 1 <= b <= nb_:
                    sa(b - 1)
                if 2 <= b:
                    sb_(b - 2)
            sb_(nb_ - 1)

        def resid_evac(l, which_g):
            md = MOD[l % 2]
            mdt = MODt[l % 2]

            def f(m, p, t0, n, col):
                xts = xt_tiles(m, t0, n)
                OP("dve", "scalar_tensor_tensor", XTt[:, m, t0:t0 + n], p.ap[:, 0:n],
                   mdt[:, which_g + m, col:col + 1], XTt[:, m, t0:t0 + n], ALU.mult, ALU.add,
                   reads=[p, md] + xts, writes=xts)
            return f

        def even_mixer(l, last):
            i = l // 2
            gen = "L%d.even" % l
            blocks = TB512[:4] if last else TB512
            nun = 16 if last else 18
            FT = R2.tile(gen, 0, [128, 4, NT], BF16, "FT")
            UT = R2.tile(gen, 18432, [128, 4, NT], BF16, "UT")
            FTU = [[Tile(None, "ft%d_%d" % (g, u), FT.grans, gen) for u in range(18)] for g in range(4)]
            UTU = [[Tile(None, "ut%d_%d" % (g, u), UT.grans, gen) for u in range(18)] for g in range(4)]

            def fu(TU, g, t0, n):
                return [TU[g][u] for u in range(t0 // 128, (t0 + n - 1) // 128 + 1)]

            def evac_a(j, p, t0, n, col):
                if j < 4:
                    OP("dve", "tensor_copy", FT.ap[:, j, t0:t0 + n], p.ap[:, 0:n], reads=[p], writes=fu(FTU, j, t0, n))
                else:
                    OP("act", "activation", UT.ap[:, j - 4, t0:t0 + n], p.ap[:, 0:n], AF.Gelu, reads=[p],
                       writes=fu(UTU, j - 4, t0, n))
            proj_stream(mk_w(gen), lambda j: win[i, j], 8, blocks,
                        lambda kc, t0, n: HT.ap[:, kc, pcol(t0):pcol(t0) + n],
                        lambda kc, t0, n: ht_tiles(t0, n), evac_a)

            WV = R4.tile(gen, 6144, [128, 8, 512], BF16, "wvin")
            WS = R4.tile(gen, 14336, [128, 4, 128], BF16, "wsT")
            BS = R4.tile(gen, 15360, [1, 512], BF16, "bs")
            DMA("pool", WV.ap.rearrange("p a b -> p (a b)"), winv[i], writes=[WV])
            DMA("pool", WS.ap.rearrange("p a b -> p (a b)"), wsT[i], writes=[WS])
            DMA("pool", BS.ap, bsd[i], writes=[BS])
            VN = [R3.tile(gen, 18432 + k * 4096, [128, 4, 4, 128], BF16, "vn%d" % k) for k in range(2)]
            VF = [R3.tile(gen, 26624 + k * 2048, [128, 512], F32, "vf%d" % k) for k in range(2)]
            SQV = [R3.tile(gen, 30720 + k * 2048, [128, 512], F32, "sqv%d" % k) for k in range(2)]
            SSV = [R3.tile(gen, 34816 + k * 64, [128, 4], F32, "ssv%d" % k) for k in range(2)]
            RSV = [R3.tile(gen, 34944 + k * 64, [128, 4], F32, "rsv%d" % k) for k in range(2)]
            RRV = [R3.tile(gen, 35072 + k * 64, [128, 4], F32, "rrv%d" % k) for k in range(2)]
            NEGH = R3.tile(gen, 35200, [128, 4], F32, "negh")
            OP("pool", "memset", NEGH.ap, -0.5, writes=[NEGH])
            groups = [(512 * g, 4) for g in range(4)] + ([] if last else [(2048, 2)])
            cnt = 0
            sgu_pend = []
            for gi, (t0g, ntt) in enumerate(groups):
                vn = VN[gi % 2]
                for tt in range(ntt):
                    t0 = t0g + tt * 128
                    pc0 = pcol(t0)
                    p = ps_next()
                    for kc in range(8):
                        MM(p.ap[:, :], HT.ap[:, kc, pc0:pc0 + 128], WV.ap[:, kc, :], kc == 0, kc == 7,
                           [WV] + ht_tiles(t0, 128), [p])
                    vf = VF[cnt % 2]
                    sqv = SQV[cnt % 2]
                    ssv = SSV[cnt % 2]
                    rsv = RSV[cnt % 2]
                    rrv = RRV[cnt % 2]
                    cnt += 1
                    OP("act", "activation", vf.ap, p.ap, AF.Gelu, reads=[p], writes=[vf])
                    OP("pool", "tensor_tensor", sqv.ap, vf.ap, vf.ap, ALU.mult, reads=[vf], writes=[sqv])
                    OP("dve", "tensor_reduce", ssv.ap, sqv.ap.rearrange("p (a b) -> p a b", a=4), AX.X, ALU.add,
                       reads=[sqv], writes=[ssv])
                    OP("dve", "tensor_scalar", rsv.ap, ssv.ap, 1.0 / 128, EPS, ALU.mult, ALU.add,
                       reads=[ssv], writes=[rsv])
                    OP("pool", "tensor_tensor", rrv.ap, rsv.ap, NEGH.ap, ALU.pow, reads=[rsv, NEGH], writes=[rrv])
                    OP("dve", "tensor_tensor", vn.ap[:, tt], vf.ap.rearrange("p (a b) -> p a b", a=4),
                       rrv.ap.unsqueeze(2).to_broadcast([128, 4, 128]), ALU.mult, reads=[vf, rrv], writes=[vn])
                def sgu(vn=vn, t0g=t0g, ntt=ntt):
                    for g in range(4):
                        p = ps_next()
                        for tt in range(ntt):
                            MM(p.ap[:, tt * 128:(tt + 1) * 128], vn.ap[:, tt, g, :], WS.ap[:, g, :], True, False,
                               [vn, WS], [p])
                            MM(p.ap[:, tt * 128:(tt + 1) * 128], ONES[0:1, :], BS.ap[0:1, g * 128:(g + 1) * 128],
                               False, True, [BS, CONST], [p])
                        n = ntt * 128
                        uts = fu(UTU, g, t0g, n)
                        OP("dve", "tensor_tensor", UT.ap[:, g, t0g:t0g + n], UT.ap[:, g, t0g:t0g + n], p.ap[:, 0:n],
                           ALU.mult, reads=[p] + uts, writes=uts)
                if sgu_pend:
                    sgu_pend.pop(0)()
                sgu_pend.append(sgu)
            while sgu_pend:
                sgu_pend.pop(0)()

            gen2 = "L%d.four" % l
            CS = R4.tile(gen2, 6144, [128, 2, 256], BF16, "cs")
            DMA("pool", CS.ap.rearrange("p a b -> p (a b)"), dftC, writes=[CS])
            PQ = [R3.tile(gen2, t * 2048, [128, 4, 256], BF16, "pq%d" % t) for t in range(18)]
            for t in range(nun):
                v = 0 if t < 16 else 1
                for gp in range(2):
                    p = ps_next()
                    for g2 in range(2):
                        g = 2 * gp + g2
                        MM(p.ap[:, g2 * 256:(g2 + 1) * 256], FT.ap[:, g, t * 128:(t + 1) * 128], CS.ap[:, v, :],
                           True, True, [CS] + fu(FTU, g, t * 128, 128), [p])
                    if (t + gp) % 2 == 0:
                        OP("act", "activation", PQ[t].ap[:, 2 * gp:2 * gp + 2, :].rearrange("p a b -> p (a b)"),
                           p.ap, AF.Copy, reads=[p], writes=[PQ[t]])
                    else:
                        OP("dve", "tensor_copy", PQ[t].ap[:, 2 * gp:2 * gp + 2, :].rearrange("p a b -> p (a b)"),
                           p.ap, reads=[p], writes=[PQ[t]])
            DT = [[R1.tile(gen2, (b * 2 + mtx) * 8192, [128, 16, 256], BF16, "dt%d_%d" % (b, mtx)) for mtx in range(2)]
                  for b in range(2)]
            dlv = [dftL[mtx].rearrange("(c p) n -> p c n", p=128) for mtx in range(2)]
            ecnt = 0
            for kt in range(8):
                d = DT[kt % 2]
                for mtx in range(2):
                    DMA("pool", d[mtx].ap, dlv[mtx][:, :, kt * 256:(kt + 1) * 256], writes=[d[mtx]])
                for g in range(4):
                    p = ps_next()
                    for lc in range(16):
                        MM(p.ap[:, 0:256], PQ[lc].ap[:, g, 0:128], d[0].ap[:, lc, :], lc == 0, False,
                           [PQ[lc], d[0]], [p])
                        MM(p.ap[:, 0:256], PQ[lc].ap[:, g, 128:256], d[1].ap[:, lc, :], False, lc == 15,
                           [PQ[lc], d[1]], [p])
                    fts = fu(FTU, g, kt * 256, 256)
                    if ecnt % 2 == 0:
                        OP("act", "activation", FT.ap[:, g, kt * 256:(kt + 1) * 256], p.ap[:, 0:256], AF.Copy,
                           reads=[p], writes=fts)
                    else:
                        OP("dve", "tensor_copy", FT.ap[:, g, kt * 256:(kt + 1) * 256], p.ap[:, 0:256],
                           reads=[p], writes=fts)
                    ecnt += 1
            if not last:
                D2 = R4.tile(gen2, 8192, [128, 2, 2, 256], BF16, "d256")
                for mtx in range(2):
                    DMA("pool", D2.ap[:, mtx], dft256[mtx].rearrange("(c p) n -> p c n", p=128), writes=[D2])
                for g in range(4):
                    p = ps_next()
                    for lc in range(2):
                        MM(p.ap[:, 0:256], PQ[16 + lc].ap[:, g, 0:128], D2.ap[:, 0, lc, :], lc == 0, False,
                           [PQ[16 + lc], D2], [p])
                        MM(p.ap[:, 0:256], PQ[16 + lc].ap[:, g, 128:256], D2.ap[:, 1, lc, :], False, lc == 1,
                           [PQ[16 + lc], D2], [p])
                    OP("act", "activation", FT.ap[:, g, 2048:2304], p.ap[:, 0:256], AF.Copy, reads=[p],
                       writes=fu(FTU, g, 2048, 256))

            def rhs_d(kc, t0, n):
                return FT.ap[:, kc, t0:t0 + n] if kc < 4 else UT.ap[:, kc - 4, t0:t0 + n]

            def rhs_t(kc, t0, n):
                return fu(FTU, kc, t0, n) if kc < 4 else fu(UTU, kc - 4, t0, n)
            out_proj_overlap(l, "L%d.eout" % l, lambda m: wout[i, m], blocks, rhs_d, rhs_t, last)

        def odd_mixer(l, last):
            i = l // 2
            gen = "L%d.odd" % l
            KT = R2.tile(gen, 0, [128, 8, NT], BF16, "KT")
            KTU = [[Tile(None, "kt%d_%d" % (c, u), KT.grans, gen) for u in range(18)] for c in range(8)]
            VT = [R3.tile(gen, t * 2048, [128, 1024], BF16, "vt%d" % t) for t in range(18)]

            def ktu(c, t0, n):
                return [KTU[c][u] for u in range(t0 // 128, (t0 + n - 1) // 128 + 1)]

            def norm_evac(W0, gsrc, dst_fn, bs, gen, nset, depth=2, sq_tiles=None):
                if sq_tiles is None:
                    SQ = [R4.tile(gen, W0 + k * bs * 2, [128, bs], BF16, "qsq%d" % k) for k in range(nset)]
                    o2 = W0 + nset * bs * 2
                else:
                    SQ = sq_tiles
                    o2 = W0
                RS = [R4.tile(gen, o2 + k * bs * 4, [128, bs], F32, "qrs%d" % k) for k in range(nset)]
                assert o2 + nset * bs * 4 <= 18432
                cnt = [0]
                pend = []

                def flush(keep=0):
                    while len(pend) > keep:
                        pend.pop(0)()

                def f(j, p, t0, n, col):
                    k = cnt[0] % nset
                    cnt[0] += 1
                    sq, rs = SQ[k], RS[k]
                    OP("act", "activation", sq.ap[:, 0:n], p.ap[:, 0:n], AF.Square, reads=[p], writes=[sq])

                    def part2():
                        p2 = ps_next()
                        MM(p2.ap[:, 0:n], BLK, sq.ap[:, 0:n], True, True, [sq, CONST], [p2])
                        OP("act", "activation", rs.ap[:, 0:n], p2.ap[:, 0:n], AF.Ln, bias=EPS, scale=1.0 / 64,
                           reads=[p2], writes=[rs])
                        OP("act", "activation", rs.ap[:, 0:n], rs.ap[:, 0:n], AF.Exp, scale=-0.5, reads=[rs], writes=[rs])
                        dst, dtiles = dst_fn(j, t0, n)
                        OP("dve", "scalar_tensor_tensor", dst, p.ap[:, 0:n], gsrc, rs.ap[:, 0:n], ALU.mult, ALU.mult,
                           reads=[p, rs, GQS, QKG], writes=dtiles)
                    pend.append(part2)
                    flush(depth)
                return f, flush

            kev, kflush = norm_evac(6144, QKGt[:, i, 1:2],
                                    lambda j, t0, n: (KT.ap[:, j, t0:t0 + n], ktu(j, t0, n)), 512, gen, 3)
            proj_stream(mk_w(gen), lambda j: wqk[i, 8 + j], 8, TB512,
                        lambda kc, t0, n: HT.ap[:, kc, pcol(t0):pcol(t0) + n],
                        lambda kc, t0, n: ht_tiles(t0, n), kev, flush_fn=kflush)

            genv = "L%d.oddv" % l
            WVt = [R4.tile(genv, k * 8192, [128, 8, 512], BF16, "wvo%d" % k) for k in range(2)]
            ec = 0
            for nt_ in range(2):
                DMA("pool", WVt[nt_].ap.rearrange("p a b -> p (a b)"), wv[i, nt_], writes=[WVt[nt_]])
            for nt_ in range(2):
                for t in range(18):
                    p = ps_next()
                    pc0 = pcol(t * 128)
                    for kc in range(8):
                        MM(p.ap[:, :], HT.ap[:, kc, pc0:pc0 + 128], WVt[nt_].ap[:, kc, :], kc == 0, kc == 7,
                           [WVt[nt_]] + ht_tiles(t * 128, 128), [p])
                    if ec % 2 == 0:
                        OP("act", "activation", VT[t].ap[:, nt_ * 512:(nt_ + 1) * 512], p.ap, AF.Copy,
                           reads=[p], writes=[VT[t]])
                    else:
                        OP("dve", "tensor_copy", VT[t].ap[:, nt_ * 512:(nt_ + 1) * 512], p.ap,
                           reads=[p], writes=[VT[t]])
                    ec += 1

            genq = "L%d.oddq" % l
            qpass = TB512[:4] if last else TB512
            QTMP = R4.tile(genq, 6144, [128, 8, 512], BF16, "qtmp")
            WQ = mk_w(genq)
            sqz = []
            for kz in range(3):
                tz = Tile(QZt[:, kz, 0:256], "qzsq%d" % kz)
                tz.buf = QZ[kz].buf
                sqz.append(tz)
            qev, qflush = norm_evac(14336, GQSt[:, i, 0:1],
                                    lambda j, t0_, n_: (QTMP.ap[:, j % 8, (t0_ % 512):(t0_ % 512) + n_], [QTMP]),
                                    256, genq, 3, sq_tiles=sqz)

            def qpost(jj):
                if jj % 8 == 7:
                    qflush()
                    t0, n, col = qpass[jj // 8]
                    pc0 = pcol(t0)
                    OP("dve", "tensor_copy", HT.ap[:, :, pc0:pc0 + n], QTMP.ap[:, :, 0:n], reads=[QTMP],
                       writes=ht_tiles(t0, n))

            def qblocks_fn(jj):
                t0, n, col = qpass[jj // 8]
                return [(t0 + 256 * s_, 256, col) for s_ in range(n // 256)]
            proj_stream(WQ, lambda j: wqk[i, j % 8], 8 * len(qpass), None,
                        lambda kc, t0_, n_: HT.ap[:, kc, pcol(t0_):pcol(t0_) + n_],
                        lambda kc, t0_, n_: ht_tiles(t0_, n_), qev, post_fn=qpost, blocks_fn=qblocks_fn)

            gena = "L%d.attn" % l
            TAB = [R4.tile(gena, k * 4992, [128, NE, 64], BF16, "tab%d" % k) for k in range(2)]
            PT = [R4.tile(gena, 9984 + k * 1024, [128, 512], BF16, "pt%d" % k) for k in range(4)]
            RD = [R4.tile(gena, 14080 + k * 2048, [128, 512], F32, "rd%d" % k) for k in range(2)]
            qbs = []
            qbs.append((0, 256, [(kr0, NE_WIN + (7 - kr0)) for kr0 in (0, 2, 4, 6)]))
            for k in range(3):
                qr0 = 4 + 8 * k
                qbs.append((64 * qr0, 512, [(kr0, qr0 - kr0 + 7 + 4) for kr0 in range(qr0 - 4, qr0 + 12, 2)]))
            qbs.append((1792, 256, [(kr0, NE_WIN + (35 - kr0)) for kr0 in (24, 26, 28, 30)]))
            if not last:
                qbs.append((2048, 256, []))
            SK = int(os.environ.get('K_SK', '2'))
            items = []
            ocnt = 0
            for h in range(16):
                for (qt0, nq, loc) in qbs:
                    chunks = [(16, None, 0, nq)]
                    for ci_, (kr0, ei0) in enumerate(loc):
                        if nq == 512:
                            lo, hi = max(0, 2 * ci_ - 7), min(7, 2 * ci_ + 1)
                        else:
                            lo, hi = 0, nq // 64 - 1
                        chunks.append((kr0 // 2, ei0 + lo, lo * 64, (hi - lo + 1) * 64))
                    chunks.append((17, None, 0, nq))
                    for ci, (t, ei0, qo, nqs) in enumerate(chunks):
                        items.append((h, qt0, nq, t, ei0, ci == 0, ci == len(chunks) - 1, ocnt, qo, nqs))
                    ocnt += 1
            staged = []
            tab_loaded = -1
            for kz in range(4):
                OP("pool", "memset", QZt[:, kz, :], 0.0, writes=[QZ[kz]])
            qz_cur = {}
            qz_cnt = [0, 0]
            for k in range(len(items) + SK):
                if k < len(items):
                    h, qt0, nq, t, ei0, first, lastc, oc, qo, nqs = items[k]
                    i2 = h // 2
                    rows = slice(64 * (h % 2), 64 * (h % 2) + 64)
                    tb = TAB[h % 2]
                    if tab_loaded < h:
                        DMA("pool", tb.ap.rearrange("p a b -> p (a b)"), tabd[i, h], writes=[tb])
                        tab_loaded = h
                    pq0 = pcol(qt0)
                    qtiles = ht_tiles(qt0, nq)
                    ps_ = PS[k % 3]
                    pt = PT[k % 4]
                    use_qz = os.environ.get('K_NOQZ') != '1'
                    if not use_qz:
                        MM(ps_.ap[:, 0:nqs], KT.ap[rows, i2, t * 128:(t + 1) * 128],
                           HT.ap[rows, i2, pq0 + qo:pq0 + qo + nqs],
                           True, ei0 is None, ktu(i2, t * 128, 128) + qtiles, [ps_])
                    if first and use_qz:
                        par = h % 2
                        qz = QZ[2 * par + qz_cnt[par] % 2]
                        qz_cnt[par] += 1
                        OP("pool", "tensor_copy", qz.ap[rows, 0:nq], HT.ap[rows, i2, pq0:pq0 + nq], reads=qtiles,
                           writes=[qz])
                        qz_cur[oc] = qz
                    if use_qz:
                        qz = qz_cur[oc]
                        MM(ps_.ap[:, 0:nqs], KT.ap[:, i2, t * 128:(t + 1) * 128], qz.ap[:, qo:qo + nqs],
                           True, ei0 is None, ktu(i2, t * 128, 128) + [qz], [ps_])
                    if ei0 is not None:
                        MM(ps_.ap[:, 0:nqs], IDENT, tb.ap[:, ei0:ei0 + nqs // 64, :].rearrange("p a b -> p (a b)"),
                           False, True, [tb, CONST], [ps_])
                    OP("act", "activation", pt.ap[:, 0:nqs], ps_.ap[:, 0:nqs], AF.Exp, reads=[ps_], writes=[pt])
                    staged.append(pt)
                kk = k - SK
                if kk >= 0:
                    h, qt0, nq, t, ei0, first, lastc, oc, qo, nqs = items[kk]
                    i2 = h // 2
                    rows = slice(64 * (h % 2), 64 * (h % 2) + 64)
                    pt_ = staged[kk]
                    po = PS[3 + 2 * (oc % 2)]
                    pd = PS[4 + 2 * (oc % 2)]
                    MM(po.ap[:, qo:qo + nqs], VT[t].ap[:, i2 * 128:(i2 + 1) * 128], pt_.ap[:, 0:nqs], first, lastc,
                       [VT[t], pt_], [po])
                    MM(pd.ap[:, qo:qo + nqs], ONES, pt_.ap[:, 0:nqs], first, lastc, [pt_, CONST], [pd])
                    if lastc:
                        pq0 = pcol(qt0)
                        rd = RD[oc % 2]
                        OP("dve", RECIP, rd.ap[rows, 0:nq], pd.ap[rows, 0:nq], reads=[pd], writes=[rd])
                        OP("dve", "tensor_tensor", HT.ap[rows, i2, pq0:pq0 + nq], po.ap[rows, 0:nq],
                           rd.ap[rows, 0:nq], ALU.mult, reads=[po, rd], writes=ht_tiles(qt0, nq))

            blocks = TB512[:4] if last else TB512
            genw = "L%d.wo" % l
            out_proj_overlap(l, genw, lambda m: wo[i, m], blocks,
                             lambda kc, t0, n: HT.ap[:, kc, pcol(t0):pcol(t0) + n],
                             lambda kc, t0, n: ht_tiles(t0, n), last)

        for l in range(n_layers):
            last = (l == DEPTH - 1)
            if l == 0 and os.environ.get('K_MODS0') != 'old':
                hook0, fin0 = mods0()
                ps_default[0] = (0, 1, 2, 3, 4, 5, 6)
                modulate(l, 1, TB512, False, hook=hook0)
                fin0()
                ps_default[0] = (0, 1, 2, 3, 4, 5, 6, 7)
            else:
                if l == 0:
                    mods(l)
                modulate(l, 1, TB512 if (not last or l % 2 == 1) else TB512[:4], False)
            if l % 2 == 0:
                even_mixer(l, last)
            else:
                odd_mixer(l, last)
            nm = (l + 1 < n_layers) and os.environ.get('K_NOMODS') != '1'
            ffn(l, last, nm)
            if (l + 1 < n_layers) and (not nm or os.environ.get('K_BOTH') == '1'):
                mods(l + 1)

        outs = []
        for c in range(8):
            outs.append(DMA("sp", yout[c * 128:(c + 1) * 128, :], XTt[:, c, 0:n_out], reads=XTT[c]))
        S.wait_ops("sp", outs)
        S.emit(ctx)
    return nc


def _consts():
    ident = np.eye(128, dtype=np.float32)
    ones = np.ones((128, 128), np.float32)
    blk = np.zeros((128, 128), np.float32)
    blk[:64, :64] = 1.0
    blk[64:, 64:] = 1.0
    cst = np.stack([ident, ones, blk], axis=1).reshape(128, 384)
    ll = np.arange(L, dtype=np.float64)
    ang = 2.0 * np.pi * ((ll[:, None] * ll[None, :]) % L) / L
    dftL = np.stack([np.cos(ang), -np.sin(ang)]).astype(np.float32)
    l2 = np.arange(256, dtype=np.float64)
    ang2 = 2.0 * np.pi * ((l2[:, None] * l2[None, :]) % 256) / 256
    dft256 = np.stack([np.cos(ang2), -np.sin(ang2)]).astype(np.float32)
    cc = np.arange(128, dtype=np.float64)
    angc = 2.0 * np.pi * ((cc[:, None] * cc[None, :]) % 128) / 128
    dftC = np.zeros((128, 2, 256), np.float64)
    for v, n in enumerate((L, 256)):
        s = 1.0 / np.sqrt(n * 128.0)
        dftC[:, v, :128] = np.cos(angc) * s
        dftC[:, v, 128:] = np.sin(angc) * s
    return cst, dftL, dft256, dftC.reshape(128, 512).astype(np.float32)


def _rpb_tables(rpb):
    NEG = np.float32(-1e30)
    kc = np.arange(64)[:, None]
    qc = np.arange(64)[None, :]
    dc = np.clip(kc - qc + 15, 0, 30)
    cs = np.clip(qc - 8, 0, 48)
    valid = (kc >= cs) & (kc < cs + 16)
    out = np.full((2, 16, 128, NE, 64), NEG, np.float32)
    for e in range(15):
        B = np.where(valid[None, None], rpb[:, :, 14 - e][:, :, dc], NEG)
        for krl in range(2):
            ee = e + krl
            if 4 <= e <= 11 and 0 <= ee + 4 < NE_WIN:
                out[:, :, krl * 64:(krl + 1) * 64, ee + 4, :] = B
            if 0 <= ee < 16:
                out[:, :, krl * 64:(krl + 1) * 64, NE_WIN + ee, :] = B
    return out.reshape(2, 16, 128, NE * 64)


def _prep(inputs):
    f = lambda a: np.ascontiguousarray(np.asarray(a, dtype=np.float32))
    x = f(inputs["x"]); c = f(inputs["c"]); cx = f(inputs["ctx"]); c_ctx = f(inputs["c_ctx"])
    cst, dftL, dft256, dftC = _consts()
    sh = {}
    sh["ada_w"] = f(inputs["ada_w"])
    sh["ada_b_r"] = f(f(inputs["ada_b"]).reshape(4, 48, 128).transpose(2, 0, 1).reshape(128, 192))
    ng = np.stack([f(inputs["norm1_g"]), f(inputs["norm2_g"])])
    sh["ng"] = f(ng.reshape(2, 4, 8, 128).transpose(3, 0, 1, 2).reshape(128, 64))
    cw = f(inputs["ffn_conv_w"]); cb = f(inputs["ffn_conv_b"])
    cp = np.concatenate([cw, cb[:, None, :]], axis=1)
    sh["convp"] = f(cp.reshape(4, 4, NFF, 128).transpose(3, 0, 1, 2).reshape(128, 4 * 4 * NFF))
    wu = f(inputs["ffn_w_up"]).reshape(4, 8, 128, 2, NFF, 128)
    sh["wup_r"] = f(wu.transpose(0, 4, 2, 1, 3, 5).reshape(4, NFF, 128, 8 * 256))
    wd = f(inputs["ffn_w_down"]).reshape(4, NFF, 128, 8, 128)
    sh["wdn_r"] = f(wd.transpose(0, 3, 2, 1, 4).reshape(4, 8, 128, NFF * 128))
    wi = f(inputs["even_w_in"])
    wfu = wi[:, :, :1024].reshape(2, 8, 128, 8, 128)
    sh["win_r"] = f(wfu.transpose(0, 3, 2, 1, 4).reshape(2, 8, 128, 1024))
    wvv = wi[:, :, 1024:].reshape(2, 8, 128, 512)
    sh["winv_r"] = f(wvv.transpose(0, 2, 1, 3).reshape(2, 128, 4096))
    sh["wsT"] = f(f(inputs["even_w_s"]).transpose(0, 3, 1, 2).reshape(2, 128, 512))
    sh["bs"] = f(f(inputs["even_b_s"]).reshape(2, 1, 512))
    wo_ = f(inputs["even_w_out"]).reshape(2, 8, 128, 8, 128)
    sh["wout_r"] = f(wo_.transpose(0, 3, 2, 1, 4).reshape(2, 8, 128, 1024))
    wq = f(inputs["odd_w_qkv"])
    wqk_ = wq[:, :, :2048].reshape(2, 8, 128, 16, 128)
    sh["wqk_r"] = f(wqk_.transpose(0, 3, 2, 1, 4).reshape(2, 16, 128, 1024))
    wv_ = wq[:, :, 2048:].reshape(2, 8, 128, 2, 512)
    sh["wv_r"] = f(wv_.transpose(0, 3, 2, 1, 4).reshape(2, 2, 128, 4096))
    woo = f(inputs["odd_w_o"]).reshape(2, 8, 128, 8, 128)
    sh["wo_r"] = f(woo.transpose(0, 3, 2, 1, 4).reshape(2, 8, 128, 1024))
    qg = f(inputs["odd_q_g"]); kg = f(inputs["odd_k_g"])
    qk = np.stack([qg, kg], axis=1)
    qk = np.concatenate([qk, qk], axis=2)
    sh["qkg"] = f(qk.transpose(2, 0, 1).reshape(128, 4))
    sh["tab"] = _rpb_tables(f(inputs["odd_rpb"]))
    sh["dftL"] = dftL
    sh["dftC"] = dftC
    sh["dft256"] = dft256
    sh["cst"] = cst
    in_maps = []
    for b in range(8):
        m = dict(sh)
        m["xin"] = f(np.concatenate([x[b].T, cx[b].T], axis=1))
        sv = np.stack([c[b], c_ctx], axis=1)
        m["svec"] = f(sv.reshape(8, 128, 2).transpose(1, 0, 2).reshape(128, 16))
        in_maps.append(m)
    return in_maps


_NC_CACHE = {}


def kernel(**inputs):
    in_maps = _prep(inputs)
    if "nc" not in _NC_CACHE:
        _NC_CACHE["nc"] = build()
    nc = _NC_CACHE["nc"]
    res = run_bass_kernel_spmd(nc, in_maps, core_ids=list(range(8)))
    out = np.stack([np.ascontiguousarray(r["yout"].T) for r in res.results], axis=0)
    return out.astype(np.float32)
```

```python
from contextlib import ExitStack
import os
import numpy as np
import concourse.bass as bass
import concourse.mybir as mybir
from concourse.bass_utils import run_bass_kernel_spmd

F32 = mybir.dt.float32
BF16 = mybir.dt.bfloat16
ALU = mybir.AluOpType
AF = mybir.ActivationFunctionType
AX = mybir.AxisListType

ENGS = ("pe", "act", "dve", "pool", "sp")
N_DMA_SEMS = 40
GRAN = 2048


class Buf:
    __slots__ = ("name", "writes", "reads")

    def __init__(self, name=""):
        self.name = name
        self.writes = []
        self.reads = []


class Region:
    __slots__ = ("gen", "cur", "prev")

    def __init__(self):
        self.gen = None
        self.cur = {}
        self.prev = {}


class Tile:
    __slots__ = ("ap", "buf", "grans", "gen", "excl")

    def __init__(self, ap, name="", grans=(), gen=None):
        self.excl = False
        self.ap = ap
        self.buf = Buf(name)
        self.grans = grans
        self.gen = gen


class Arena:
    def __init__(self, tensor, nbytes):
        self.t = tensor
        self.nbytes = nbytes
        self.regs = [Region() for _ in range((nbytes + GRAN - 1) // GRAN)]
        self.whole = Region()
        self.use_whole = False

    def tile(self, gen, off, shape, dtype, name=""):
        esz = 4 if dtype == F32 else 2
        n = 1
        for s in shape[1:]:
            n *= s
        nb = n * esz
        assert off % 4 == 0 and off + nb <= self.nbytes, (name, off, nb, self.nbytes)
        ap = self.t[:, off // 2:(off + nb) // 2]
        if dtype == F32:
            ap = ap.bitcast(F32)
        if len(shape) == 3:
            ap = ap.rearrange("p (a b) -> p a b", a=shape[1])
        elif len(shape) == 4:
            ap = ap.rearrange("p (a b c) -> p a b c", a=shape[1], b=shape[2])
        if shape[0] != 128:
            ap = ap[0:shape[0]]
        grans = tuple(self.regs[off // GRAN:(off + nb - 1) // GRAN + 1])
        if self.use_whole:
            grans = grans + (self.whole,)
        return Tile(ap, name, grans, gen)


class Op:
    __slots__ = ("dom", "idx", "fn", "vc", "waits", "signal", "stream", "is_dma")


class Sched:
    def __init__(self, nc):
        self.nc = nc
        self.streams = {e: [] for e in ENGS}
        self.vc = {e: {} for e in ENGS}
        self.count = {}
        self.dma_last = [None] * N_DMA_SEMS
        self.dma_rr = {"pool": 0, "sp": 0, "act": 0}

    def _deps(self, reads, writes):
        deps = []
        for t in reads:
            for o in t.buf.writes:
                deps.append((o, "raw"))
            if t.excl:
                for o in t.buf.reads[-4:]:
                    deps.append((o, "rar"))
        for t in writes:
            b = t.buf
            for o in b.writes:
                deps.append((o, "waw"))
            for o in b.reads:
                deps.append((o, "war"))
        for t in list(reads) + list(writes):
            for r in t.grans:
                if r.gen != t.gen:
                    r.prev = r.cur
                    r.cur = {}
                    r.gen = t.gen
                for o in r.prev.values():
                    deps.append((o, "raw"))
        return deps

    def _book(self, op, reads, writes):
        for t in reads:
            t.buf.reads.append(op)
            if len(t.buf.reads) > 96:
                t.buf.reads = t.buf.reads[-96:]
        for t in writes:
            b = t.buf
            if b.reads:
                b.writes = [op]
                b.reads = []
            else:
                b.writes.append(op)
                if len(b.writes) > 64:
                    b.writes = b.writes[-64:]
        for t in list(reads) + list(writes):
            for r in t.grans:
                r.cur[op.dom] = op

    def _resolve(self, stream, deps):
        base = self.vc[stream]
        waits = {}
        for d, kind in deps:
            if d.dom == stream:
                if stream == "pe" or stream == "sp":
                    continue
                if kind == "rar":
                    continue
            if base.get(d.dom, 0) >= d.idx:
                continue
            d.signal = True
            waits[d.dom] = max(waits.get(d.dom, 0), d.idx)
            for k, v in d.vc.items():
                if base.get(k, 0) < v:
                    base[k] = v
            base[d.dom] = max(base.get(d.dom, 0), d.idx)
        return waits

    def _mk(self, dom, stream, fn, deps, is_dma):
        o = Op()
        o.dom = dom
        o.stream = stream
        o.is_dma = is_dma
        o.idx = self.count.get(dom, 0) + 1
        self.count[dom] = o.idx
        o.fn = fn
        o.signal = is_dma
        o.waits = self._resolve(stream, deps)
        o.vc = dict(self.vc[stream])
        self.streams[stream].append(o)
        return o

    def op(self, eng, fn, reads=(), writes=()):
        deps = self._deps(reads, writes)
        o = self._mk(eng, eng, fn, deps, False)
        self._book(o, reads, writes)
        return o

    def dma(self, queue, fn, reads=(), writes=()):
        deps = self._deps(reads, writes)
        lo, n = (0, N_DMA_SEMS - 8) if queue == "pool" else (N_DMA_SEMS - 8, 8)
        k = lo + self.dma_rr[queue] % n
        self.dma_rr[queue] += 1
        prev = self.dma_last[k]
        if prev is not None:
            deps.append((prev, "raw"))
        o = self._mk("d%d" % k, queue, fn, deps, True)
        self.dma_last[k] = o
        self._book(o, reads, writes)
        return o

    def wait_ops(self, eng, ops):
        return self._mk(eng, eng, None, [(o, "raw") for o in ops], False)

    def emit(self, ctx):
        nc = self.nc
        sems = {}
        for e in ENGS:
            sems[e] = ctx.enter_context(nc.semaphore("s_" + e))
        for k in range(N_DMA_SEMS):
            sems["d%d" % k] = ctx.enter_context(nc.semaphore("s_d%d" % k))
        ticket = {}
        for e in ENGS:
            t = 0
            for o in self.streams[e]:
                if o.is_dma:
                    ticket[(o.dom, o.idx)] = 16 * o.idx
                elif o.signal:
                    t += 1
                    ticket[(o.dom, o.idx)] = t
        handles = {"pe": nc.tensor, "act": nc.scalar, "dve": nc.vector,
                   "pool": nc.gpsimd, "sp": nc.sync}

        def run(e):
            h = handles[e]
            for o in self.streams[e]:
                for dom, idx in o.waits.items():
                    h.wait_ge(sems[dom], ticket[(dom, idx)])
                if o.fn is None:
                    continue
                ins = o.fn(h)
                if o.is_dma:
                    ins.then_inc(sems[o.dom], 16)
                elif o.signal:
                    ins.then_inc(sems[o.dom], 1)

        with nc.Block() as block:
            @block.tensor
            def _(eng):
                run("pe")

            @block.scalar
            def _(eng):
                run("act")

            @block.vector
            def _(eng):
                run("dve")

            @block.gpsimd
            def _(eng):
                run("pool")

            @block.sync
            def _(eng):
                run("sp")


D = 1024
L = 2048
NCX = 256
NT = L + NCX
DFF = 2816
NFF = 22
LAT0 = 1
CTX0 = 2051
NCOL = 2308
EPS = 1e-6
NE_WIN = 23
NE = 39
DEPTH = 4


def pcol(t):
    return t + LAT0 if t < L else (t - L) + CTX0


TB512 = [(0, 512, 0), (512, 512, 0), (1024, 512, 0), (1536, 512, 0), (2048, 256, 1)]
TB256 = [(256 * k, 256, 0) for k in range(8)] + [(2048, 256, 1)]
SEGS = [(0, 406, 0), (406, 406, 0), (812, 406, 0), (1218, 415, 0), (1633, 415, 0), (2048, 256, 1)]


def build(n_layers=DEPTH, dbg=False):
    nc = bass.Bass("TRN2", target_bir_lowering=False)

    def din(name, shape):
        return nc.dram_tensor(name, list(shape), F32, kind="ExternalInput").ap()

    xin = din("xin", [D, NT])
    svec = din("svec", [128, 16])
    ada_w = din("ada_w", [4, D, 6 * D])
    ada_b = din("ada_b_r", [128, 4 * 48])
    ngd = din("ng", [128, 64])
    convd = din("convp", [128, 4 * 4 * NFF])
    wup = din("wup_r", [4, NFF, 128, 8 * 256])
    wdn = din("wdn_r", [4, 8, 128, NFF * 128])
    win = din("win_r", [2, 8, 128, 8 * 128])
    winv = din("winv_r", [2, 128, 8 * 512])
    wsT = din("wsT", [2, 128, 4 * 128])
    bsd = din("bs", [2, 1, 512])
    wout = din("wout_r", [2, 8, 128, 8 * 128])
    wqk = din("wqk_r", [2, 16, 128, 8 * 128])
    wv = din("wv_r", [2, 2, 128, 8 * 512])
    wo = din("wo_r", [2, 8, 128, 8 * 128])
    qkgd = din("qkg", [128, 4])
    tabd = din("tab", [2, 16, 128, NE * 64])
    dftL = din("dftL", [2, L, L])
    dftC = din("dftC", [128, 2 * 256])
    dft256 = din("dft256", [2, 256, 256])
    cstd = din("cst", [128, 3 * 128])
    n_out = NT if dbg else L
    yout = nc.dram_tensor("yout", [D, n_out], F32, kind="ExternalOutput").ap()

    with ExitStack() as ctx:
        def sb(name, shape, dt):
            return ctx.enter_context(nc.sbuf_tensor(name, shape, dt))

        XTt = sb("XT", [128, 8, NT], F32)
        R1t = sb("R1", [128, 8 * NCOL], BF16)
        R2t = sb("R2", [128, 18432], BF16)
        R3t = sb("R3", [128, 18432], BF16)
        R4t = sb("R4", [128, 9216], BF16)
        CONSTt = sb("CONST", [128, 3, 128], BF16)
        NGt = sb("NG", [128, 2, 4, 8], F32)
        CONVt = sb("CONVP", [128, 4, 4, NFF], F32)
        QKGt = sb("QKG", [128, 2, 2], F32)
        GQSt = sb("GQS", [128, 2, 2], F32)
        ADABt = sb("ADAB", [128, 4, 48], F32)
        SVt = sb("SV", [128, 8, 2], F32)
        STt = sb("ST", [128, 8, 2], BF16)
        MODt = [sb("MOD%d" % k, [128, 48, 2], F32) for k in range(2)]
        A12t = [sb("A12_%d" % k, [128, 2, 8, 2], F32) for k in range(2)]
        QZt = sb("QZ", [128, 4, 512], BF16)
        PSt = [ctx.enter_context(nc.psum_tensor("ps%d" % k, [128, 512], F32)) for k in range(8)]

        R1 = Arena(R1t, 8 * NCOL * 2)
        R2 = Arena(R2t, 36864)
        R3 = Arena(R3t, 36864)
        R4 = Arena(R4t, 18432)

        S = Sched(nc)
        PS = [Tile(PSt[k][:], "ps%d" % k) for k in range(8)]
        for t_ in PS:
            t_.excl = True
        psrot = [0]

        ps_default = [(0, 1, 2, 3, 4, 5, 6, 7)]

        def ps_next(pool=None):
            if pool is None:
                pool = ps_default[0]
            k = pool[psrot[0] % len(pool)]
            psrot[0] += 1
            return PS[k]

        XTT = [[Tile(XTt[:, c, u * 128:(u + 1) * 128], "xt%d_%d" % (c, u)) for u in range(18)] for c in range(8)]

        def xt_tiles(c, t0, n):
            return [XTT[c][u] for u in range(t0 // 128, (t0 + n - 1) // 128 + 1)]

        def xt_all(t0, n):
            r = []
            for c in range(8):
                r += xt_tiles(c, t0, n)
            return r

        R1.use_whole = True
        HT = R1.tile("r1", 0, [128, 8, NCOL], BF16, "HT")
        HTU = [Tile(None, "ht%d" % u, (R1.whole,), "r1") for u in range(18)]
        HTPAD = Tile(None, "htpad", (R1.whole,), "r1")

        def ht_tiles(t0, n):
            return [HTU[u] for u in range(t0 // 128, (t0 + n - 1) // 128 + 1)]

        def ht_halo(t0, n, col):
            s0, s1 = (0, L) if col == 0 else (L, NT)
            lo = max(t0 - 1, s0)
            hi = min(t0 + n + 1, s1)
            return ht_tiles(lo, hi - lo) + [HTPAD]

        QZ = [Tile(QZt[:, k, :], "qz%d" % k) for k in range(4)]
        CONST = Tile(CONSTt, "const")
        IDENT = CONSTt[:, 0, :]
        ONES = CONSTt[:, 1, :]
        BLK = CONSTt[:, 2, :]
        NG = Tile(NGt, "ng")
        CONV = Tile(CONVt, "conv")
        QKG = Tile(QKGt, "qkg")
        GQS = Tile(GQSt, "gqs")
        ADAB = Tile(ADABt, "adab")
        SV = Tile(SVt, "sv")
        ST = Tile(STt, "st")
        MOD = [Tile(MODt[k], "mod%d" % k) for k in range(2)]
        A12 = [Tile(A12t[k], "a12_%d" % k) for k in range(2)]

        RECIP = "reciprocal"

        def OP(eng, method, *args, reads=(), writes=(), **kw):
            return S.op(eng, lambda e: getattr(e, method)(*args, **kw), reads, writes)

        def MM(out, lhsT, rhs, start, stop, reads, writes):
            return S.op("pe", lambda e: e.matmul(out, lhsT, rhs, start=start, stop=stop), reads, writes)

        def DMA(queue, out, in_, reads=(), writes=()):
            return S.dma(queue, lambda e: e.dma_start(out=out, in_=in_), reads, writes)

        for c in range(8):
            DMA("sp", XTt[:, c, :], xin[c * 128:(c + 1) * 128, :], writes=XTT[c])
        DMA("pool", CONSTt[:].rearrange("p a b -> p (a b)"), cstd, writes=[CONST])
        DMA("sp", NGt[:].rearrange("p a b c -> p (a b c)"), ngd, writes=[NG])
        DMA("sp", CONVt[:].rearrange("p a b c -> p (a b c)"), convd, writes=[CONV])
        DMA("sp", QKGt[:].rearrange("p a b -> p (a b)"), qkgd, writes=[QKG])
        DMA("sp", ADABt[:].rearrange("p a b -> p (a b)"), ada_b, writes=[ADAB])
        DMA("sp", SVt[:].rearrange("p a b -> p (a b)"), svec, writes=[SV])
        OP("act", "activation", STt[:], SVt[:], AF.Silu, reads=[SV], writes=[ST])
        OP("dve", "tensor_scalar", GQSt[:], QKGt[:], 0.125, None, ALU.mult, reads=[QKG], writes=[GQS])

        def mods(l):
            gen = "L%d.mods" % l
            md = MOD[l % 2]
            mdt = MODt[l % 2]
            a12 = A12[l % 2]
            a12t = A12t[l % 2]
            BW = int(os.environ.get("K_BW", "512"))
            nb = 6144 // BW
            ADA = [R2.tile(gen, k * 8192, [128, 8, BW], BF16, "ada%d" % k) for k in range(2)]
            adv = ada_w[l].rearrange("(c p) n -> p c n", p=128)
            psm = PS[7]
            for blk in range(nb):
                a = ADA[blk % 2]
                DMA("pool", a.ap, adv[:, :, blk * BW:(blk + 1) * BW], writes=[a])
                for jj in range(BW // 128):
                    j = blk * (BW // 128) + jj
                    for kc in range(8):
                        MM(psm.ap[:, 2 * j:2 * j + 2], a.ap[:, kc, jj * 128:(jj + 1) * 128], STt[:, kc, :],
                           kc == 0, kc == 7, [a, ST], [psm])
            mods_final(l)

        def mods_final_part(l, part):
            md = MOD[l % 2]
            mdt = MODt[l % 2]
            a12 = A12[l % 2]
            a12t = A12t[l % 2]
            psm = PS[7]
            j0, j1 = (0, 16) if part == 0 else (16, 48)
            OP("dve", "tensor_tensor", mdt[:, j0:j1, :], psm.ap[:, 2 * j0:2 * j1].rearrange("p (a b) -> p a b", b=2),
               ADABt[:, l, j0:j1].unsqueeze(2).to_broadcast([128, j1 - j0, 2]), ALU.add,
               reads=[psm, ADAB], writes=[md])
            w, jj0 = (0, 8) if part == 0 else (1, 32)
            OP("dve", "tensor_scalar", a12t[:, w], mdt[:, jj0:jj0 + 8, :], 1.0, None, ALU.add,
               reads=[md], writes=[a12])
            OP("dve", "tensor_tensor", a12t[:, w], a12t[:, w],
               NGt[:, w, l, :].unsqueeze(2).to_broadcast([128, 8, 2]), ALU.mult,
               reads=[a12, NG], writes=[a12])

        def mods0():
            gen = "L0.mods"
            ADA = [R2.tile(gen, k * 8192, [128, 8, 512], BF16, "ada%d" % k) for k in range(4)] + \
                  [R4.tile(gen, k * 8192, [128, 8, 512], BF16, "adb%d" % k) for k in range(2)]
            adv = ada_w[0].rearrange("(c p) n -> p c n", p=128)
            psm = PS[7]
            st = [4]

            def dma(blk):
                a = ADA[blk % 6]
                DMA("pool", a.ap, adv[:, :, blk * 512:(blk + 1) * 512], reads=(XTT[7][:1] if blk >= 4 else []),
                    writes=[a])

            def mm(blk):
                a = ADA[blk % 6]
                for jj in range(4):
                    j = blk * 4 + jj
                    for kc in range(8):
                        MM(psm.ap[:, 2 * j:2 * j + 2], a.ap[:, kc, jj * 128:(jj + 1) * 128], STt[:, kc, :],
                           kc == 0, kc == 7, [a, ST], [psm])
                if blk + 6 < 12:
                    dma(blk + 6)

            for blk in range(6):
                dma(blk)
            for blk in range(4):
                mm(blk)
            mods_final_part(0, 0)

            def hook(bi):
                for _ in range(2):
                    if st[0] < 12:
                        mm(st[0])
                        st[0] += 1

            def finish():
                while st[0] < 12:
                    mm(st[0])
                    st[0] += 1
                mods_final_part(0, 1)
            return hook, finish

        def mods_final(l):
            md = MOD[l % 2]
            mdt = MODt[l % 2]
            a12 = A12[l % 2]
            a12t = A12t[l % 2]
            psm = PS[7]
            OP("dve", "tensor_tensor", mdt[:], psm.ap[:, 0:96].rearrange("p (a b) -> p a b", b=2),
               ADABt[:, l, :].unsqueeze(2).to_broadcast([128, 48, 2]), ALU.add,
               reads=[psm, ADAB], writes=[md])
            for w, j0 in ((0, 8), (1, 32)):
                OP("dve", "tensor_scalar", a12t[:, w], mdt[:, j0:j0 + 8, :], 1.0, None, ALU.add,
                   reads=[md], writes=[a12])
                OP("dve", "tensor_tensor", a12t[:, w], a12t[:, w],
                   NGt[:, w, l, :].unsqueeze(2).to_broadcast([128, 8, 2]), ALU.mult,
                   reads=[a12, NG], writes=[a12])

        def modulate(l, which, blocks, zero_pads, external=False, hook=None):
            gen = "L%d.m%d" % (l, which)
            md = MOD[l % 2]
            mdt = MODt[l % 2]
            a12 = A12[l % 2]
            a12t = A12t[l % 2]
            sh0 = 0 if which == 1 else 24
            SQ = [R3.tile(gen, k * 4096, [128, 4, 512], BF16, "sq%d" % k) for k in range(2)]
            RS = [R3.tile(gen, 8192 + k * 2048, [128, 512], F32, "rs%d" % k) for k in range(2)]
            RR = [R3.tile(gen, 12288 + k * 2048, [128, 512], F32, "rr%d" % k) for k in range(3)]
            T1 = [R3.tile(gen, 18432 + k * 2048, [128, 512], F32, "t1_%d" % k) for k in range(4)]
            if zero_pads:
                for c0 in (0, 2049, 2050, 2307):
                    OP("pool", "memset", HT.ap[:, :, c0:c0 + 1], 0.0, writes=[HTPAD])
            cnt = [0]

            def stage_a(bi):
                t0, n, col = blocks[bi]
                sq = SQ[0]
                sq2 = SQ[1]
                rs = RS[bi % 2]
                rr = RR[bi % 3]
                xts = xt_all(t0, n)
                OP("act", "activation", sq.ap[:, 0:4, 0:n], XTt[:, 0:4, t0:t0 + n], AF.Square, reads=xts, writes=[sq])
                OP("dve", "tensor_tensor", sq2.ap[:, :, 0:n], XTt[:, 4:8, t0:t0 + n], XTt[:, 4:8, t0:t0 + n], ALU.mult,
                   reads=xts, writes=[sq2])
                pr = ps_next()
                for c in range(8):
                    src = sq.ap[:, c, 0:n] if c < 4 else sq2.ap[:, c - 4, 0:n]
                    MM(pr.ap[:, 0:n], ONES, src, c == 0, c == 7, [sq, sq2, CONST], [pr])
                OP("act", "activation", rs.ap[:, 0:n], pr.ap[:, 0:n], AF.Ln, bias=EPS, scale=1.0 / D,
                   reads=[pr], writes=[rs])
                OP("act", "activation", rr.ap[:, 0:n], rs.ap[:, 0:n], AF.Exp, scale=-0.5, reads=[rs], writes=[rr])

            def stage_b(bi):
                t0, n, col = blocks[bi]
                rr = RR[bi % 3]
                pc0 = pcol(t0)
                hts = ht_tiles(t0, n)
                for c in range(8):
                    t1 = T1[cnt[0] % 4]
                    cnt[0] += 1
                    OP("dve", "tensor_tensor", t1.ap[:, 0:n], XTt[:, c, t0:t0 + n],
                       rr.ap[:, 0:n], ALU.mult, reads=xt_tiles(c, t0, n) + [rr], writes=[t1])
                    if c < 6:
                        OP("act", "activation", HT.ap[:, c, pc0:pc0 + n], t1.ap[:, 0:n], AF.Identity,
                           bias=mdt[:, sh0 + c, col:col + 1], scale=a12t[:, which - 1, c, col:col + 1],
                           reads=[t1, md, a12], writes=hts)
                    else:
                        OP("dve", "tensor_scalar", HT.ap[:, c, pc0:pc0 + n], t1.ap[:, 0:n],
                           a12t[:, which - 1, c, col:col + 1], mdt[:, sh0 + c, col:col + 1], ALU.mult, ALU.add,
                           reads=[t1, md, a12], writes=hts)

            nb_ = len(blocks)
            if external:
                return stage_a, stage_b
            for bi in range(nb_ + 1):
                if bi < nb_:
                    stage_a(bi)
                if hook:
                    hook(bi)
                if bi >= 1:
                    stage_b(bi - 1)

        def ffn(l, last, next_mods):
            gen = "L%d.ffn" % l
            md = MOD[l % 2]
            mdt = MODt[l % 2]
            segs = SEGS[:5] if last else SEGS
            passes = [segs[0:3], segs[3:]]
            HW = 1218
            NWU, NT1, NWD, NAD = 3, 2, 2, 2
            WU = [R3.tile(gen, k * 4096, [128, 8, 256], BF16, "wu%d" % k) for k in range(NWU)]
            T1 = [R3.tile(gen, 12288 + k * 2048, [128, 512], F32, "ft1_%d" % k) for k in range(NT1)]
            T2 = [R3.tile(gen, 16384 + k * 2048, [128, 512], F32, "ft2_%d" % k) for k in range(NT1)]
            ADAI = [R3.tile(gen, 20480 + k * 2048, [128, 8, 128], BF16, "adai%d" % k) for k in range(NAD)]
            WD = [R3.tile(gen, 24576 + k * 5632, [128, NFF, 128], BF16, "wd%d" % k) for k in range(NWD)]
            H = [(R2.tile(gen, j * 2 * HW, [128, HW], BF16, "h%d" % j) if j < 15 else
                  R4.tile(gen, (j - 15) * 2 * HW, [128, HW], BF16, "h%d" % j)) for j in range(NFF)]
            if next_mods:
                advn = ada_w[l + 1].rearrange("(c p) n -> p c n", p=128)
            psm = PS[7]
            P7 = (0, 1, 2, 3, 4, 5, 6)
            mstate = [0]

            def mods_step():
                if not next_mods:
                    return
                g = mstate[0]
                mstate[0] += 1
                if g < 48:
                    a = ADAI[g % NAD]
                    DMA("pool", a.ap, advn[:, :, g * 128:(g + 1) * 128], writes=[a])
                jm = g - (NAD - 1)
                if 0 <= jm < 48:
                    a = ADAI[jm % NAD]
                    for kc in range(8):
                        MM(psm.ap[:, 2 * jm:2 * jm + 2], a.ap[:, kc, :], STt[:, kc, :], kc == 0, kc == 7,
                           [a, ST], [psm])

            ucnt = 0
            for pi, pss in enumerate(passes):
                NPRE = NWU - 1
                for j in range(NFF + NPRE):
                    jj = j
                    if jj < NFF:
                        wt = WU[jj % NWU]
                        DMA("pool", wt.ap.rearrange("p a b -> p (a b)"), wup[l, jj], writes=[wt])
                    j = j - NPRE
                    if j < 0:
                        continue
                    wt = WU[j % NWU]
                    so = 0
                    mods_step()
                    for (t0, n, col) in pss:
                        pc0 = pcol(t0)
                        pg = ps_next(P7)
                        pv = ps_next(P7)
                        hts = ht_halo(t0, n, col)
                        for kc in range(8):
                            MM(pg.ap[:, 0:n + 2], wt.ap[:, kc, 0:128], HT.ap[:, kc, pc0 - 1:pc0 + n + 1],
                               kc == 0, kc == 7, [wt] + hts, [pg])
                        for kc in range(8):
                            MM(pv.ap[:, 0:n], wt.ap[:, kc, 128:256], HT.ap[:, kc, pc0:pc0 + n],
                               kc == 0, kc == 7, [wt] + hts, [pv])
                        t1 = T1[ucnt % NT1]
                        t2 = T2[ucnt % NT1]
                        ucnt += 1
                        OP("act", "activation", t1.ap[:, 0:n], pg.ap[:, 1:n + 1], AF.Identity,
                           bias=CONVt[:, l, 3, j:j + 1], scale=CONVt[:, l, 1, j:j + 1],
                           reads=[pg, CONV], writes=[t1])
                        OP("dve", "scalar_tensor_tensor", t1.ap[:, 0:n], pg.ap[:, 0:n], CONVt[:, l, 0, j:j + 1],
                           t1.ap[:, 0:n], ALU.mult, ALU.add, reads=[pg, CONV, t1], writes=[t1])
                        OP("dve", "scalar_tensor_tensor", t1.ap[:, 0:n], pg.ap[:, 2:n + 2], CONVt[:, l, 2, j:j + 1],
                           t1.ap[:, 0:n], ALU.mult, ALU.add, reads=[pg, CONV, t1], writes=[t1])
                        OP("act", "activation", t2.ap[:, 0:n], t1.ap[:, 0:n], AF.Silu, reads=[t1], writes=[t2])
                        OP("dve", "tensor_tensor", H[j].ap[:, so:so + n], t2.ap[:, 0:n], pv.ap[:, 0:n], ALU.mult,
                           reads=[t2, pv], writes=[H[j]])
                        so += n
                MPRE = NWD - 1
                for m in range(8 + MPRE):
                    mm_ = m
                    if mm_ < 8:
                        wt = WD[mm_ % NWD]
                        DMA("pool", wt.ap.rearrange("p a b -> p (a b)"), wdn[l, mm_], writes=[wt])
                    m = m - MPRE
                    if m < 0:
                        continue
                    wt = WD[m % NWD]
                    so = 0
                    mods_step()
                    for (t0, n, col) in pss:
                        po = ps_next(P7)
                        for kc in range(NFF):
                            MM(po.ap[:, 0:n], wt.ap[:, kc, :], H[kc].ap[:, so:so + n], kc == 0, kc == NFF - 1,
                               [wt, H[kc]], [po])
                        xts = xt_tiles(m, t0, n)
                        OP("dve", "scalar_tensor_tensor", XTt[:, m, t0:t0 + n], po.ap[:, 0:n],
                           mdt[:, 40 + m, col:col + 1], XTt[:, m, t0:t0 + n], ALU.mult, ALU.add,
                           reads=[po, md] + xts, writes=xts)
                        so += n
            if next_mods:
                while mstate[0] < 48 + NAD:
                    mods_step()
                mods_final(l + 1)

        def mk_w(gen, woff=0):
            return [R4.tile(gen, woff + k * 2048, [128, 8, 128], BF16, "w%d" % k) for k in range(3)]

        def proj_stream(W, wsrc_fn, nj, blocks, rhs_fn, rhs_tiles_fn, evac_fn, flush_fn=None, post_fn=None,
                        blocks_fn=None):
            NW = len(W)
            for j in range(nj + NW - 1):
                if j < nj:
                    wt = W[j % NW]
                    DMA("pool", wt.ap.rearrange("p a b -> p (a b)"), wsrc_fn(j), writes=[wt])
                jj = j - (NW - 1)
                if jj < 0:
                    continue
                wt = W[jj % NW]
                for (t0, n, col) in (blocks_fn(jj) if blocks_fn else blocks):
                    p = ps_next()
                    for kc in range(8):
                        MM(p.ap[:, 0:n], wt.ap[:, kc, :], rhs_fn(kc, t0, n), kc == 0, kc == 7,
                           [wt] + rhs_tiles_fn(kc, t0, n), [p])
                    evac_fn(jj, p, t0, n, col)
                if post_fn:
                    post_fn(jj)
            if flush_fn:
                flush_fn()

        def out_proj_overlap(l, gen, wsrc_fn, blocks, rhs_fn, rhs_tiles_fn, last):
            WR = [R4.tile(gen, k * 2048, [128, 8, 128], BF16, "wr%d" % k) for k in range(8)]
            for m in range(8):
                DMA("pool", WR[m].ap.rearrange("p a b -> p (a b)"), wsrc_fn(m), writes=[WR[m]])
            ev = resid_evac(l, 16)
            mblocks = TB512[:4] if last else TB512
            assert list(mblocks) == list(blocks)
            sa, sb_ = modulate(l, 2, mblocks, True, external=True)
            nb_ = len(blocks)
            for b in range(nb_ + 2):
                if b < nb_:
                    t0, n, col = blocks[b]
                    for m in range(8):
                        p = ps_next()
                        for kc in range(8):
                            MM(p.ap[:, 0:n], WR[m].ap[:, kc, :], rhs_fn(kc, t0, n), kc == 0, kc == 7,
                               [WR[m]] + rhs_tiles_fn(kc, t0, n), [p])
                        ev(m, p, t0, n, col)
                if 1 <= b <= nb_:
                    sa(b - 1)
                if 2 <= b:
                    sb_(b - 2)
            sb_(nb_ - 1)

        def resid_evac(l, which_g):
            md = MOD[l % 2]
            mdt = MODt[l % 2]

            def f(m, p, t0, n, col):
                xts = xt_tiles(m, t0, n)
                OP("dve", "scalar_tensor_tensor", XTt[:, m, t0:t0 + n], p.ap[:, 0:n],
                   mdt[:, which_g + m, col:col + 1], XTt[:, m, t0:t0 + n], ALU.mult, ALU.add,
                   reads=[p, md] + xts, writes=xts)
            return f

        def even_mixer(l, last):
            i = l // 2
            gen = "L%d.even" % l
            blocks = TB512[:4] if last else TB512
            nun = 16 if last else 18
            FT = R2.tile(gen, 0, [128, 4, NT], BF16, "FT")
            UT = R2.tile(gen, 18432, [128, 4, NT], BF16, "UT")
            FTU = [[Tile(None, "ft%d_%d" % (g, u), FT.grans, gen) for u in range(18)] for g in range(4)]
            UTU = [[Tile(None, "ut%d_%d" % (g, u), UT.grans, gen) for u in range(18)] for g in range(4)]

            def fu(TU, g, t0, n):
                return [TU[g][u] for u in range(t0 // 128, (t0 + n - 1) // 128 + 1)]

            def evac_a(j, p, t0, n, col):
                if j < 4:
                    OP("dve", "tensor_copy", FT.ap[:, j, t0:t0 + n], p.ap[:, 0:n], reads=[p], writes=fu(FTU, j, t0, n))
                else:
                    OP("act", "activation", UT.ap[:, j - 4, t0:t0 + n], p.ap[:, 0:n], AF.Gelu, reads=[p],
                       writes=fu(UTU, j - 4, t0, n))
            proj_stream(mk_w(gen), lambda j: win[i, j], 8, blocks,
                        lambda kc, t0, n: HT.ap[:, kc, pcol(t0):pcol(t0) + n],
                        lambda kc, t0, n: ht_tiles(t0, n), evac_a)

            WV = R4.tile(gen, 6144, [128, 8, 512], BF16, "wvin")
            WS = R4.tile(gen, 14336, [128, 4, 128], BF16, "wsT")
            BS = R4.tile(gen, 15360, [1, 512], BF16, "bs")
            DMA("pool", WV.ap.rearrange("p a b -> p (a b)"), winv[i], writes=[WV])
            DMA("pool", WS.ap.rearrange("p a b -> p (a b)"), wsT[i], writes=[WS])
            DMA("pool", BS.ap, bsd[i], writes=[BS])
            VN = [R3.tile(gen, 18432 + k * 4096, [128, 4, 4, 128], BF16, "vn%d" % k) for k in range(2)]
            VF = [R3.tile(gen, 26624 + k * 2048, [128, 512], F32, "vf%d" % k) for k in range(2)]
            SQV = [R3.tile(gen, 30720 + k * 2048, [128, 512], F32, "sqv%d" % k) for k in range(2)]
            SSV = [R3.tile(gen, 34816 + k * 64, [128, 4], F32, "ssv%d" % k) for k in range(2)]
            RSV = [R3.tile(gen, 34944 + k * 64, [128, 4], F32, "rsv%d" % k) for k in range(2)]
            RRV = [R3.tile(gen, 35072 + k * 64, [128, 4], F32, "rrv%d" % k) for k in range(2)]
            NEGH = R3.tile(gen, 35200, [128, 4], F32, "negh")
            OP("pool", "memset", NEGH.ap, -0.5, writes=[NEGH])
            groups = [(512 * g, 4) for g in range(4)] + ([] if last else [(2048, 2)])
            cnt = 0
            sgu_pend = []
            for gi, (t0g, ntt) in enumerate(groups):
                vn = VN[gi % 2]
                for tt in range(ntt):
                    t0 = t0g + tt * 128
                    pc0 = pcol(t0)
                    p = ps_next()
                    for kc in range(8):
                        MM(p.ap[:, :], HT.ap[:, kc, pc0:pc0 + 128], WV.ap[:, kc, :], kc == 0, kc == 7,
                           [WV] + ht_tiles(t0, 128), [p])
                    vf = VF[cnt % 2]
                    sqv = SQV[cnt % 2]
                    ssv = SSV[cnt % 2]
                    rsv = RSV[cnt % 2]
                    rrv = RRV[cnt % 2]
                    cnt += 1
                    OP("act", "activation", vf.ap, p.ap, AF.Gelu, reads=[p], writes=[vf])
                    OP("pool", "tensor_tensor", sqv.ap, vf.ap, vf.ap, ALU.mult, reads=[vf], writes=[sqv])
                    OP("dve", "tensor_reduce", ssv.ap, sqv.ap.rearrange("p (a b) -> p a b", a=4), AX.X, ALU.add,
                       reads=[sqv], writes=[ssv])
                    OP("dve", "tensor_scalar", rsv.ap, ssv.ap, 1.0 / 128, EPS, ALU.mult, ALU.add,
                       reads=[ssv], writes=[rsv])
                    OP("pool", "tensor_tensor", rrv.ap, rsv.ap, NEGH.ap, ALU.pow, reads=[rsv, NEGH], writes=[rrv])
                    OP("dve", "tensor_tensor", vn.ap[:, tt], vf.ap.rearrange("p (a b) -> p a b", a=4),
                       rrv.ap.unsqueeze(2).to_broadcast([128, 4, 128]), ALU.mult, reads=[vf, rrv], writes=[vn])
                def sgu(vn=vn, t0g=t0g, ntt=ntt):
                    for g in range(4):
                        p = ps_next()
                        for tt in range(ntt):
                            MM(p.ap[:, tt * 128:(tt + 1) * 128], vn.ap[:, tt, g, :], WS.ap[:, g, :], True, False,
                               [vn, WS], [p])
                            MM(p.ap[:, tt * 128:(tt + 1) * 128], ONES[0:1, :], BS.ap[0:1, g * 128:(g + 1) * 128],
                               False, True, [BS, CONST], [p])
                        n = ntt * 128
                        uts = fu(UTU, g, t0g, n)
                        OP("dve", "tensor_tensor", UT.ap[:, g, t0g:t0g + n], UT.ap[:, g, t0g:t0g + n], p.ap[:, 0:n],
                           ALU.mult, reads=[p] + uts, writes=uts)
                if sgu_pend:
                    sgu_pend.pop(0)()
                sgu_pend.append(sgu)
            while sgu_pend:
                sgu_pend.pop(0)()

            gen2 = "L%d.four" % l
            CS = R4.tile(gen2, 6144, [128, 2, 256], BF16, "cs")
            DMA("pool", CS.ap.rearrange("p a b -> p (a b)"), dftC, writes=[CS])
            PQ = [R3.tile(gen2, t * 2048, [128, 4, 256], BF16, "pq%d" % t) for t in range(18)]
            for t in range(nun):
                v = 0 if t < 16 else 1
                for gp in range(2):
                    p = ps_next()
                    for g2 in range(2):
                        g = 2 * gp + g2
                        MM(p.ap[:, g2 * 256:(g2 + 1) * 256], FT.ap[:, g, t * 128:(t + 1) * 128], CS.ap[:, v, :],
                           True, True, [CS] + fu(FTU, g, t * 128, 128), [p])
                    if (t + gp) % 2 == 0:
                        OP("act", "activation", PQ[t].ap[:, 2 * gp:2 * gp + 2, :].rearrange("p a b -> p (a b)"),
                           p.ap, AF.Copy, reads=[p], writes=[PQ[t]])
                    else:
                        OP("dve", "tensor_copy", PQ[t].ap[:, 2 * gp:2 * gp + 2, :].rearrange("p a b -> p (a b)"),
                           p.ap, reads=[p], writes=[PQ[t]])
            DT = [[R1.tile(gen2, (b * 2 + mtx) * 8192, [128, 16, 256], BF16, "dt%d_%d" % (b, mtx)) for mtx in range(2)]
                  for b in range(2)]
            dlv = [dftL[mtx].rearrange("(c p) n -> p c n", p=128) for mtx in range(2)]
            ecnt = 0
            for kt in range(8):
                d = DT[kt % 2]
                for mtx in range(2):
                    DMA("pool", d[mtx].ap, dlv[mtx][:, :, kt * 256:(kt + 1) * 256], writes=[d[mtx]])
                for g in range(4):
                    p = ps_next()
                    for lc in range(16):
                        MM(p.ap[:, 0:256], PQ[lc].ap[:, g, 0:128], d[0].ap[:, lc, :], lc == 0, False,
                           [PQ[lc], d[0]], [p])
                        MM(p.ap[:, 0:256], PQ[lc].ap[:, g, 128:256], d[1].ap[:, lc, :], False, lc == 15,
                           [PQ[lc], d[1]], [p])
                    fts = fu(FTU, g, kt * 256, 256)
                    if ecnt % 2 == 0:
                        OP("act", "activation", FT.ap[:, g, kt * 256:(kt + 1) * 256], p.ap[:, 0:256], AF.Copy,
                           reads=[p], writes=fts)
                    else:
                        OP("dve", "tensor_copy", FT.ap[:, g, kt * 256:(kt + 1) * 256], p.ap[:, 0:256],
                           reads=[p], writes=fts)
                    ecnt += 1
            if not last:
                D2 = R4.tile(gen2, 8192, [128, 2, 2, 256], BF16, "d256")
                for mtx in range(2):
                    DMA("pool", D2.ap[:, mtx], dft256[mtx].rearrange("(c p) n -> p c n", p=128), writes=[D2])
                for g in range(4):
                    p = ps_next()
                    for lc in range(2):
                        MM(p.ap[:, 0:256], PQ[16 + lc].ap[:, g, 0:128], D2.ap[:, 0, lc, :], lc == 0, False,
                           [PQ[16 + lc], D2], [p])
                        MM(p.ap[:, 0:256], PQ[16 + lc].ap[:, g, 128:256], D2.ap[:, 1, lc, :], False, lc == 1,
                           [PQ[16 + lc], D2], [p])
                    OP("act", "activation", FT.ap[:, g, 2048:2304], p.ap[:, 0:256], AF.Copy, reads=[p],
                       writes=fu(FTU, g, 2048, 256))

            def rhs_d(kc, t0, n):
                return FT.ap[:, kc, t0:t0 + n] if kc < 4 else UT.ap[:, kc - 4, t0:t0 + n]

            def rhs_t(kc, t0, n):
                return fu(FTU, kc, t0, n) if kc < 4 else fu(UTU, kc - 4, t0, n)
            out_proj_overlap(l, "L%d.eout" % l, lambda m: wout[i, m], blocks, rhs_d, rhs_t, last)

        def odd_mixer(l, last):
            i = l // 2
            gen = "L%d.odd" % l
            KT = R2.tile(gen, 0, [128, 8, NT], BF16, "KT")
            KTU = [[Tile(None, "kt%d_%d" % (c, u), KT.grans, gen) for u in range(18)] for c in range(8)]
            VT = [R3.tile(gen, t * 2048, [128, 1024], BF16, "vt%d" % t) for t in range(18)]

            def ktu(c, t0, n):
                return [KTU[c][u] for u in range(t0 // 128, (t0 + n - 1) // 128 + 1)]

            def norm_evac(W0, gsrc, dst_fn, bs, gen, nset, depth=2, sq_tiles=None):
                if sq_tiles is None:
                    SQ = [R4.tile(gen, W0 + k * bs * 2, [128, bs], BF16, "qsq%d" % k) for k in range(nset)]
                    o2 = W0 + nset * bs * 2
                else:
                    SQ = sq_tiles
                    o2 = W0
                RS = [R4.tile(gen, o2 + k * bs * 4, [128, bs], F32, "qrs%d" % k) for k in range(nset)]
                assert o2 + nset * bs * 4 <= 18432
                cnt = [0]
                pend = []

                def flush(keep=0):
                    while len(pend) > keep:
                        pend.pop(0)()

                def f(j, p, t0, n, col):
                    k = cnt[0] % nset
                    cnt[0] += 1
                    sq, rs = SQ[k], RS[k]
                    OP("act", "activation", sq.ap[:, 0:n], p.ap[:, 0:n], AF.Square, reads=[p], writes=[sq])

                    def part2():
                        p2 = ps_next()
                        MM(p2.ap[:, 0:n], BLK, sq.ap[:, 0:n], True, True, [sq, CONST], [p2])
                        OP("act", "activation", rs.ap[:, 0:n], p2.ap[:, 0:n], AF.Ln, bias=EPS, scale=1.0 / 64,
                           reads=[p2], writes=[rs])
                        OP("act", "activation", rs.ap[:, 0:n], rs.ap[:, 0:n], AF.Exp, scale=-0.5, reads=[rs], writes=[rs])
                        dst, dtiles = dst_fn(j, t0, n)
                        OP("dve", "scalar_tensor_tensor", dst, p.ap[:, 0:n], gsrc, rs.ap[:, 0:n], ALU.mult, ALU.mult,
                           reads=[p, rs, GQS, QKG], writes=dtiles)
                    pend.append(part2)
                    flush(depth)
                return f, flush

            kev, kflush = norm_evac(6144, QKGt[:, i, 1:2],
                                    lambda j, t0, n: (KT.ap[:, j, t0:t0 + n], ktu(j, t0, n)), 512, gen, 3)
            proj_stream(mk_w(gen), lambda j: wqk[i, 8 + j], 8, TB512,
                        lambda kc, t0, n: HT.ap[:, kc, pcol(t0):pcol(t0) + n],
                        lambda kc, t0, n: ht_tiles(t0, n), kev, flush_fn=kflush)

            genv = "L%d.oddv" % l
            WVt = [R4.tile(genv, k * 8192, [128, 8, 512], BF16, "wvo%d" % k) for k in range(2)]
            ec = 0
            for nt_ in range(2):
                DMA("pool", WVt[nt_].ap.rearrange("p a b -> p (a b)"), wv[i, nt_], writes=[WVt[nt_]])
            for nt_ in range(2):
                for t in range(18):
                    p = ps_next()
                    pc0 = pcol(t * 128)
                    for kc in range(8):
                        MM(p.ap[:, :], HT.ap[:, kc, pc0:pc0 + 128], WVt[nt_].ap[:, kc, :], kc == 0, kc == 7,
                           [WVt[nt_]] + ht_tiles(t * 128, 128), [p])
                    if ec % 2 == 0:
                        OP("act", "activation", VT[t].ap[:, nt_ * 512:(nt_ + 1) * 512], p.ap, AF.Copy,
                           reads=[p], writes=[VT[t]])
                    else:
                        OP("dve", "tensor_copy", VT[t].ap[:, nt_ * 512:(nt_ + 1) * 512], p.ap,
                           reads=[p], writes=[VT[t]])
                    ec += 1

            genq = "L%d.oddq" % l
            qpass = TB512[:4] if last else TB512
            QTMP = R4.tile(genq, 6144, [128, 8, 512], BF16, "qtmp")
            WQ = mk_w(genq)
            sqz = []
            for kz in range(3):
                tz = Tile(QZt[:, kz, 0:256], "qzsq%d" % kz)
                tz.buf = QZ[kz].buf
                sqz.append(tz)
            qev, qflush = norm_evac(14336, GQSt[:, i, 0:1],
                                    lambda j, t0_, n_: (QTMP.ap[:, j % 8, (t0_ % 512):(t0_ % 512) + n_], [QTMP]),
                                    256, genq, 3, sq_tiles=sqz)

            def qpost(jj):
                if jj % 8 == 7:
                    qflush()
                    t0, n, col = qpass[jj // 8]
                    pc0 = pcol(t0)
                    OP("dve", "tensor_copy", HT.ap[:, :, pc0:pc0 + n], QTMP.ap[:, :, 0:n], reads=[QTMP],
                       writes=ht_tiles(t0, n))

            def qblocks_fn(jj):
                t0, n, col = qpass[jj // 8]
                return [(t0 + 256 * s_, 256, col) for s_ in range(n // 256)]
            proj_stream(WQ, lambda j: wqk[i, j % 8], 8 * len(qpass), None,
                        lambda kc, t0_, n_: HT.ap[:, kc, pcol(t0_):pcol(t0_) + n_],
                        lambda kc, t0_, n_: ht_tiles(t0_, n_), qev, post_fn=qpost, blocks_fn=qblocks_fn)

            gena = "L%d.attn" % l
            TAB = [R4.tile(gena, k * 4992, [128, NE, 64], BF16, "tab%d" % k) for k in range(2)]
            PT = [R4.tile(gena, 9984 + k * 1024, [128, 512], BF16, "pt%d" % k) for k in range(4)]
            RD = [R4.tile(gena, 14080 + k * 2048, [128, 512], F32, "rd%d" % k) for k in range(2)]
            qbs = []
            qbs.append((0, 256, [(kr0, NE_WIN + (7 - kr0)) for kr0 in (0, 2, 4, 6)]))
            for k in range(3):
                qr0 = 4 + 8 * k
                qbs.append((64 * qr0, 512, [(kr0, qr0 - kr0 + 7 + 4) for kr0 in range(qr0 - 4, qr0 + 12, 2)]))
            qbs.append((1792, 256, [(kr0, NE_WIN + (35 - kr0)) for kr0 in (24, 26, 28, 30)]))
            if not last:
                qbs.append((2048, 256, []))
            SK = int(os.environ.get('K_SK', '3'))
            SBK = [PS[0], PS[1], PS[2], PS[7]]
            items = []
            ocnt = 0
            for h in range(16):
                for (qt0, nq, loc) in qbs:
                    chunks = [(16, None, 0, nq)]
                    for ci_, (kr0, ei0) in enumerate(loc):
                        if nq == 512:
                            lo, hi = max(0, 2 * ci_ - 7), min(7, 2 * ci_ + 1)
                        else:
                            lo, hi = 0, nq // 64 - 1
                        chunks.append((kr0 // 2, ei0 + lo, lo * 64, (hi - lo + 1) * 64))
                    chunks.append((17, None, 0, nq))
                    for ci, (t, ei0, qo, nqs) in enumerate(chunks):
                        items.append((h, qt0, nq, t, ei0, ci == 0, ci == len(chunks) - 1, ocnt, qo, nqs))
                    ocnt += 1
            staged = []
            tab_loaded = -1
            for kz in range(4):
                OP("pool", "memset", QZt[:, kz, :], 0.0, writes=[QZ[kz]])
            qz_cur = {}
            qz_cnt = [0, 0]
            for k in range(len(items) + SK):
                if k < len(items):
                    h, qt0, nq, t, ei0, first, lastc, oc, qo, nqs = items[k]
                    i2 = h // 2
                    rows = slice(64 * (h % 2), 64 * (h % 2) + 64)
                    tb = TAB[h % 2]
                    if tab_loaded < h:
                        DMA("pool", tb.ap.rearrange("p a b -> p (a b)"), tabd[i, h], writes=[tb])
                        tab_loaded = h
                    pq0 = pcol(qt0)
                    qtiles = ht_tiles(qt0, nq)
                    ps_ = SBK[k % 4]
                    pt = PT[k % 4]
                    use_qz = os.environ.get('K_NOQZ') != '1'
                    if not use_qz:
                        MM(ps_.ap[:, 0:nqs], KT.ap[rows, i2, t * 128:(t + 1) * 128],
                           HT.ap[rows, i2, pq0 + qo:pq0 + qo + nqs],
                           True, ei0 is None, ktu(i2, t * 128, 128) + qtiles, [ps_])
                    if first and use_qz:
                        par = h % 2
                        qz = QZ[2 * par + qz_cnt[par] % 2]
                        qz_cnt[par] += 1
                        OP("pool", "tensor_copy", qz.ap[rows, 0:nq], HT.ap[rows, i2, pq0:pq0 + nq], reads=qtiles,
                           writes=[qz])
                        qz_cur[oc] = qz
                    if use_qz:
                        qz = qz_cur[oc]
                        MM(ps_.ap[:, 0:nqs], KT.ap[:, i2, t * 128:(t + 1) * 128], qz.ap[:, qo:qo + nqs],
                           True, ei0 is None, ktu(i2, t * 128, 128) + [qz], [ps_])
                    if ei0 is not None:
                        MM(ps_.ap[:, 0:nqs], IDENT, tb.ap[:, ei0:ei0 + nqs // 64, :].rearrange("p a b -> p (a b)"),
                           False, True, [tb, CONST], [ps_])
                    OP("act", "activation", pt.ap[:, 0:nqs], ps_.ap[:, 0:nqs], AF.Exp, reads=[ps_], writes=[pt])
                    staged.append(pt)
                kk = k - SK
                if kk >= 0:
                    h, qt0, nq, t, ei0, first, lastc, oc, qo, nqs = items[kk]
                    i2 = h // 2
                    rows = slice(64 * (h % 2), 64 * (h % 2) + 64)
                    pt_ = staged[kk]
                    po = PS[3 + 2 * (oc % 2)]
                    pd = PS[4 + 2 * (oc % 2)]
                    MM(po.ap[:, qo:qo + nqs], VT[t].ap[:, i2 * 128:(i2 + 1) * 128], pt_.ap[:, 0:nqs], first, lastc,
                       [VT[t], pt_], [po])
                    MM(pd.ap[:, qo:qo + nqs], ONES, pt_.ap[:, 0:nqs], first, lastc, [pt_, CONST], [pd])
                    if lastc:
                        pq0 = pcol(qt0)
                        rd = RD[oc % 2]
                        OP("dve", RECIP, rd.ap[rows, 0:nq], pd.ap[rows, 0:nq], reads=[pd], writes=[rd])
                        OP("dve", "tensor_tensor", HT.ap[rows, i2, pq0:pq0 + nq], po.ap[rows, 0:nq],
                           rd.ap[rows, 0:nq], ALU.mult, reads=[po, rd], writes=ht_tiles(qt0, nq))

            blocks = TB512[:4] if last else TB512
            genw = "L%d.wo" % l
            out_proj_overlap(l, genw, lambda m: wo[i, m], blocks,
                             lambda kc, t0, n: HT.ap[:, kc, pcol(t0):pcol(t0) + n],
                             lambda kc, t0, n: ht_tiles(t0, n), last)

        for l in range(n_layers):
            last = (l == DEPTH - 1)
            if l == 0 and os.environ.get('K_MODS0') != 'old':
                hook0, fin0 = mods0()
                ps_default[0] = (0, 1, 2, 3, 4, 5, 6)
                modulate(l, 1, TB512, False, hook=hook0)
                fin0()
                ps_default[0] = (0, 1, 2, 3, 4, 5, 6, 7)
            else:
                if l == 0:
                    mods(l)
                modulate(l, 1, TB512 if (not last or l % 2 == 1) else TB512[:4], False)
            if l % 2 == 0:
                even_mixer(l, last)
            else:
                odd_mixer(l, last)
            nm = (l + 1 < n_layers) and os.environ.get('K_NOMODS') != '1'
            ffn(l, last, nm)
            if (l + 1 < n_layers) and (not nm or os.environ.get('K_BOTH') == '1'):
                mods(l + 1)

        outs = []
        for c in range(8):
            outs.append(DMA("sp", yout[c * 128:(c + 1) * 128, :], XTt[:, c, 0:n_out], reads=XTT[c]))
        S.wait_ops("sp", outs)
        S.emit(ctx)
    return nc


def _consts():
    ident = np.eye(128, dtype=np.float32)
    ones = np.ones((128, 128), np.float32)
    blk = np.zeros((128, 128), np.float32)
    blk[:64, :64] = 1.0
    blk[64:, 64:] = 1.0
    cst = np.stack([ident, ones, blk], axis=1).reshape(128, 384)
    ll = np.arange(L, dtype=np.float64)
    ang = 2.0 * np.pi * ((ll[:, None] * ll[None, :]) % L) / L
    dftL = np.stack([np.cos(ang), -np.sin(ang)]).astype(np.float32)
    l2 = np.arange(256, dtype=np.float64)
    ang2 = 2.0 * np.pi * ((l2[:, None] * l2[None, :]) % 256) / 256
    dft256 = np.stack([np.cos(ang2), -np.sin(ang2)]).astype(np.float32)
    cc = np.arange(128, dtype=np.float64)
    angc = 2.0 * np.pi * ((cc[:, None] * cc[None, :]) % 128) / 128
    dftC = np.zeros((128, 2, 256), np.float64)
    for v, n in enumerate((L, 256)):
        s = 1.0 / np.sqrt(n * 128.0)
        dftC[:, v, :128] = np.cos(angc) * s
        dftC[:, v, 128:] = np.sin(angc) * s
    return cst, dftL, dft256, dftC.reshape(128, 512).astype(np.float32)


def _rpb_tables(rpb):
    NEG = np.float32(-1e30)
    kc = np.arange(64)[:, None]
    qc = np.arange(64)[None, :]
    dc = np.clip(kc - qc + 15, 0, 30)
    cs = np.clip(qc - 8, 0, 48)
    valid = (kc >= cs) & (kc < cs + 16)
    out = np.full((2, 16, 128, NE, 64), NEG, np.float32)
    for e in range(15):
        B = np.where(valid[None, None], rpb[:, :, 14 - e][:, :, dc], NEG)
        for krl in range(2):
            ee = e + krl
            if 4 <= e <= 11 and 0 <= ee + 4 < NE_WIN:
                out[:, :, krl * 64:(krl + 1) * 64, ee + 4, :] = B
            if 0 <= ee < 16:
                out[:, :, krl * 64:(krl + 1) * 64, NE_WIN + ee, :] = B
    return out.reshape(2, 16, 128, NE * 64)


def _prep(inputs):
    f = lambda a: np.ascontiguousarray(np.asarray(a, dtype=np.float32))
    x = f(inputs["x"]); c = f(inputs["c"]); cx = f(inputs["ctx"]); c_ctx = f(inputs["c_ctx"])
    cst, dftL, dft256, dftC = _consts()
    sh = {}
    sh["ada_w"] = f(inputs["ada_w"])
    sh["ada_b_r"] = f(f(inputs["ada_b"]).reshape(4, 48, 128).transpose(2, 0, 1).reshape(128, 192))
    ng = np.stack([f(inputs["norm1_g"]), f(inputs["norm2_g"])])
    sh["ng"] = f(ng.reshape(2, 4, 8, 128).transpose(3, 0, 1, 2).reshape(128, 64))
    cw = f(inputs["ffn_conv_w"]); cb = f(inputs["ffn_conv_b"])
    cp = np.concatenate([cw, cb[:, None, :]], axis=1)
    sh["convp"] = f(cp.reshape(4, 4, NFF, 128).transpose(3, 0, 1, 2).reshape(128, 4 * 4 * NFF))
    wu = f(inputs["ffn_w_up"]).reshape(4, 8, 128, 2, NFF, 128)
    sh["wup_r"] = f(wu.transpose(0, 4, 2, 1, 3, 5).reshape(4, NFF, 128, 8 * 256))
    wd = f(inputs["ffn_w_down"]).reshape(4, NFF, 128, 8, 128)
    sh["wdn_r"] = f(wd.transpose(0, 3, 2, 1, 4).reshape(4, 8, 128, NFF * 128))
    wi = f(inputs["even_w_in"])
    wfu = wi[:, :, :1024].reshape(2, 8, 128, 8, 128)
    sh["win_r"] = f(wfu.transpose(0, 3, 2, 1, 4).reshape(2, 8, 128, 1024))
    wvv = wi[:, :, 1024:].reshape(2, 8, 128, 512)
    sh["winv_r"] = f(wvv.transpose(0, 2, 1, 3).reshape(2, 128, 4096))
    sh["wsT"] = f(f(inputs["even_w_s"]).transpose(0, 3, 1, 2).reshape(2, 128, 512))
    sh["bs"] = f(f(inputs["even_b_s"]).reshape(2, 1, 512))
    wo_ = f(inputs["even_w_out"]).reshape(2, 8, 128, 8, 128)
    sh["wout_r"] = f(wo_.transpose(0, 3, 2, 1, 4).reshape(2, 8, 128, 1024))
    wq = f(inputs["odd_w_qkv"])
    wqk_ = wq[:, :, :2048].reshape(2, 8, 128, 16, 128)
    sh["wqk_r"] = f(wqk_.transpose(0, 3, 2, 1, 4).reshape(2, 16, 128, 1024))
    wv_ = wq[:, :, 2048:].reshape(2, 8, 128, 2, 512)
    sh["wv_r"] = f(wv_.transpose(0, 3, 2, 1, 4).reshape(2, 2, 128, 4096))
    woo = f(inputs["odd_w_o"]).reshape(2, 8, 128, 8, 128)
    sh["wo_r"] = f(woo.transpose(0, 3, 2, 1, 4).reshape(2, 8, 128, 1024))
    qg = f(inputs["odd_q_g"]); kg = f(inputs["odd_k_g"])
    qk = np.stack([qg, kg], axis=1)
    qk = np.concatenate([qk, qk], axis=2)
    sh["qkg"] = f(qk.transpose(2, 0, 1).reshape(128, 4))
    sh["tab"] = _rpb_tables(f(inputs["odd_rpb"]))
    sh["dftL"] = dftL
    sh["dftC"] = dftC
    sh["dft256"] = dft256
    sh["cst"] = cst
    in_maps = []
    for b in range(8):
        m = dict(sh)
        m["xin"] = f(np.concatenate([x[b].T, cx[b].T], axis=1))
        sv = np.stack([c[b], c_ctx], axis=1)
        m["svec"] = f(sv.reshape(8, 128, 2).transpose(1, 0, 2).reshape(128, 16))
        in_maps.append(m)
    return in_maps


_NC_CACHE = {}


def kernel(**inputs):
    in_maps = _prep(inputs)
    if "nc" not in _NC_CACHE:
        _NC_CACHE["nc"] = build()
    nc = _NC_CACHE["nc"]
    res = run_bass_kernel_spmd(nc, in_maps, core_ids=list(range(8)))
    out = np.stack([np.ascontiguousarray(r["yout"].T) for r in res.results], axis=0)
    return out.astype(np.float32)
```
